# Optimizing a Trainium2 kernel written in Bass

```python
import math
import jax, jax.numpy as jnp
from jax import lax
import numpy as np

D_MODEL = 2048
BATCH = 4
SEQ = 2048
DEPTH = 2

N_BRANCH = 3
RMS_EPS = 1e-6
NEG_INF = -1e30
FORCE_SCORE = 1e9

GLA_HEADS = 4
GLA_DK = D_MODEL // 2
GLA_DV = D_MODEL
GLA_HK = GLA_DK // GLA_HEADS
GLA_HV = GLA_DV // GLA_HEADS
GLA_LOWRANK = 16
GLA_GATE_NORMALIZER = 16.0
GLA_CHUNK = 64

NSA_HEAD_DIM = 128
NSA_HEADS = D_MODEL // NSA_HEAD_DIM
NSA_KV_GROUPS = 4
NSA_Q_PER_KV = NSA_HEADS // NSA_KV_GROUPS
NSA_WIDTH = NSA_HEADS * NSA_HEAD_DIM
NSA_KV_WIDTH = NSA_KV_GROUPS * NSA_HEAD_DIM
CMP_LEN = 32
CMP_STRIDE = 16
CMP_HIDDEN = 2 * NSA_HEAD_DIM
SEL_BLOCK = 64
SEL_TOPK = 16
SEL_QBLOCK = 16
WINDOW = 512
WIN_QBLOCK = 128
ROPE_THETA = 500000.0
ROPE_DIM = NSA_HEAD_DIM // 4

SSD_D_INNER = 2 * D_MODEL
SSD_HEAD_DIM = 64
SSD_HEADS = SSD_D_INNER // SSD_HEAD_DIM
SSD_GROUPS = 8
SSD_HEADS_PER_GROUP = SSD_HEADS // SSD_GROUPS
SSD_D_STATE = 128
SSD_CONV = 4
SSD_CHUNK = 64
SSD_CONV_DIM = SSD_D_INNER + 2 * SSD_GROUPS * SSD_D_STATE

D_FF = ((8 * D_MODEL + 3 * 256 - 1) // (3 * 256)) * 256

MIX_WIDTH = GLA_DV + NSA_WIDTH + SSD_D_INNER
IN_SIZES = (N_BRANCH * D_MODEL, GLA_DK, GLA_DK, GLA_DV, GLA_LOWRANK, GLA_DV,
            NSA_WIDTH, 6 * NSA_KV_WIDTH, 3 * NSA_HEADS,
            SSD_D_INNER, SSD_CONV_DIM, SSD_HEADS)
IN_COLS = sum(IN_SIZES)

kernel_name = 'hybrid_gla_nsa_ssd_block'


def rmsnorm(x, g):
    xf = x.astype(jnp.float32)
    y = xf * lax.rsqrt(jnp.mean(xf * xf, axis=-1, keepdims=True) + RMS_EPS)
    return (y * g.astype(jnp.float32)).astype(x.dtype)


def partial_rope(t, pos):
    half = ROPE_DIM // 2
    inv_freq = ROPE_THETA ** (-jnp.arange(half, dtype=jnp.float32) / half)
    ang = pos.astype(jnp.float32)[:, None] * inv_freq[None, :]
    cos, sin = jnp.cos(ang), jnp.sin(ang)
    t1, t2, rest = t[..., :half], t[..., half:ROPE_DIM], t[..., ROPE_DIM:]
    return jnp.concatenate([t1 * cos - t2 * sin, t2 * cos + t1 * sin, rest], axis=-1)


def gla_mixer(q, k, v, g_low, r, gate_w2, gate_b, norm_g):
    f32 = jnp.float32
    bsz, L, _ = q.shape
    nc = L // GLA_CHUNK
    log_a = jax.nn.log_sigmoid((g_low @ gate_w2 + gate_b).astype(f32)) / GLA_GATE_NORMALIZER

    def to_chunks(t, d):
        return t.astype(f32).reshape(bsz, nc, GLA_CHUNK, GLA_HEADS, d).transpose(1, 0, 3, 2, 4)

    qc = to_chunks(q, GLA_HK) * (GLA_HK ** -0.5)
    kc = to_chunks(k, GLA_HK)
    vc = to_chunks(v, GLA_HV)
    ac = to_chunks(log_a, GLA_HK)
    causal = jnp.tril(jnp.ones((GLA_CHUNK, GLA_CHUNK), bool))[:, :, None]

    def step(state, inp):
        qi, ki, vi, ai = inp
        b = jnp.cumsum(ai, axis=-2)
        diff = b[:, :, :, None, :] - b[:, :, None, :, :]
        decay = jnp.exp(jnp.where(causal, diff, -jnp.inf))
        scores = jnp.einsum('bhid,bhijd->bhij', qi, ki[:, :, None, :, :] * decay)
        out = (jnp.einsum('bhij,bhjv->bhiv', scores, vi)
               + jnp.einsum('bhid,bhdv->bhiv', qi * jnp.exp(b), state))
        b_last = b[:, :, -1:, :]
        new_state = (jnp.exp(b_last[:, :, 0, :])[..., None] * state
                     + jnp.einsum('bhjd,bhjv->bhdv', ki * jnp.exp(b_last - b), vi))
        return new_state, out

    s0 = jnp.zeros((bsz, GLA_HEADS, GLA_HK, GLA_HV), f32)
    _, o = lax.scan(step, s0, (qc, kc, vc, ac))
    o = o.transpose(1, 0, 3, 2, 4).reshape(bsz, L, GLA_HEADS, GLA_HV)
    o = rmsnorm(o, norm_g).reshape(bsz, L, GLA_DV) * jax.nn.silu(r.astype(f32))
    return o.astype(q.dtype)


def nsa_mixer(q, kv, g, pos_k, pos_v, wk1, wk2, wv1, wv2):
    f32 = jnp.float32
    bsz, L, _ = q.shape
    G, R, hd = NSA_KV_GROUPS, NSA_Q_PER_KV, NSA_HEAD_DIM
    scale = hd ** -0.5
    pos = jnp.arange(L)
    qh = partial_rope(q.astype(f32).reshape(bsz, L, G, R, hd).transpose(0, 2, 3, 1, 4), pos)

    def kv_heads(t):
        return t.astype(f32).reshape(bsz, L, G, hd).transpose(0, 2, 1, 3)

    k_cmp, v_cmp, k_slc, v_slc, k_win, v_win = [kv_heads(t) for t in jnp.split(kv, 6, axis=-1)]
    k_cmp = partial_rope(k_cmp, pos)
    k_slc = partial_rope(k_slc, pos)
    k_win = partial_rope(k_win, pos)

    n_cmp = (L - CMP_LEN) // CMP_STRIDE + 1
    cmp_starts = np.arange(n_cmp) * CMP_STRIDE
    cmp_idx = cmp_starts[:, None] + np.arange(CMP_LEN)[None, :]

    def compress(t, pe, w1, w2):
        blocks = (t[:, :, cmp_idx, :] + pe.astype(f32)).reshape(bsz, G, n_cmp, CMP_LEN * hd)
        return jax.nn.silu(blocks @ w1.astype(f32)) @ w2.astype(f32)

    kc = compress(k_cmp, pos_k, wk1, wk2)
    vc = compress(v_cmp, pos_v, wv1, wv2)
    cmp_ok = (cmp_starts + CMP_LEN - 1)[None, :] <= np.arange(L)[:, None]
    s_cmp = jnp.einsum('bgrld,bgnd->bgrln', qh, kc) * scale
    p_cmp = jax.nn.softmax(jnp.where(cmp_ok, s_cmp, NEG_INF), axis=-1) * cmp_ok
    o_cmp = jnp.einsum('bgrln,bgnd->bgrld', p_cmp, vc)

    n_sel = L // SEL_BLOCK
    k_eff = min(SEL_TOPK, n_sel)
    sel_starts = np.arange(n_sel) * SEL_BLOCK
    overlap = np.clip(np.minimum(cmp_starts[:, None] + CMP_LEN, sel_starts[None, :] + SEL_BLOCK)
                      - np.maximum(cmp_starts[:, None], sel_starts[None, :]), 0, None).astype(np.float32) / CMP_LEN
    p_slc = jnp.einsum('bgrln,nj->bglj', p_cmp, jnp.asarray(overlap))
    blk_t = (np.arange(L) // SEL_BLOCK)[:, None]
    blk_j = np.arange(n_sel)[None, :]
    forced = (blk_j == 0) | (blk_j == blk_t) | (blk_j == blk_t - 1)
    sel_score = jnp.where(forced, FORCE_SCORE, jnp.where(blk_j <= blk_t, p_slc, NEG_INF))
    _, sel_idx = lax.top_k(sel_score, k_eff)

    kb = k_slc.reshape(bsz, G, n_sel, SEL_BLOCK, hd)
    vb = v_slc.reshape(bsz, G, n_sel, SEL_BLOCK, hd)
    nqb = L // SEL_QBLOCK
    q_blk = qh.reshape(bsz, G, R, nqb, SEL_QBLOCK, hd).transpose(3, 0, 1, 2, 4, 5)
    idx_blk = sel_idx.reshape(bsz, G, nqb, SEL_QBLOCK, k_eff).transpose(2, 0, 1, 3, 4)
    pos_blk = pos.reshape(nqb, SEL_QBLOCK)
    gather = jax.vmap(jax.vmap(lambda blocks, ids: blocks[ids]))
    in_block = jnp.arange(SEL_BLOCK)
    n_keys = k_eff * SEL_BLOCK

    def sel_attend(args):
        qb, ib, pb = args
        ks = gather(kb, ib)
        vs = gather(vb, ib)
        kpos = ib[..., None] * SEL_BLOCK + in_block
        ok = (kpos <= pb[:, None, None]).reshape(bsz, G, 1, SEL_QBLOCK, n_keys)
        s = jnp.einsum('bgrqd,bgqksd->bgrqks', qb, ks).reshape(bsz, G, R, SEL_QBLOCK, n_keys) * scale
        p = jax.nn.softmax(jnp.where(ok, s, NEG_INF), axis=-1)
        return jnp.einsum('bgrqt,bgqtd->bgrqd', p, vs.reshape(bsz, G, SEL_QBLOCK, n_keys, hd))

    o_sel = lax.map(sel_attend, (q_blk, idx_blk, pos_blk))
    o_sel = o_sel.transpose(1, 2, 3, 0, 4, 5).reshape(bsz, G, R, L, hd)

    n_prev = WINDOW // WIN_QBLOCK
    nwb = L // WIN_QBLOCK
    kw_len = (n_prev + 1) * WIN_QBLOCK

    def band(t):
        tp = jnp.pad(t, ((0, 0), (0, 0), (WINDOW, 0), (0, 0))).reshape(bsz, G, nwb + n_prev, WIN_QBLOCK, hd)
        return jnp.concatenate([tp[:, :, j:j + nwb] for j in range(n_prev + 1)], axis=3)

    qpos = np.arange(L).reshape(nwb, WIN_QBLOCK)[:, :, None]
    kpos = (np.arange(nwb)[:, None] * WIN_QBLOCK + np.arange(kw_len)[None, :] - WINDOW)[:, None, :]
    win_ok = (kpos <= qpos) & (kpos > qpos - WINDOW) & (kpos >= 0)
    s_win = jnp.einsum('bgrnqd,bgnkd->bgrnqk', qh.reshape(bsz, G, R, nwb, WIN_QBLOCK, hd), band(k_win)) * scale
    p_win = jax.nn.softmax(jnp.where(win_ok, s_win, NEG_INF), axis=-1)
    o_win = jnp.einsum('bgrnqk,bgnkd->bgrnqd', p_win, band(v_win)).reshape(bsz, G, R, L, hd)

    gates = jax.nn.sigmoid(g.astype(f32)).reshape(bsz, L, G, R, 3).transpose(0, 2, 3, 1, 4)
    o = gates[..., 0:1] * o_cmp + gates[..., 1:2] * o_sel + gates[..., 2:3] * o_win
    return o.transpose(0, 3, 1, 2, 4).reshape(bsz, L, NSA_WIDTH).astype(q.dtype)


def ssd_mixer(z, xbc, dt_raw, conv_w, conv_b, dt_bias, a_log, d_skip, norm_g):
    f32 = jnp.float32
    bsz, L, _ = z.shape
    G, R, P, N, C = SSD_GROUPS, SSD_HEADS_PER_GROUP, SSD_HEAD_DIM, SSD_D_STATE, SSD_CHUNK
    nc = L // C
    conv = lax.conv_general_dilated(
        xbc, conv_w[:, None, :].astype(xbc.dtype), window_strides=(1,),
        padding=[(SSD_CONV - 1, 0)], dimension_numbers=('NWC', 'WIO', 'NWC'),
        feature_group_count=SSD_CONV_DIM)
    xbc = jax.nn.silu((conv + conv_b).astype(f32))
    xs, bm, cm = jnp.split(xbc, (SSD_D_INNER, SSD_D_INNER + G * N), axis=-1)
    xs = xs.reshape(bsz, L, G, R, P)
    dt = jax.nn.softplus(dt_raw.astype(f32) + dt_bias.astype(f32)).reshape(bsz, L, G, R)
    a = (-jnp.exp(a_log.astype(f32))).reshape(G, R)
    xdt = (xs * dt[..., None]).reshape(bsz, nc, C, G, R, P).transpose(1, 0, 2, 3, 4, 5)
    adt = (dt * a).reshape(bsz, nc, C, G, R).transpose(1, 0, 3, 4, 2)
    bc = bm.reshape(bsz, nc, C, G, N).transpose(1, 0, 2, 3, 4)
    cc = cm.reshape(bsz, nc, C, G, N).transpose(1, 0, 2, 3, 4)
    causal = jnp.tril(jnp.ones((C, C), bool))

    def step(state, inp):
        xi, ai, bi, ci = inp
        acum = jnp.cumsum(ai, axis=-1)
        seg = jnp.exp(jnp.where(causal, acum[..., :, None] - acum[..., None, :], -jnp.inf))
        cb = jnp.einsum('blgn,bsgn->bgls', ci, bi)
        y_diag = jnp.einsum('bgrls,bsgrp->blgrp', cb[:, :, None] * seg, xi)
        y_off = jnp.einsum('blgn,bgrpn->blgrp', ci, state) * jnp.exp(acum).transpose(0, 3, 1, 2)[..., None]
        decay = jnp.exp(acum[..., -1:] - acum).transpose(0, 3, 1, 2)[..., None]
        new_state = (state * jnp.exp(acum[..., -1])[..., None, None]
                     + jnp.einsum('bsgn,bsgrp->bgrpn', bi, xi * decay))
        return new_state, y_diag + y_off

    s0 = jnp.zeros((bsz, G, R, P, N), f32)
    _, ys = lax.scan(step, s0, (xdt, adt, bc, cc))
    y = ys.transpose(1, 0, 2, 3, 4, 5).reshape(bsz, L, G, R, P) + xs * d_skip.astype(f32).reshape(G, R)[..., None]
    y = y.reshape(bsz, L, G, R * P) * jax.nn.silu(z.astype(f32)).reshape(bsz, L, G, R * P)
    y = y * lax.rsqrt(jnp.mean(y * y, axis=-1, keepdims=True) + RMS_EPS)
    return (y.reshape(bsz, L, SSD_D_INNER) * norm_g.astype(f32)).astype(z.dtype)


def hybrid_layer(x, norm_mix, w_in, gla_gate_w2, gla_gate_b, gla_out_norm,
                 nsa_cmp_pos_k, nsa_cmp_pos_v, nsa_cmp_k_w1, nsa_cmp_k_w2, nsa_cmp_v_w1, nsa_cmp_v_w2,
                 ssd_conv_w, ssd_conv_b, ssd_dt_bias, ssd_a_log, ssd_d, ssd_out_norm,
                 w_branch, w_out, norm_ffn, w_ffn_gate, w_ffn_up, w_ffn_down):
    bsz, L, _ = x.shape
    h = rmsnorm(x, norm_mix)
    split_at = tuple(int(i) for i in np.cumsum(IN_SIZES)[:-1])
    (gate_cols, gla_q, gla_k, gla_v, gla_low, gla_r,
     nsa_q, nsa_kv, nsa_g, ssd_z, ssd_xbc, ssd_dt) = jnp.split(h @ w_in, split_at, axis=-1)
    o_gla = gla_mixer(gla_q, gla_k, gla_v, gla_low, gla_r, gla_gate_w2, gla_gate_b, gla_out_norm).astype(x.dtype)
    o_nsa = nsa_mixer(nsa_q, nsa_kv, nsa_g, nsa_cmp_pos_k, nsa_cmp_pos_v,
                      nsa_cmp_k_w1, nsa_cmp_k_w2, nsa_cmp_v_w1, nsa_cmp_v_w2).astype(x.dtype)
    o_ssd = ssd_mixer(ssd_z, ssd_xbc, ssd_dt, ssd_conv_w, ssd_conv_b,
                      ssd_dt_bias, ssd_a_log, ssd_d, ssd_out_norm).astype(x.dtype)
    gates = jax.nn.sigmoid(gate_cols.astype(jnp.float32)).astype(x.dtype).reshape(bsz, L, N_BRANCH, D_MODEL)
    u_gla = o_gla @ w_branch[:GLA_DV]
    u_nsa = o_nsa @ w_branch[GLA_DV:GLA_DV + NSA_WIDTH]
    u_ssd = o_ssd @ w_branch[GLA_DV + NSA_WIDTH:]
    merged = gates[:, :, 0] * u_gla + gates[:, :, 1] * u_nsa + gates[:, :, 2] * u_ssd
    x = x + merged @ w_out
    h2 = rmsnorm(x, norm_ffn)
    return x + (jax.nn.silu(h2 @ w_ffn_gate) * (h2 @ w_ffn_up)) @ w_ffn_down


def setup_inputs(seed: int = 0) -> dict:
    key = jax.random.key(seed)
    keys = list(jax.random.split(key, 32))
    f32 = jnp.float32

    def nrm(k, shape, scale):
        return jax.random.normal(k, shape, f32) * scale

    def gain(k, shape):
        return 1.0 + nrm(k, shape, 0.01)

    dt0 = jnp.exp(jax.random.uniform(keys[16], (DEPTH, SSD_HEADS), f32, math.log(1e-3), math.log(1e-1)))
    w_branch = jnp.concatenate([
        nrm(keys[20], (DEPTH, GLA_DV, D_MODEL), GLA_DV ** -0.5),
        nrm(keys[21], (DEPTH, NSA_WIDTH, D_MODEL), NSA_WIDTH ** -0.5),
        nrm(keys[22], (DEPTH, SSD_D_INNER, D_MODEL), SSD_D_INNER ** -0.5)], axis=1)
    return {
        'x': nrm(keys[0], (BATCH, SEQ, D_MODEL), 1.0),
        'norm_mix': gain(keys[1], (DEPTH, D_MODEL)),
        'w_in': nrm(keys[2], (DEPTH, D_MODEL, IN_COLS), D_MODEL ** -0.5),
        'gla_gate_w2': nrm(keys[3], (DEPTH, GLA_LOWRANK, GLA_DK), GLA_LOWRANK ** -0.5),
        'gla_gate_b': nrm(keys[4], (DEPTH, GLA_DK), 0.01),
        'gla_out_norm': gain(keys[5], (DEPTH, GLA_HV)),
        'nsa_cmp_pos_k': nrm(keys[6], (DEPTH, CMP_LEN, NSA_HEAD_DIM), 0.02),
        'nsa_cmp_pos_v': nrm(keys[7], (DEPTH, CMP_LEN, NSA_HEAD_DIM), 0.02),
        'nsa_cmp_k_w1': nrm(keys[8], (DEPTH, CMP_LEN * NSA_HEAD_DIM, CMP_HIDDEN), (CMP_LEN * NSA_HEAD_DIM) ** -0.5),
        'nsa_cmp_k_w2': nrm(keys[9], (DEPTH, CMP_HIDDEN, NSA_HEAD_DIM), CMP_HIDDEN ** -0.5),
        'nsa_cmp_v_w1': nrm(keys[10], (DEPTH, CMP_LEN * NSA_HEAD_DIM, CMP_HIDDEN), (CMP_LEN * NSA_HEAD_DIM) ** -0.5),
        'nsa_cmp_v_w2': nrm(keys[11], (DEPTH, CMP_HIDDEN, NSA_HEAD_DIM), CMP_HIDDEN ** -0.5),
        'ssd_conv_w': nrm(keys[12], (DEPTH, SSD_CONV, SSD_CONV_DIM), SSD_CONV ** -0.5),
        'ssd_conv_b': nrm(keys[13], (DEPTH, SSD_CONV_DIM), 0.01),
        'ssd_dt_bias': dt0 + jnp.log(-jnp.expm1(-dt0)),
        'ssd_a_log': jnp.log(jax.random.uniform(keys[14], (DEPTH, SSD_HEADS), f32, 1.0, 16.0)),
        'ssd_d': gain(keys[15], (DEPTH, SSD_HEADS)),
        'ssd_out_norm': gain(keys[17], (DEPTH, SSD_D_INNER)),
        'w_branch': w_branch,
        'w_out': nrm(keys[23], (DEPTH, D_MODEL, D_MODEL), D_MODEL ** -0.5),
        'norm_ffn': gain(keys[24], (DEPTH, D_MODEL)),
        'w_ffn_gate': nrm(keys[25], (DEPTH, D_MODEL, D_FF), D_MODEL ** -0.5),
        'w_ffn_up': nrm(keys[26], (DEPTH, D_MODEL, D_FF), D_MODEL ** -0.5),
        'w_ffn_down': nrm(keys[27], (DEPTH, D_FF, D_MODEL), D_FF ** -0.5),
        'norm_final': gain(keys[28], (D_MODEL,)),
    }


def reference(x, norm_mix, w_in, gla_gate_w2, gla_gate_b, gla_out_norm,
              nsa_cmp_pos_k, nsa_cmp_pos_v, nsa_cmp_k_w1, nsa_cmp_k_w2, nsa_cmp_v_w1, nsa_cmp_v_w2,
              ssd_conv_w, ssd_conv_b, ssd_dt_bias, ssd_a_log, ssd_d, ssd_out_norm,
              w_branch, w_out, norm_ffn, w_ffn_gate, w_ffn_up, w_ffn_down, norm_final):
    for l in range(DEPTH):
        x = hybrid_layer(x, norm_mix[l], w_in[l], gla_gate_w2[l], gla_gate_b[l], gla_out_norm[l],
                         nsa_cmp_pos_k[l], nsa_cmp_pos_v[l], nsa_cmp_k_w1[l], nsa_cmp_k_w2[l],
                         nsa_cmp_v_w1[l], nsa_cmp_v_w2[l],
                         ssd_conv_w[l], ssd_conv_b[l], ssd_dt_bias[l], ssd_a_log[l], ssd_d[l], ssd_out_norm[l],
                         w_branch[l], w_out[l], norm_ffn[l], w_ffn_gate[l], w_ffn_up[l], w_ffn_down[l])
    return rmsnorm(x, norm_final)
```

```python
import numpy as np
import concourse.bass as bass
import concourse.mybir as mybir
from concourse.bass_utils import run_bass_kernel_spmd

F32 = mybir.dt.float32
BF16 = mybir.dt.bfloat16
AF = mybir.ActivationFunctionType
ALU = mybir.AluOpType
AX = mybir.AxisListType


class Buf:
    def __init__(self, k, t, name):
        self.k = k
        self.t = t
        self.name = name
        self.w = None
        self.r = {}
        self.dsem = None
        self.psum = False

    def __getitem__(self, idx):
        return self.t[idx]


class SemSlot:
    def __init__(self, h):
        self.h = h
        self.cnt = 0


class Scope:
    def __init__(self, k):
        import contextlib
        self.k = k
        self.st = contextlib.ExitStack()
        self.mine = []

    def __enter__(self):
        return self

    def sb(self, name, shape, dt):
        self.k.uid += 1
        name = f"{name}_{self.k.uid}"
        t = self.st.enter_context(self.k.nc.sbuf_tensor(name, list(shape), dt))
        b = Buf(self.k, t, name)
        self.mine.append(b)
        return b

    def __exit__(self, *a):
        self.k.barrier()
        for b in self.mine:
            if b.dsem is not None:
                self.k.free_slots.append(b.dsem)
        self.st.close()
        return False


class K:
    ENGS = ("pe", "act", "dve", "pool", "sp")

    def __init__(self, nc):
        self.nc = nc
        self.eng = {"pe": nc.tensor, "act": nc.scalar, "dve": nc.vector,
                    "pool": nc.gpsimd, "sp": nc.sync}
        self.sem = {}
        self.cnt = {}
        self.known = {}
        self.epoch = 0
        self.bufs = []
        self.slots = []
        self.free_slots = []
        self.nsem = 0
        self.uid = 0
        self._new_sems()
        self.n_ins = 0

    def _new_sems(self):
        for e in self.ENGS:
            self.sem[e] = self.nc.alloc_semaphore(name=f"s_{e}_{self.epoch}")
            self.cnt[e] = 0
            self.nsem += 1
        self.known = {e: {} for e in self.ENGS}

    def sb(self, name, shape, dt):
        t = self.nc.alloc_sbuf_tensor(name, list(shape), dt)
        b = Buf(self, t, name)
        self.bufs.append(b)
        return b

    def ps(self, name, shape, dt=F32):
        t = self.nc.alloc_psum_tensor(name, list(shape), dt)
        b = Buf(self, t, name)
        b.psum = True
        self.bufs.append(b)
        return b

    def _need(self, e, deps, b, is_write):
        if b.w is not None:
            kk, v = b.w
            if kk == "dma":
                deps[("d", b)] = max(deps.get(("d", b), 0), v)
            else:
                deps[kk] = max(deps.get(kk, 0), v)
        if is_write or b.psum:
            for kk, v in b.r.items():
                if not is_write and kk == e:
                    continue
                if kk == "dma":
                    deps[("d", b)] = max(deps.get(("d", b), 0), v)
                else:
                    deps[kk] = max(deps.get(kk, 0), v)

    def _emit_waits(self, e, deps):
        eng = self.eng[e]
        kn = self.known[e]
        for kk, v in deps.items():
            if isinstance(kk, tuple):
                b = kk[1]
                key = ("d", id(b.dsem))
                if kn.get(key, 0) >= v:
                    continue
                eng.wait_ge(b.dsem.h, 16 * v)
                kn[key] = v
            else:
                if kk == e and (e == "pe" or v > self.cnt[e]):
                    continue
                if kn.get(kk, 0) >= v:
                    continue
                eng.wait_ge(self.sem[kk], v)
                kn[kk] = v

    def op(self, e, fn, reads=(), writes=(), sig=True):
        deps = {}
        for b in reads:
            self._need(e, deps, b, False)
        for b in writes:
            self._need(e, deps, b, True)
        self._emit_waits(e, deps)
        ins = fn(self.eng[e])
        self.n_ins += 1
        if sig:
            self.cnt[e] += 1
            ins.then_inc(self.sem[e], 1)
            v = self.cnt[e]
        else:
            v = self.cnt[e] + 1
        for b in reads:
            b.r[e] = max(b.r.get(e, 0), v)
        for b in writes:
            b.w = (e, v)
            b.r = {}
        return ins

    def dma(self, q, out, in_, reads=(), writes=(), sbuf=None, **kw):
        deps = {}
        for b in reads:
            self._need(q, deps, b, False)
        for b in writes:
            self._need(q, deps, b, True)
        self._emit_waits(q, deps)
        b = sbuf
        if b.dsem is None:
            b.dsem = self._slot("sw" if q == "pool" else "hw")
        assert b.dsem.kind == ("sw" if q == "pool" else "hw"), b.name
        ins = self.eng[q].dma_start(out=out, in_=in_, **kw)
        ins.then_inc(b.dsem.h, 16)
        self.n_ins += 1
        b.dsem.cnt += 1
        for x in reads:
            if x is not b:
                raise ValueError("dma reads must be the tracked sbuf")
            x.r["dma"] = b.dsem.cnt
        for x in writes:
            if x is not b:
                raise ValueError("dma writes must be the tracked sbuf")
            x.w = ("dma", b.dsem.cnt)
            x.r = {}
        return ins

    def _slot(self, kind):
        for i, sl in enumerate(self.free_slots):
            if sl.kind == kind:
                return self.free_slots.pop(i)
        sl = SemSlot(self.nc.alloc_semaphore(name=f"d_{self.nsem}"))
        sl.kind = kind
        self.nsem += 1
        self.slots.append(sl)
        return sl

    def scope(self):
        return Scope(self)

    def dma_fence(self, q="sp"):
        kn = self.known[q]
        for sl in self.slots:
            if sl.cnt > 0:
                key = ("d", id(sl))
                if kn.get(key, 0) < sl.cnt:
                    self.eng[q].wait_ge(sl.h, 16 * sl.cnt)
                    kn[key] = sl.cnt

    def barrier(self):
        for e in self.ENGS:
            kn = self.known[e]
            for o in self.ENGS:
                if (o == e and e == "pe") or self.cnt[o] == 0:
                    continue
                if kn.get(o, 0) < self.cnt[o]:
                    self.eng[e].wait_ge(self.sem[o], self.cnt[o])
                    kn[o] = self.cnt[o]
        for e in self.ENGS:
            self.dma_fence(e)

    def finish(self):
        self.barrier()


class Cfg:
    def __init__(s, D=2048, L=2048, DEPTH=2):
        s.D, s.L, s.DEPTH = D, L, DEPTH
        s.KT, s.NT, s.NB = D // 128, L // 128, L // 512
        s.GH, s.DK, s.DV, s.LOW = 4, D // 2, D, 16
        s.HK, s.HV = s.DK // 4, s.DV // 4
        s.KC, s.VC = s.HK // 128, s.HV // 128
        s.HD, s.NH, s.NG = 128, D // 128, 4
        s.R, s.NW, s.KVW = s.NH // 4, D, 512
        s.CL, s.CS, s.CH, s.SB, s.WIN = 32, 16, 256, 64, 512
        s.NCMP, s.NSEL = (L - 32) // 16 + 1, L // 64
        s.TOPK = min(16, s.NSEL)
        s.DI, s.P, s.SG, s.N, s.CONV = 2 * D, 64, 8, 128, 4
        s.SH = s.DI // 64
        s.HPG = s.SH // 8
        s.CD = s.DI + 2 * 8 * 128
        s.FF = ((8 * D + 3 * 256 - 1) // (3 * 256)) * 256
        s.FT = s.FF // 128
        s.MIX = s.DV + s.NW + s.DI
        sizes = (3 * D, s.DK, s.DK, s.DV, 16, s.DV, s.NW, 6 * s.KVW, 3 * s.NH, s.DI, s.CD, s.SH)
        names = ("gate", "q", "k", "v", "low", "r", "nq", "nkv", "ng", "z", "xbc", "dt")
        s.off = {}
        o = 0
        for n, z in zip(names, sizes):
            s.off[n] = o
            o += z
        s.IN_COLS = o


def host_consts(c):
    import ml_dtypes
    bf = ml_dtypes.bfloat16
    L = c.L
    i = np.arange(128)
    tri = (i[:, None] <= i[None, :]).astype(np.float32)
    upper = (i[:, None] > i[None, :]).astype(np.float32)
    half = 16
    inv = (500000.0 ** (-np.arange(half, dtype=np.float32) / half)).astype(np.float32)
    ang = np.arange(L, dtype=np.float32)[None, :] * inv[:, None]
    cos, sin = np.cos(ang).astype(np.float32), np.sin(ang).astype(np.float32)
    ropeC = np.concatenate([cos, cos, np.ones((96, L), np.float32)], 0)
    ropeS = np.concatenate([sin, sin, np.zeros((96, L), np.float32)], 0)
    Rm = np.zeros((128, 128), np.float32)
    for d in range(16):
        Rm[d + 16, d] = -1.0
        Rm[d, d + 16] = 1.0
    n = np.arange(128)
    cmpmask = ((16 * n[:, None] + 31) <= np.arange(L)[None, :]).astype(np.float32)
    cmpmask[c.NCMP:] = 0
    cs = np.arange(c.NCMP) * 16
    ss = np.arange(c.NSEL) * 64
    ovl = np.clip(np.minimum(cs[:, None] + 32, ss[None, :] + 64) - np.maximum(cs[:, None], ss[None, :]), 0, None
                  ).astype(np.float32) / 32
    ovl_p = np.zeros((128, c.NSEL), np.float32)
    ovl_p[:c.NCMP] = ovl
    E = (np.arange(L)[None, :] // 64 == np.arange(c.NSEL)[:, None]).astype(np.float32)
    blk_t = (np.arange(L) // 64)[:, None]
    blk_j = np.arange(c.NSEL)[None, :]
    forced = (blk_j == 0) | (blk_j == blk_t) | (blk_j == blk_t - 1)
    valid = blk_j <= blk_t
    selmul = (valid & ~forced).astype(np.float32)
    selbias = np.where(forced, 1e9, np.where(valid, 0.0, -1.0)).astype(np.float32)
    def tm(a):
        return np.ascontiguousarray(a.reshape(c.NT, 128, -1).transpose(1, 0, 2))
    return {
        "c_ident": np.eye(128, dtype=np.float32).astype(bf),
        "c_tri": tri, "c_trib": tri.astype(bf), "c_upperb": upper.astype(bf),
        "c_ropeC": ropeC, "c_ropeS": ropeS, "c_Rm": Rm.astype(bf),
        "c_cmpmask": cmpmask.astype(bf), "c_ovl": ovl_p, "c_E": E.astype(bf),
        "c_selmul": tm(selmul), "c_selbias": tm(selbias),
    }


class Ctx:
    pass


def _w3(ap2, p=128):
    return ap2.rearrange("(k p) c -> p k c", p=p)


def build(c, debug=False, phases=None, depth=None):
    nc = bass.Bass("TRN2", target_bir_lowering=False)
    k = K(nc)
    C = Ctx()
    C.nc, C.k, C.c = nc, k, c
    D, L = c.D, c.L
    DEPTH = c.DEPTH if depth is None else depth

    def inp(name, shape, dt=F32):
        return nc.dram_tensor(name, list(shape), dt, kind="ExternalInput").ap()

    def scratch(name, shape, dt):
        kind = "ExternalOutput" if debug else "Internal"
        return nc.dram_tensor(name, list(shape), dt, kind=kind).ap()

    I = {}
    I["xT"] = inp("xT", [D, L])
    I["norm_mix"] = inp("norm_mix", [c.DEPTH, 128, c.KT])
    I["w_in"] = inp("w_in", [c.DEPTH, D, c.IN_COLS])
    I["gla_w2aug"] = inp("gla_w2aug", [c.DEPTH, 17, c.DK])
    I["gla_out_norm"] = inp("gla_out_norm", [c.DEPTH, 128, c.VC])
    I["nsa_peT_k"] = inp("nsa_peT_k", [c.DEPTH, 128, 32])
    I["nsa_peT_v"] = inp("nsa_peT_v", [c.DEPTH, 128, 32])
    I["nsa_k_w1"] = inp("nsa_k_w1", [c.DEPTH, 4096, 256])
    I["nsa_k_w2"] = inp("nsa_k_w2", [c.DEPTH, 256, 128])
    I["nsa_v_w1"] = inp("nsa_v_w1", [c.DEPTH, 4096, 256])
    I["nsa_v_w2"] = inp("nsa_v_w2", [c.DEPTH, 256, 128])
    I["ssd_conv_w"] = inp("ssd_conv_w", [c.DEPTH, 128, c.CD // 128, 4])
    I["ssd_conv_b"] = inp("ssd_conv_b", [c.DEPTH, 128, c.CD // 128])
    I["ssd_rows"] = inp("ssd_rows", [c.DEPTH, 3, c.SH])
    I["ssd_out_norm"] = inp("ssd_out_norm", [c.DEPTH, c.DI])
    I["w_branch"] = inp("w_branch", [c.DEPTH, c.MIX, D])
    I["w_out"] = inp("w_out", [c.DEPTH, D, D])
    I["norm_ffn"] = inp("norm_ffn", [c.DEPTH, 128, c.KT])
    I["w_ffn_gate"] = inp("w_ffn_gate", [c.DEPTH, D, c.FF])
    I["w_ffn_up"] = inp("w_ffn_up", [c.DEPTH, D, c.FF])
    I["w_ffn_down"] = inp("w_ffn_down", [c.DEPTH, c.FF, D])
    I["norm_final"] = inp("norm_final", [128, c.KT])
    hc = host_consts(c)
    for n_, a_ in hc.items():
        I[n_] = inp(n_, a_.shape, BF16 if a_.dtype != np.float32 else F32)
    C.I = I
    C.out = nc.dram_tensor("outT", [D, L], F32, kind="ExternalOutput").ap()
    C.xres = scratch("xres", [D, L], F32)
    C.mixT = scratch("mixT", [c.MIX, L], BF16)
    C.s_xs = scratch("s_xs", [L, c.DI], BF16)
    C.s_zs = scratch("s_zs", [L, c.DI], BF16)
    C.s_bm = scratch("s_bm", [L, 1024], BF16)
    C.s_bmT = scratch("s_bmT", [1024, L], BF16)
    C.s_cmT = scratch("s_cmT", [1024, L], BF16)
    C.s_acT = scratch("s_acT", [c.SH, L], F32)
    C.debug = debug
    C.wb = {
        "gate": nc.dram_tensor("wb_gate", [D, 3 * D], BF16, kind="Internal").ap(),
        "branch": nc.dram_tensor("wb_branch", [c.MIX, D], BF16, kind="Internal").ap(),
        "out": nc.dram_tensor("wb_out", [D, D], BF16, kind="Internal").ap(),
        "fg": nc.dram_tensor("wb_fg", [D, c.FF], BF16, kind="Internal").ap(),
        "fu": nc.dram_tensor("wb_fu", [D, c.FF], BF16, kind="Internal").ap(),
        "fd": nc.dram_tensor("wb_fd", [c.FF, D], BF16, kind="Internal").ap(),
    }
    C.pc = Buf(k, None, "precast")
    if debug:
        C.d_x1 = scratch("d_x1", [D, L], F32)
        C.d_m = scratch("d_m", [D, L], BF16)
        C.d_h = scratch("d_h", [D, L], BF16)

    C.pb = [k.ps(f"pb{i}", [128, 512], F32) for i in range(8)]
    C._bank = 0

    C.rot = list(range(8))

    def bank():
        b = C.pb[C.rot[C._bank % len(C.rot)]]
        C._bank += 1
        return b
    C.bank = bank
    K_ = {}
    C.hc = hc
    for n_, a_ in hc.items():
        if n_ not in ("c_ident", "c_tri", "c_trib", "c_upperb"):
            continue
        dt = BF16 if a_.dtype != np.float32 else F32
        K_[n_] = k.sb("s" + n_, list(a_.shape), dt)
        k.dma("sp", K_[n_][:], I[n_], writes=[K_[n_]], sbuf=K_[n_])
    K_["ones_bf"] = k.sb("ones_bf", [128, 128], BF16)
    k.op("dve", lambda e: e.memset(K_["ones_bf"][:], 1.0), writes=[K_["ones_bf"]])
    K_["eps"] = k.sb("eps_col", [128, 1], F32)
    k.op("dve", lambda e: e.memset(K_["eps"][:], 1e-6), writes=[K_["eps"]])
    K_["one"] = k.sb("one_col", [128, 1], F32)
    k.op("dve", lambda e: e.memset(K_["one"][:], 1.0), writes=[K_["one"]])
    C.K = K_

    run = phases or ("init", "gla", "nsa", "ssd", "merge", "final")
    if "init" in run:
        phase_init(C)
    for l in range(DEPTH):
        with k.scope() as S0:
            dt_tok = S0.sb("dt_tok", [128, c.NT, c.SH], F32)
            acum_tok = S0.sb("acum_tok", [128, c.NT, c.SH], F32)
            with k.scope() as S:
                hT = S.sb("hT", [128, c.KT, L], BF16)
                phase_norm(C, S, C.xres, I["norm_mix"][l], hT)
                if "gla" in run:
                    phase_gla(C, l, hT)
                if "nsa" in run:
                    phase_nsa(C, l, hT)
                if "ssd" in run:
                    phase_ssd_prep(C, l, hT, dt_tok, acum_tok)
            if "ssd" in run:
                phase_ssd_loop(C, l, dt_tok, acum_tok)
        if "merge" in run:
            phase_merge_ffn(C, l)
    if "final" in run:
        phase_final(C)
    k.finish()
    C.n_ins = k.n_ins
    return nc, C


def phase_init(C):
    k, c = C.k, C.c
    with k.scope() as S:
        bufs = [S.sb(f"xi{i}", [128, c.L], F32) for i in range(2)]
        for kt in range(c.KT):
            b = bufs[kt % 2]
            k.dma("sp", b[:], C.I["xT"][kt * 128:(kt + 1) * 128, :], writes=[b], sbuf=b)
            k.dma("sp", C.xres[kt * 128:(kt + 1) * 128, :], b[:], reads=[b], sbuf=b)


def phase_norm(C, S, x_dram, g_ap, hT, tbs=None, hoff=0):
    k, c = C.k, C.c
    KT = c.KT
    with k.scope() as S2:
        gcol = S2.sb("gcol", [128, KT], F32)
        k.dma("sp", gcol[:], g_ap, writes=[gcol], sbuf=gcol)
        xb = [S2.sb(f"nx{i}", [128, KT, 512], F32) for i in range(2)]
        sq = [S2.sb(f"nsq{i}", [128, 512], BF16) for i in range(2)]
        rs = S2.sb("nrs", [128, 512], F32)
        rs2 = S2.sb("nrs2", [128, 512], F32)
        for i, tb in enumerate(tbs if tbs is not None else range(c.NB)):
            x = xb[i % 2]
            k.dma("sp", x[:], _w3(x_dram[:, tb * 512:(tb + 1) * 512]), writes=[x], sbuf=x)
            ps = C.bank()
            for kt in range(KT):
                s_ = sq[kt % 2]
                k.op("act", lambda e: e.activation(out=s_[:], in_=x[:, kt, :], func=AF.Square),
                     reads=[x], writes=[s_])
                k.op("pe", lambda e: e.matmul(ps[:], lhsT=C.K["ones_bf"][:], rhs=s_[:],
                                              start=(kt == 0), stop=(kt == KT - 1)),
                     reads=[s_, C.K["ones_bf"]], writes=[ps])
            k.op("act", lambda e: e.activation(out=rs[:], in_=ps[:], func=AF.Sqrt,
                                               bias=C.K["eps"][:], scale=1.0 / c.D),
                 reads=[ps, C.K["eps"]], writes=[rs])
            k.op("dve", lambda e: e.reciprocal(out=rs2[:], in_=rs[:]), reads=[rs], writes=[rs2])
            o = (hoff + i) * 512
            for kt in range(KT):
                k.op("dve", lambda e: e.scalar_tensor_tensor(
                    out=hT[:, kt, o:o + 512], in0=x[:, kt, :], scalar=gcol[:, kt:kt + 1],
                    in1=rs2[:], op0=ALU.mult, op1=ALU.mult), reads=[x, gcol, rs2], writes=[hT])


class WRing:
    def __init__(self, C, S, name, shape, n=2, q="pool"):
        self.C, self.q = C, q
        self.bufs = [S.sb(f"{name}{i}", shape, BF16) for i in range(n)]
        self.i = 0

    def load(self, src3, kt, ncols):
        b = self.bufs[self.i % len(self.bufs)]
        self.i += 1
        self.C.k.dma(self.q, b[:, 0:kt, 0:ncols], src3, writes=[b], sbuf=b)
        return b


def mm_acc(C, ps_ap, ps_buf, pairs, rbufs):
    n = len(pairs)
    for i, (l_, r_) in enumerate(pairs):
        C.k.op("pe", lambda e: e.matmul(ps_ap, lhsT=l_, rhs=r_, start=(i == 0), stop=(i == n - 1)),
               reads=rbufs, writes=[ps_buf], sig=(i == n - 1))


def phase_gla(C, l, hT):
    k, c, I, Kc = C.k, C.c, C.I, C.K
    D, L, KT, NT, NB = c.D, c.L, c.KT, c.NT, c.NB
    HK, HV, KC, VC = c.HK, c.HV, c.KC, c.VC
    w_in = I["w_in"][l]
    with k.scope() as S:
        wr = WRing(C, S, "gw", [128, KT, 512])
        gaug = S.sb("gaug", [32, L], BF16)
        w2aug = S.sb("w2aug", [32, c.DK], BF16)
        k.op("dve", lambda e: e.memset(gaug[:], 1.0), writes=[gaug])
        k.dma("pool", w2aug[0:17, :], I["gla_w2aug"][l], writes=[w2aug], sbuf=w2aug)
        wl = wr.load(_w3(w_in[:, c.off["low"]:c.off["low"] + 16]), KT, 16)
        for tb in range(NB):
            ps = C.bank()
            mm_acc(C, ps[0:16, :], ps, [(wl[:, kt, 0:16], hT[:, kt, tb * 512:(tb + 1) * 512]) for kt in range(KT)],
                   [wl, hT])
            k.op("act", lambda e: e.activation(out=gaug[0:16, tb * 512:(tb + 1) * 512], in_=ps[0:16, :], func=AF.Copy),
                 reads=[ps], writes=[gaug])
        gn = S.sb("gnorm", [128, VC], F32)
        k.dma("sp", gn[:], I["gla_out_norm"][l], writes=[gn], sbuf=gn)
        for hd in range(c.GH):
            phase_gla_head(C, S, l, hT, hd, wr, gaug, w2aug, gn)


def phase_gla_head(C, S0, l, hT, hd, wr, gaug, w2aug, gn):
    k, c, I, Kc = C.k, C.c, C.I, C.K
    D, L, KT, NT, NB = c.D, c.L, c.KT, c.NT, c.NB
    HK, HV, KC, VC = c.HK, c.HV, c.KC, c.VC
    w_in = I["w_in"][l]
    tri = Kc["c_tri"]
    with k.scope() as S:
        qT = S.sb("qT", [128, KC, L], BF16)
        kT = S.sb("kT", [128, KC, L], BF16)
        ktok = S.sb("ktok", [128, NT, HK], BF16)
        vtok = S.sb("vtok", [128, NT, HV], BF16)
        srT = S.sb("srT", [128, VC, L], BF16)
        ebl = S.sb("ebl", [128, KC, NT], F32)
        ogT = S.sb("ogT", [128, VC, L], BF16)
        def fm(col0, ncol, dst, func):
            w = wr.load(_w3(w_in[:, col0:col0 + ncol]), KT, ncol)
            for ct in range(ncol // 128):
                for tb in range(NB):
                    ps = C.bank()
                    mm_acc(C, ps[:], ps, [(w[:, kt, ct * 128:(ct + 1) * 128], hT[:, kt, tb * 512:(tb + 1) * 512])
                                          for kt in range(KT)], [w, hT])
                    k.op("act", lambda e: e.activation(out=dst[:, ct, tb * 512:(tb + 1) * 512], in_=ps[:], func=func),
                         reads=[ps], writes=[dst])
            return w

        def tm(w, ncol, dst):
            for tt in range(NT):
                ps = C.bank()
                mm_acc(C, ps[:, 0:ncol], ps, [(hT[:, kt, tt * 128:(tt + 1) * 128], w[:, kt, 0:ncol])
                                              for kt in range(KT)], [w, hT])
                k.op("act", lambda e: e.activation(out=dst[:, tt, :], in_=ps[:, 0:ncol], func=AF.Copy),
                     reads=[ps], writes=[dst])

        fm(c.off["q"] + hd * HK, HK, qT, AF.Copy)
        wk = fm(c.off["k"] + hd * HK, HK, kT, AF.Copy)
        tm(wk, HK, ktok)
        wv = wr.load(_w3(w_in[:, c.off["v"] + hd * HV: c.off["v"] + (hd + 1) * HV]), KT, HV)
        tm(wv, HV, vtok)
        fm(c.off["r"] + hd * HV, HV, srT, AF.Silu)
        spb = [S.sb(f"sp{i}", [128, HK], F32) for i in range(2)]
        t1 = [S.sb(f"gt1_{i}", [128, HK], F32) for i in range(2)]
        t2 = [S.sb(f"gt2_{i}", [128, 128], F32) for i in range(2)]
        t3 = [S.sb(f"gt3_{i}", [128, 128], F32) for i in range(2)]
        for tt in range(NT):
            sl = slice(tt * 128, (tt + 1) * 128)
            sp, a1 = spb[tt % 2], t1[tt % 2]
            ps = C.bank()
            k.op("pe", lambda e: e.matmul(ps[:, 0:HK], lhsT=gaug[0:17, sl], rhs=w2aug[0:17, hd * HK:(hd + 1) * HK],
                                          start=True, stop=True), reads=[gaug, w2aug], writes=[ps])
            k.op("act", lambda e: e.activation(out=a1[:], in_=ps[:, 0:HK], func=AF.Exp, scale=-1.0),
                 reads=[ps], writes=[a1])
            k.op("act", lambda e: e.activation(out=sp[:], in_=a1[:], func=AF.Ln, bias=Kc["one"][:], scale=1.0),
                 reads=[a1, Kc["one"]], writes=[sp])
            ps2 = C.bank()
            k.op("pe", lambda e: e.matmul(ps2[:, 0:HK], lhsT=tri[:], rhs=sp[:], start=True, stop=True),
                 reads=[tri, sp], writes=[ps2])
            k.op("act", lambda e: e.activation(out=a1[:], in_=ps2[:, 0:HK], func=AF.Exp, scale=1.0 / 16),
                 reads=[ps2], writes=[a1])
            k.op("dve", lambda e: e.tensor_tensor(out=ktok[:, tt, :], in0=ktok[:, tt, :], in1=a1[:], op=ALU.mult),
                 reads=[ktok, a1], writes=[ktok])
            for kc in range(KC):
                ep, en = t2[kc % 2], t3[kc % 2]
                ps3 = C.bank()
                k.op("pe", lambda e: e.matmul(ps3[:, 0:128], lhsT=sp[:, kc * 128:(kc + 1) * 128], rhs=tri[:],
                                              start=True, stop=True), reads=[tri, sp], writes=[ps3])
                k.op("act", lambda e: e.activation(out=ep[:], in_=ps3[:, 0:128], func=AF.Exp, scale=1.0 / 16),
                     reads=[ps3], writes=[ep])
                k.op("act", lambda e: e.activation(out=en[:], in_=ps3[:, 0:128], func=AF.Exp, scale=-1.0 / 16),
                     reads=[ps3], writes=[en])
                k.op("dve", lambda e: e.tensor_tensor(out=kT[:, kc, sl], in0=kT[:, kc, sl], in1=ep[:], op=ALU.mult),
                     reads=[kT, ep], writes=[kT])
                k.op("dve", lambda e: e.scalar_tensor_tensor(out=qT[:, kc, sl], in0=qT[:, kc, sl],
                                                             scalar=float(HK) ** -0.5, in1=en[:],
                                                             op0=ALU.mult, op1=ALU.mult),
                     reads=[qT, en], writes=[qT])
                k.op("act", lambda e: e.activation(out=ebl[:, kc, tt:tt + 1], in_=en[:, 127:128], func=AF.Copy),
                     reads=[en], writes=[ebl])
        S32 = S.sb("S32", [128, KC, HV], F32)
        Sb = S.sb("Sb", [128, KC, HV], BF16)
        k.op("dve", lambda e: e.memset(S32[:], 0.0), writes=[S32])
        k.op("dve", lambda e: e.memset(Sb[:], 0.0), writes=[Sb])
        scm = [S.sb(f"scm{i}", [128, 128], BF16) for i in range(2)]
        sq = [S.sb(f"gsq{i}", [128, VC, 128], BF16) for i in range(2)]
        rsd = [S.sb(f"grs{i}", [128, 128], F32) for i in range(2)]
        tmp = [S.sb(f"gtm{i}", [128, 128], F32) for i in range(2)]
        tS = [S.sb(f"gtS{i}", [128, HV], F32) for i in range(2)]
        for ch in range(NT):
            sl = slice(ch * 128, (ch + 1) * 128)
            sc, sqb, rs = scm[ch % 2], sq[ch % 2], rsd[ch % 2]
            ps = C.bank()
            mm_acc(C, ps[:, 0:128], ps, [(kT[:, kc, sl], qT[:, kc, sl]) for kc in range(KC)], [kT, qT])
            k.op("dve", lambda e: e.tensor_tensor(out=sc[:], in0=ps[:, 0:128], in1=tri[:], op=ALU.mult),
                 reads=[ps, tri], writes=[sc])
            po = C.bank()
            for vc in range(VC):
                pairs = [(vtok[:, ch, vc * 128:(vc + 1) * 128], sc[:])]
                pairs += [(Sb[:, kc, vc * 128:(vc + 1) * 128], qT[:, kc, sl]) for kc in range(KC)]
                mm_acc(C, po[:, vc * 128:(vc + 1) * 128], po, pairs, [vtok, sc, Sb, qT])
            k.op("act", lambda e: e.activation(out=sqb[:].rearrange("p a b -> p (a b)"), in_=po[:, 0:VC * 128],
                                               func=AF.Square), reads=[po], writes=[sqb])
            pss = C.bank()
            mm_acc(C, pss[:, 0:128], pss, [(Kc["ones_bf"][:], sqb[:, vc, :]) for vc in range(VC)], [sqb, Kc["ones_bf"]])
            k.op("act", lambda e: e.activation(out=tmp[0][:], in_=pss[:, 0:128], func=AF.Sqrt, bias=Kc["eps"][:],
                                               scale=1.0 / HV), reads=[pss, Kc["eps"]], writes=[tmp[0]])
            k.op("dve", lambda e: e.reciprocal(out=rs[:], in_=tmp[0][:]), reads=[tmp[0]], writes=[rs])
            for vc in range(VC):
                k.op("dve", lambda e: e.scalar_tensor_tensor(out=tmp[1][:], in0=po[:, vc * 128:(vc + 1) * 128],
                                                             scalar=gn[:, vc:vc + 1], in1=rs[:],
                                                             op0=ALU.mult, op1=ALU.mult),
                     reads=[po, gn, rs], writes=[tmp[1]])
                k.op("dve", lambda e: e.tensor_tensor(out=ogT[:, vc, sl], in0=tmp[1][:], in1=srT[:, vc, sl], op=ALU.mult),
                     reads=[tmp[1], srT], writes=[ogT])
            for kc in range(KC):
                pd = C.bank()
                k.op("pe", lambda e: e.matmul(pd[:, 0:HV], lhsT=ktok[:, ch, kc * 128:(kc + 1) * 128], rhs=vtok[:, ch, :],
                                              start=True, stop=True), reads=[ktok, vtok], writes=[pd])
                ts = tS[kc % 2]
                k.op("dve", lambda e: e.tensor_scalar(out=ts[:], in0=pd[:, 0:HV], scalar1=ebl[:, kc, ch:ch + 1],
                                                      scalar2=None, op0=ALU.mult), reads=[pd, ebl], writes=[ts])
                k.op("dve", lambda e: e.scalar_tensor_tensor(out=S32[:, kc, :], in0=S32[:, kc, :],
                                                             scalar=ebl[:, kc, ch:ch + 1], in1=ts[:],
                                                             op0=ALU.mult, op1=ALU.add),
                     reads=[S32, ebl, ts], writes=[S32])
                k.op("act", lambda e: e.activation(out=Sb[:, kc, :], in_=S32[:, kc, :], func=AF.Copy),
                     reads=[S32], writes=[Sb])
        for vc in range(VC):
            r0 = hd * HV + vc * 128
            k.dma("sp", C.mixT[r0:r0 + 128, :], ogT[:, vc, :], reads=[ogT], sbuf=ogT)


def shared_inputs(c, inp):
    f = np.float32
    def colT(v, kt):
        v = np.asarray(v, f)
        return np.ascontiguousarray(v.reshape(v.shape[0], kt, 128).transpose(0, 2, 1))
    m = {}
    m["norm_mix"] = colT(inp["norm_mix"], c.KT)
    m["w_in"] = np.asarray(inp["w_in"], f)
    m["gla_w2aug"] = np.ascontiguousarray(np.concatenate(
        [np.asarray(inp["gla_gate_w2"], f), np.asarray(inp["gla_gate_b"], f)[:, None, :]], axis=1))
    m["gla_out_norm"] = colT(inp["gla_out_norm"], c.VC)
    m["nsa_peT_k"] = np.ascontiguousarray(np.asarray(inp["nsa_cmp_pos_k"], f).transpose(0, 2, 1))
    m["nsa_peT_v"] = np.ascontiguousarray(np.asarray(inp["nsa_cmp_pos_v"], f).transpose(0, 2, 1))
    m["nsa_k_w1"] = np.asarray(inp["nsa_cmp_k_w1"], f)
    m["nsa_k_w2"] = np.asarray(inp["nsa_cmp_k_w2"], f)
    m["nsa_v_w1"] = np.asarray(inp["nsa_cmp_v_w1"], f)
    m["nsa_v_w2"] = np.asarray(inp["nsa_cmp_v_w2"], f)
    cw = np.asarray(inp["ssd_conv_w"], f)
    CT = c.CD // 128
    m["ssd_conv_w"] = np.ascontiguousarray(cw.transpose(0, 2, 1).reshape(cw.shape[0], CT, 128, 4).transpose(0, 2, 1, 3))
    m["ssd_conv_b"] = colT(inp["ssd_conv_b"], CT)
    m["ssd_rows"] = np.ascontiguousarray(np.stack(
        [np.asarray(inp["ssd_dt_bias"], f), np.asarray(inp["ssd_a_log"], f), np.asarray(inp["ssd_d"], f)], axis=1))
    m["ssd_out_norm"] = np.asarray(inp["ssd_out_norm"], f)
    m["w_branch"] = np.asarray(inp["w_branch"], f)
    m["w_out"] = np.asarray(inp["w_out"], f)
    m["norm_ffn"] = colT(inp["norm_ffn"], c.KT)
    m["w_ffn_gate"] = np.asarray(inp["w_ffn_gate"], f)
    m["w_ffn_up"] = np.asarray(inp["w_ffn_up"], f)
    m["w_ffn_down"] = np.asarray(inp["w_ffn_down"], f)
    m["norm_final"] = colT(np.asarray(inp["norm_final"], f)[None], c.KT)[0]
    m.update(host_consts(c))
    return m


def bc_mid(ap2, n):
    return ap2.unsqueeze(1).to_broadcast([ap2.shape[0], n, ap2.shape[1]])


def phase_nsa(C, l, hT):
    k, c, I, Kc = C.k, C.c, C.I, C.K
    KT = c.KT
    with k.scope() as S:
        for n_, a_ in C.hc.items():
            if n_ in ("c_ident", "c_tri", "c_trib", "c_upperb"):
                continue
            dt = BF16 if a_.dtype != np.float32 else F32
            Kc[n_] = S.sb("s" + n_, list(a_.shape), dt)
            k.dma("sp", Kc[n_][:], I[n_], writes=[Kc[n_]], sbuf=Kc[n_])
        wr = WRing(C, S, "nw", [128, KT, 128], n=3)
        w1 = {}
        w2 = {}
        pe = {}
        for kind in ("k", "v"):
            w2[kind] = S.sb(f"w2{kind}", [128, 2, 128], BF16)
            k.dma("pool", w2[kind][:], _w3(I[f"nsa_{kind}_w2"][l]), writes=[w2[kind]], sbuf=w2[kind])
            pe[kind] = S.sb(f"pe{kind}", [128, 32], F32)
            k.dma("sp", pe[kind][:], I[f"nsa_peT_{kind}"][l], writes=[pe[kind]], sbuf=pe[kind])
        for g in range(c.NG):
            C.rot = [0, 1, 2, 3]
            nsa_group(C, l, hT, g, wr, w1, w2, pe)
            C.rot = list(range(8))


def nsa_group(C, l, hT, g, wr, w1, w2, pe):
    k, c, I, Kc = C.k, C.c, C.I, C.K
    D, L, KT, NT, NB, R = c.D, c.L, c.KT, c.NT, c.NB, c.R
    NCMP, NSEL = c.NCMP, c.NSEL
    NA = 129 + NSEL
    w_in = I["w_in"][l]
    scale = 128.0 ** -0.5
    trib, upb, ident = Kc["c_trib"], Kc["c_upperb"], Kc["c_ident"]
    with k.scope() as S:
        qT = S.sb("nqT", [128, R, L], BF16)
        kx = {n_: S.sb(f"n{n_}", [128, L], BF16) for n_ in ("ks", "kw")}
        vsl = S.sb("nvsl", [128, NT, 129], BF16)
        vw = S.sb("nvw", [128, NT, 129], BF16)
        gt = S.sb("ngt", [128, NT, 3 * R], F32)
        onT = S.sb("onT", [128, R, L], BF16)
        ra = [S.sb(f"nra{i}", [128, 512], F32) for i in range(2)]
        rb = [S.sb(f"nrb{i}", [128, 512], F32) for i in range(2)]

        def fm_rope(w, wc0, dst_ap_fn, dstbuf, rope):
            for tb in range(NB):
                tsl = slice(tb * 512, (tb + 1) * 512)
                ps = C.bank()
                mm_acc(C, ps[:], ps, [(w[:, kt, wc0:wc0 + 128], hT[:, kt, tsl]) for kt in range(KT)], [w, hT])
                dst = dst_ap_fn(tsl)
                k.op("act", lambda e: e.activation(out=dst, in_=ps[:], func=AF.Copy), reads=[ps], writes=[dstbuf])
                if rope:
                    pr = C.bank()
                    a_, b_ = ra[tb % 2], rb[tb % 2]
                    k.op("pe", lambda e: e.matmul(pr[:, :], lhsT=Kc["c_Rm"][:], rhs=dst, start=True, stop=True),
                         reads=[Kc["c_Rm"], dstbuf], writes=[pr])
                    k.op("dve", lambda e: e.tensor_tensor(out=a_[:], in0=ps[:, :], in1=Kc["c_ropeC"][:, tsl], op=ALU.mult),
                         reads=[ps, Kc["c_ropeC"]], writes=[a_])
                    k.op("dve", lambda e: e.tensor_tensor(out=b_[:], in0=pr[:, :], in1=Kc["c_ropeS"][:, tsl], op=ALU.mult),
                         reads=[pr, Kc["c_ropeS"]], writes=[b_])
                    k.op("dve", lambda e: e.tensor_tensor(out=dst, in0=a_[:], in1=b_[:], op=ALU.add),
                         reads=[a_, b_], writes=[dstbuf])

        q0 = c.off["nq"] + g * R * 128
        for r in range(R):
            w = wr.load(_w3(w_in[:, q0 + r * 128:q0 + (r + 1) * 128]), KT, 128)
            fm_rope(w, 0, lambda tsl, r=r: qT[:, r, tsl], qT, True)
        def kvcol(kind):
            return c.off["nkv"] + kind * c.KVW + g * 128
        for kind, name, rope in ((2, "ks", True), (4, "kw", True)):
            w = wr.load(_w3(w_in[:, kvcol(kind):kvcol(kind) + 128]), KT, 128)
            fm_rope(w, 0, lambda tsl, name=name: kx[name][:, tsl], kx[name], rope)
        for kind, dst in ((3, vsl), (5, vw)):
            w = wr.load(_w3(w_in[:, kvcol(kind):kvcol(kind) + 128]), KT, 128)
            k.op("dve", lambda e: e.memset(dst[:], 1.0), writes=[dst])
            for tt in range(NT):
                ps = C.bank()
                mm_acc(C, ps[:, 0:128], ps, [(hT[:, kt, tt * 128:(tt + 1) * 128], w[:, kt, 0:128]) for kt in range(KT)], [w, hT])
                k.op("act", lambda e: e.activation(out=dst[:, tt, 0:128], in_=ps[:, 0:128], func=AF.Copy), reads=[ps], writes=[dst])
        g0 = c.off["ng"] + g * R * 3
        w = wr.load(_w3(w_in[:, g0:g0 + 3 * R]), KT, 3 * R)
        for tt in range(NT):
            ps = C.bank()
            mm_acc(C, ps[:, 0:3 * R], ps, [(hT[:, kt, tt * 128:(tt + 1) * 128], w[:, kt, 0:3 * R]) for kt in range(KT)], [w, hT])
            k.op("act", lambda e: e.activation(out=gt[:, tt, :], in_=ps[:, 0:3 * R], func=AF.Sigmoid), reads=[ps], writes=[gt])

        kcT = S.sb("kcT", [128, 128], BF16)
        vaug = S.sb("vaug", [128, NA], BF16)
        k.op("dve", lambda e: e.memset(vaug[:], 1.0), writes=[vaug])
        k.op("dve", lambda e: e.tensor_copy(out=vaug[:, 129:NA], in_=Kc["c_ovl"][:]), reads=[Kc["c_ovl"]], writes=[vaug])
        with k.scope() as S3:
            for n_ in ("kc", "vc"):
                kx[n_] = S3.sb(f"n{n_}", [128, L], BF16)
            kpe = S3.sb("kpe", [128, 32, NCMP], BF16)
            hs = S3.sb("nhs", [128, 2, 128], BF16)
            w1b = S3.sb("w1b", [128, 32, 256], BF16)
            for kind, name, rope in ((0, "kc", True), (1, "vc", False)):
                w = wr.load(_w3(w_in[:, kvcol(kind):kvcol(kind) + 128]), KT, 128)
                fm_rope(w, 0, lambda tsl, name=name: kx[name][:, tsl], kx[name], rope)
            for kind, src in (("k", kx["kc"]), ("v", kx["vc"])):
                k.dma("pool", w1b[:], _w3(I[f"nsa_{kind}_w1"][l]), writes=[w1b], sbuf=w1b)
                for l_ in range(32):
                    k.op("dve", lambda e: e.tensor_scalar(out=kpe[:, l_, :], in0=src[:, l_:l_ + 16 * (NCMP - 1) + 1:16],
                                                          scalar1=pe[kind][:, l_:l_ + 1], scalar2=None, op0=ALU.add),
                         reads=[src, pe[kind]], writes=[kpe])
                for cc in range(2):
                    ps = C.bank()
                    mm_acc(C, ps[:, 0:NCMP], ps, [(w1b[:, l_, cc * 128:(cc + 1) * 128], kpe[:, l_, :]) for l_ in range(32)],
                           [w1b, kpe])
                    k.op("act", lambda e: e.activation(out=hs[:, cc, 0:NCMP], in_=ps[:, 0:NCMP], func=AF.Silu), reads=[ps], writes=[hs])
                ps = C.bank()
                if kind == "k":
                    mm_acc(C, ps[:, 0:NCMP], ps, [(w2[kind][:, cc, :], hs[:, cc, 0:NCMP]) for cc in range(2)], [w2[kind], hs])
                    k.op("act", lambda e: e.activation(out=kcT[:, 0:NCMP], in_=ps[:, 0:NCMP], func=AF.Copy), reads=[ps], writes=[kcT])
                else:
                    mm_acc(C, ps[0:NCMP, 0:128], ps, [(hs[:, cc, 0:NCMP], w2[kind][:, cc, :]) for cc in range(2)], [w2[kind], hs])
                    k.op("act", lambda e: e.activation(out=vaug[0:NCMP, 0:128], in_=ps[0:NCMP, 0:128], func=AF.Copy), reads=[ps], writes=[vaug])

        oacc = [S.sb(f"oacc{i}", [128, R, 128], F32) for i in range(2)]
        ob = [S.sb(f"nob{i}", [128, R, 128], BF16) for i in range(2)]
        eb = [S.sb(f"neb{i}", [128, R, 128], BF16) for i in range(3)]
        pslc = S.sb("pslc", [128, NSEL], F32)
        score = S.sb("score", [128, NSEL], F32)
        sc2 = S.sb("sc2", [128, NSEL], F32)
        m8 = S.sb("m8", [128, 8], F32)
        m8b = S.sb("m8b", [128, 8], F32)
        selb = S.sb("selb", [128, NSEL], BF16)
        selT = S.sb("selT", [32, 128], BF16)
        rd = [S.sb(f"nrd{i}", [128, 1], F32) for i in range(4)]
        wv_ = [S.sb(f"nwv{i}", [128, 1], F32) for i in range(4)]
        pacc = [C.pb[4 + r] for r in range(R)]
        ebi = 0

        def finish_branch(T, br, first):
            for r in range(R):
                pa = pacc[r]
                k.op("dve", lambda e: e.tensor_scalar(out=rd[r][:], in0=pa[:, 128:129], scalar1=1e-30, scalar2=None, op0=ALU.max),
                     reads=[pa], writes=[rd[r]])
                k.op("dve", lambda e: e.reciprocal(out=rd[r][:], in_=rd[r][:]), reads=[rd[r]], writes=[rd[r]])
                k.op("dve", lambda e: e.tensor_tensor(out=wv_[r][:], in0=rd[r][:], in1=gt[:, T, 3 * r + br:3 * r + br + 1], op=ALU.mult),
                     reads=[rd[r], gt], writes=[wv_[r]])
                oa = oacc[T % 2]
                if first:
                    k.op("dve", lambda e: e.tensor_scalar(out=oa[:, r, :], in0=pa[:, 0:128], scalar1=wv_[r][:], scalar2=None, op0=ALU.mult),
                         reads=[pa, wv_[r]], writes=[oa])
                else:
                    k.op("dve", lambda e: e.scalar_tensor_tensor(out=oa[:, r, :], in0=pa[:, 0:128], scalar=wv_[r][:], in1=oa[:, r, :],
                                                                 op0=ALU.mult, op1=ALU.add), reads=[pa, wv_[r], oa], writes=[oa])

        for T in range(NT):
            sl = slice(T * 128, (T + 1) * 128)
            qv = qT[:, 0:R, sl]
            ps = C.bank()
            k.op("pe", lambda e: e.matmul(ps[0:NCMP, 0:R * 128].rearrange("p (r t) -> p r t", r=R), lhsT=kcT[:, 0:NCMP], rhs=qv,
                                          start=True, stop=True), reads=[kcT, qT], writes=[ps])
            e_ = eb[ebi % 3]; ebi += 1
            k.op("act", lambda e: e.activation(out=e_[0:NCMP].rearrange("p r t -> p (r t)"), in_=ps[0:NCMP, 0:R * 128], func=AF.Exp, scale=scale),
                 reads=[ps], writes=[e_])
            k.op("dve", lambda e: e.tensor_tensor(out=e_[0:NCMP], in0=e_[0:NCMP], in1=bc_mid(Kc["c_cmpmask"][0:NCMP, sl], R), op=ALU.mult),
                 reads=[e_, Kc["c_cmpmask"]], writes=[e_])
            for r in range(R):
                pa = pacc[r]
                k.op("pe", lambda e: e.matmul(pa[:, 0:NA], lhsT=e_[0:NCMP, r, :], rhs=vaug[0:NCMP, :], start=True, stop=True),
                     reads=[e_, vaug], writes=[pa])
            finish_branch(T, 0, True)
            for r in range(R):
                pa = pacc[r]
                if r == 0:
                    k.op("dve", lambda e: e.tensor_scalar(out=pslc[:], in0=pa[:, 129:NA], scalar1=rd[r][:], scalar2=None, op0=ALU.mult),
                         reads=[pa, rd[r]], writes=[pslc])
                else:
                    k.op("dve", lambda e: e.scalar_tensor_tensor(out=pslc[:], in0=pa[:, 129:NA], scalar=rd[r][:], in1=pslc[:],
                                                                 op0=ALU.mult, op1=ALU.add), reads=[pa, rd[r], pslc], writes=[pslc])
            if c.TOPK < NSEL:
                assert c.TOPK == 16
                k.op("dve", lambda e: e.tensor_tensor(out=score[:], in0=pslc[:], in1=Kc["c_selmul"][:, T, :], op=ALU.mult),
                     reads=[pslc, Kc["c_selmul"]], writes=[score])
                k.op("dve", lambda e: e.tensor_tensor(out=score[:], in0=score[:], in1=Kc["c_selbias"][:, T, :], op=ALU.add),
                     reads=[score, Kc["c_selbias"]], writes=[score])
                k.op("dve", lambda e: e.max(out=m8[:], in_=score[:]), reads=[score], writes=[m8])
                k.op("dve", lambda e: e.match_replace(out=sc2[:], in_to_replace=m8[:], in_values=score[:], imm_value=-2.0),
                     reads=[score, m8], writes=[sc2])
                k.op("dve", lambda e: e.max(out=m8b[:], in_=sc2[:]), reads=[sc2], writes=[m8b])
                k.op("dve", lambda e: e.tensor_scalar(out=selb[:], in0=score[:], scalar1=m8b[:, 7:8], scalar2=None, op0=ALU.is_ge),
                     reads=[score, m8b], writes=[selb])
            else:
                k.op("dve", lambda e: e.memset(selb[:], 1.0), writes=[selb])
            pt = C.bank()
            ptb = pt[:].bitcast(BF16)
            k.op("pe", lambda e: e.transpose(out=ptb[0:NSEL, 0:128], in_=selb[:], identity=ident[:]), reads=[selb, ident], writes=[pt])
            k.op("act", lambda e: e.activation(out=selT[0:NSEL, :], in_=ptb[0:NSEL, 0:128], func=AF.Copy), reads=[pt], writes=[selT])
            for br, kT_, vT_ in ((1, kx["ks"], vsl), (2, kx["kw"], vw)):
                kts = list(range(0, T + 1)) if br == 1 else list(range(max(0, T - c.WIN // 128), T + 1))
                for i, kt in enumerate(kts):
                    ksl = slice(kt * 128, (kt + 1) * 128)
                    ps = C.bank()
                    k.op("pe", lambda e: e.matmul(ps[:, 0:R * 128].rearrange("p (r t) -> p r t", r=R), lhsT=kT_[:, ksl], rhs=qv,
                                                  start=True, stop=True), reads=[kT_, qT], writes=[ps])
                    e_ = eb[ebi % 3]; ebi += 1
                    k.op("act", lambda e: e.activation(out=e_[:].rearrange("p r t -> p (r t)"), in_=ps[:, 0:R * 128], func=AF.Exp, scale=scale),
                         reads=[ps], writes=[e_])
                    if kt == T:
                        k.op("dve", lambda e: e.tensor_tensor(out=e_[:], in0=e_[:], in1=bc_mid(trib[:], R), op=ALU.mult),
                             reads=[e_, trib], writes=[e_])
                    elif br == 1:
                        pm = C.bank()
                        k.op("pe", lambda e: e.matmul(pm[:, 0:128], lhsT=Kc["c_E"][0:NSEL, ksl], rhs=selT[0:NSEL, :], start=True, stop=True),
                             reads=[Kc["c_E"], selT], writes=[pm])
                        k.op("dve", lambda e: e.tensor_tensor(out=e_[:], in0=e_[:], in1=bc_mid(pm[:, 0:128], R), op=ALU.mult),
                             reads=[e_, pm], writes=[e_])
                    elif kt == T - c.WIN // 128:
                        k.op("dve", lambda e: e.tensor_tensor(out=e_[:], in0=e_[:], in1=bc_mid(upb[:], R), op=ALU.mult),
                             reads=[e_, upb], writes=[e_])
                    for r in range(R):
                        pa = pacc[r]
                        k.op("pe", lambda e: e.matmul(pa[:, 0:129], lhsT=e_[:, r, :], rhs=vT_[:, kt, :], start=(i == 0), stop=(i == len(kts) - 1)),
                             reads=[e_, vT_], writes=[pa], sig=(i == len(kts) - 1))
                finish_branch(T, br, False)
            oa, o_ = oacc[T % 2], ob[T % 2]
            k.op("act", lambda e: e.activation(out=o_[:].rearrange("p r t -> p (r t)"), in_=oa[:].rearrange("p r t -> p (r t)"), func=AF.Copy),
                 reads=[oa], writes=[o_])
            for r in range(R):
                pt = C.bank()
                ptb = pt[:].bitcast(BF16)
                k.op("pe", lambda e: e.transpose(out=ptb[:, 0:128], in_=o_[:, r, :], identity=ident[:]), reads=[o_, ident], writes=[pt])
                k.op("act", lambda e: e.activation(out=onT[:, r, sl], in_=ptb[:, 0:128], func=AF.Copy), reads=[pt], writes=[onT])
        for r in range(R):
            r0 = c.DV + (g * R + r) * 128
            k.dma("sp", C.mixT[r0:r0 + 128, :], onT[:, r, :], reads=[onT], sbuf=onT)


def bc_last(ap2, n):
    return ap2.unsqueeze(2).to_broadcast([ap2.shape[0], ap2.shape[1], n])


def phase_ssd_prep(C, l, hT, dt_tok, acum_tok):
    k, c, I, Kc = C.k, C.c, C.I, C.K
    D, L, KT, NT, NB = c.D, c.L, c.KT, c.NT, c.NB
    DI, SH, CD = c.DI, c.SH, c.CD
    w_in = I["w_in"][l]
    tri, ident = Kc["c_tri"], Kc["c_ident"]
    XT = DI // 128
    with k.scope() as S:
        wr = WRing(C, S, "sw", [128, KT, 512])
        stg = [S.sb(f"sstg{i}", [128, NT, 512], BF16) for i in range(2)]
        si = 0
        for cb in range(DI // 512):
            w = wr.load(_w3(w_in[:, c.off["z"] + cb * 512: c.off["z"] + (cb + 1) * 512]), KT, 512)
            st = stg[si % 2]; si += 1
            for tt in range(NT):
                ps = C.bank()
                mm_acc(C, ps[:], ps, [(hT[:, kt, tt * 128:(tt + 1) * 128], w[:, kt, :]) for kt in range(KT)], [w, hT])
                k.op("act", lambda e: e.activation(out=st[:, tt, :], in_=ps[:], func=AF.Silu), reads=[ps], writes=[st])
            k.dma("sp", C.s_zs[:, cb * 512:(cb + 1) * 512].rearrange("(t p) c -> p t c", p=128), st[:], reads=[st], sbuf=st)
        cw = S.sb("scw", [128, CD // 128, 4], F32)
        cbias = S.sb("scb", [128, CD // 128], F32)
        k.dma("sp", cw[:], I["ssd_conv_w"][l], writes=[cw], sbuf=cw)
        k.dma("sp", cbias[:], I["ssd_conv_b"][l], writes=[cbias], sbuf=cbias)
        xc = [S.sb(f"sxc{i}", [128, L + 4], F32) for i in range(2)]
        acc = [S.sb(f"sacc{i}", [128, L], F32) for i in range(2)]
        yT = [S.sb(f"syT{i}", [128, L], BF16) for i in range(2)]
        for b_ in xc:
            k.op("dve", lambda e: e.memset(b_[:, 0:4], 0.0), writes=[b_])
        for cb in range(CD // 512):
            w = wr.load(_w3(w_in[:, c.off["xbc"] + cb * 512: c.off["xbc"] + (cb + 1) * 512]), KT, 512)
            st = None
            for ci in range(4):
                ct = cb * 4 + ci
                x_, a_, y_ = xc[ct % 2], acc[ct % 2], yT[ct % 2]
                for tb in range(NB):
                    ps = C.bank()
                    mm_acc(C, ps[:], ps, [(w[:, kt, ci * 128:(ci + 1) * 128], hT[:, kt, tb * 512:(tb + 1) * 512]) for kt in range(KT)], [w, hT])
                    k.op("act", lambda e: e.activation(out=x_[:, 3 + tb * 512: 3 + (tb + 1) * 512], in_=ps[:], func=AF.Copy),
                         reads=[ps], writes=[x_])
                k.op("dve", lambda e: e.tensor_scalar(out=a_[:], in0=x_[:, 0:L], scalar1=cw[:, ct, 0:1], scalar2=None, op0=ALU.mult),
                     reads=[x_, cw], writes=[a_])
                for j in range(1, 4):
                    k.op("dve", lambda e: e.scalar_tensor_tensor(out=a_[:], in0=x_[:, j:j + L], scalar=cw[:, ct, j:j + 1], in1=a_[:],
                                                                 op0=ALU.mult, op1=ALU.add), reads=[x_, cw, a_], writes=[a_])
                k.op("act", lambda e: e.activation(out=y_[:], in_=a_[:], func=AF.Silu, bias=cbias[:, ct:ct + 1], scale=1.0),
                     reads=[a_, cbias], writes=[y_])
                is_x = ct < XT
                is_b = XT <= ct < XT + 8
                if is_x or is_b:
                    if st is None:
                        st = stg[si % 2]; si += 1
                    for tt in range(NT):
                        pt = C.bank()
                        ptb = pt[:].bitcast(BF16)
                        k.op("pe", lambda e: e.transpose(out=ptb[:, 0:128], in_=y_[:, tt * 128:(tt + 1) * 128], identity=ident[:]),
                             reads=[y_, ident], writes=[pt])
                        k.op("act", lambda e: e.activation(out=st[:, tt, ci * 128:(ci + 1) * 128], in_=ptb[:, 0:128], func=AF.Copy),
                             reads=[pt], writes=[st])
                if not is_x:
                    g = ct - XT
                    dst = C.s_bmT if g < 8 else C.s_cmT
                    g = g % 8
                    k.dma("sp", dst[g * 128:(g + 1) * 128, :], y_[:], reads=[y_], sbuf=y_)
            if st is not None:
                if cb * 4 < XT:
                    k.dma("sp", C.s_xs[:, cb * 512:(cb + 1) * 512].rearrange("(t p) c -> p t c", p=128), st[:], reads=[st], sbuf=st)
                else:
                    o = cb * 512 - DI
                    k.dma("sp", C.s_bm[:, o:o + 512].rearrange("(t p) c -> p t c", p=128), st[:], reads=[st], sbuf=st)
        rows = S.sb("srows", [128, 3, SH], F32)
        k.dma("sp", rows[:], I["ssd_rows"][l].partition_broadcast(128), writes=[rows], sbuf=rows)
        arow = S.sb("sarow", [128, SH], F32)
        k.op("act", lambda e: e.activation(out=arow[:], in_=rows[:, 1, :], func=AF.Exp), reads=[rows], writes=[arow])
        k.op("dve", lambda e: e.tensor_scalar(out=arow[:], in0=arow[:], scalar1=-1.0, scalar2=None, op0=ALU.mult), reads=[arow], writes=[arow])
        acT = S.sb("sacT", [SH, L], F32)
        t1 = [S.sb(f"sdt{i}", [128, SH], F32) for i in range(2)]
        adt = [S.sb(f"sadt{i}", [128, SH], F32) for i in range(2)]
        w = wr.load(_w3(w_in[:, c.off["dt"]: c.off["dt"] + SH]), KT, SH)
        for tt in range(NT):
            a1, a2 = t1[tt % 2], adt[tt % 2]
            ps = C.bank()
            mm_acc(C, ps[:, 0:SH], ps, [(hT[:, kt, tt * 128:(tt + 1) * 128], w[:, kt, 0:SH]) for kt in range(KT)], [w, hT])
            k.op("dve", lambda e: e.tensor_tensor(out=a1[:], in0=ps[:, 0:SH], in1=rows[:, 0, :], op=ALU.add), reads=[ps, rows], writes=[a1])
            k.op("act", lambda e: e.activation(out=a1[:], in_=a1[:], func=AF.Exp), reads=[a1], writes=[a1])
            k.op("act", lambda e: e.activation(out=dt_tok[:, tt, :], in_=a1[:], func=AF.Ln, bias=Kc["one"][:], scale=1.0),
                 reads=[a1, Kc["one"]], writes=[dt_tok])
            k.op("dve", lambda e: e.tensor_tensor(out=a2[:], in0=dt_tok[:, tt, :], in1=arow[:], op=ALU.mult), reads=[dt_tok, arow], writes=[a2])
            ps2 = C.bank()
            k.op("pe", lambda e: e.matmul(ps2[:, 0:SH], lhsT=tri[:], rhs=a2[:], start=True, stop=True), reads=[tri, a2], writes=[ps2])
            k.op("act", lambda e: e.activation(out=acum_tok[:, tt, :], in_=ps2[:, 0:SH], func=AF.Copy), reads=[ps2], writes=[acum_tok])
            ps3 = C.bank()
            k.op("pe", lambda e: e.matmul(ps3[0:SH, 0:128], lhsT=a2[:], rhs=tri[:], start=True, stop=True), reads=[tri, a2], writes=[ps3])
            k.op("act", lambda e: e.activation(out=acT[:, tt * 128:(tt + 1) * 128], in_=ps3[0:SH, 0:128], func=AF.Copy), reads=[ps3], writes=[acT])
        k.dma("sp", C.s_acT[:, :], acT[:], reads=[acT], sbuf=acT)


def precast_weights(C, l):
    k, c, I = C.k, C.c, C.I
    D = c.D
    jobs = [(C.wb["gate"], I["w_in"][l][:, 0:3 * D], 8), (C.wb["branch"], I["w_branch"][l], 16), (C.wb["out"], I["w_out"][l], 4),
            (C.wb["fg"], I["w_ffn_gate"][l], 8), (C.wb["fu"], I["w_ffn_up"][l], 8), (C.wb["fd"], I["w_ffn_down"][l], 8)]
    for dst, src, n in jobs:
        rows = src.shape[0]
        step = (rows + n - 1) // n
        for r0 in range(0, rows, step):
            r1 = min(rows, r0 + step)
            k.dma("pool", dst[r0:r1, :], src[r0:r1, :], sbuf=C.pc)


def phase_ssd_loop(C, l, dt_tok, acum_tok):
    k, c, I, Kc = C.k, C.c, C.I, C.K
    D, L, KT, NT, NB = c.D, c.L, c.KT, c.NT, c.NB
    DI, SH, HPG, G = c.DI, c.SH, c.HPG, c.SG
    GW = HPG * 64
    tri, trib, ident = Kc["c_tri"], Kc["c_trib"], Kc["c_ident"]
    with k.scope() as S:
        k.dma_fence("sp")
        precast_weights(C, l)
        ngr = S.sb("ngrow", [128, DI], F32)
        k.dma("sp", ngr[:], I["ssd_out_norm"][l].partition_broadcast(128), writes=[ngr], sbuf=ngr)
        rows = S.sb("lrows", [128, 3, SH], F32)
        k.dma("sp", rows[:], I["ssd_rows"][l].partition_broadcast(128), writes=[rows], sbuf=rows)
        st32 = [S.sb(f"st32_{g}", [128, GW], F32) for g in range(G)]
        stb = [S.sb(f"stb_{g}", [128, GW], BF16) for g in range(G)]
        for g in range(G):
            k.op("dve", lambda e: e.memset(st32[g][:], 0.0), writes=[st32[g]])
            k.op("dve", lambda e: e.memset(stb[g][:], 0.0), writes=[stb[g]])
        xs_ = [S.sb(f"lxs{i}", [128, DI], BF16) for i in range(2)]
        zs_ = [S.sb(f"lzs{i}", [128, DI], BF16) for i in range(2)]
        bm_ = [S.sb(f"lbm{i}", [128, G * 128], BF16) for i in range(2)]
        bT_ = [S.sb(f"lbT{i}", [128, G, 128], BF16) for i in range(2)]
        cT_ = [S.sb(f"lcT{i}", [128, G, 128], BF16) for i in range(2)]
        Arow = [S.sb(f"Arow{g}", [128, HPG, 128], F32) for g in range(G)]
        segb = [S.sb(f"segb{g}", [128, HPG, 128], BF16) for g in range(G)]
        xdt = S.sb("xdt", [128, DI], BF16)
        xdd = S.sb("xdd", [128, DI], BF16)
        oT_ = [S.sb(f"loT{i}", [128, DI // 128, 128], BF16) for i in range(2)]
        cbm = S.sb("cbm", [128, G, 128], BF16)
        dec = S.sb("dec", [128, SH], F32)
        eAl = S.sb("eAl", [128, SH], F32)
        eA = S.sb("eA", [128, SH], F32)
        tt_ = [S.sb(f"lt{i}", [128, GW], F32) for i in range(3)]
        uu_ = [S.sb(f"lu{i}", [128, GW], F32) for i in range(3)]
        ob_ = [S.sb(f"lob{i}", [128, GW], BF16) for i in range(3)]
        junk = S.sb("ljunk", [128, GW], BF16)
        ssq = [S.sb(f"lssq{i}", [128, 1], F32) for i in range(2)]
        for ch in range(NT):
            sl = slice(ch * 128, (ch + 1) * 128)
            xs, zs, bm, bT, cT, oT = xs_[ch % 2], zs_[ch % 2], bm_[ch % 2], bT_[ch % 2], cT_[ch % 2], oT_[ch % 2]
            k.dma("sp", xs[:], C.s_xs[sl, :], writes=[xs], sbuf=xs)
            k.dma("sp", zs[:], C.s_zs[sl, :], writes=[zs], sbuf=zs)
            k.dma("sp", bm[:], C.s_bm[sl, :], writes=[bm], sbuf=bm)
            k.dma("sp", bT[:], C.s_bmT[:, sl].rearrange("(g p) t -> p g t", p=128), writes=[bT], sbuf=bT)
            k.dma("sp", cT[:], C.s_cmT[:, sl].rearrange("(g p) t -> p g t", p=128), writes=[cT], sbuf=cT)
            for g in range(G):
                k.dma("sp", Arow[g][:], C.s_acT[g * HPG:(g + 1) * HPG, sl].partition_broadcast(128), writes=[Arow[g]], sbuf=Arow[g])
            for g in range(G):
                hs = slice(g * HPG, (g + 1) * HPG)
                k.op("dve", lambda e: e.tensor_tensor(out=dec[:, hs], in0=Arow[g][:, :, 127], in1=acum_tok[:, ch, hs], op=ALU.subtract),
                     reads=[Arow[g], acum_tok], writes=[dec])
                k.op("act", lambda e: e.activation(out=eAl[:, hs], in_=Arow[g][:, :, 127], func=AF.Exp), reads=[Arow[g]], writes=[eAl])
            k.op("act", lambda e: e.activation(out=dec[:], in_=dec[:], func=AF.Exp), reads=[dec], writes=[dec])
            k.op("act", lambda e: e.activation(out=eA[:], in_=acum_tok[:, ch, :], func=AF.Exp), reads=[acum_tok], writes=[eA])
            xs3 = xs[:].rearrange("p (h q) -> p h q", q=64)
            k.op("dve", lambda e: e.tensor_tensor(out=xdt[:].rearrange("p (h q) -> p h q", q=64), in0=xs3, in1=bc_last(dt_tok[:, ch, :], 64), op=ALU.mult),
                 reads=[xs, dt_tok], writes=[xdt])
            k.op("pool", lambda e: e.tensor_tensor(out=xdd[:].rearrange("p (h q) -> p h q", q=64), in0=xdt[:].rearrange("p (h q) -> p h q", q=64),
                                                    in1=bc_last(dec[:], 64), op=ALU.mult), reads=[xdt, dec], writes=[xdd])
            for g in range(G):
                for hh in range(HPG):
                    h = g * HPG + hh
                    k.op("dve", lambda e: e.tensor_scalar(out=Arow[g][:, hh, :], in0=Arow[g][:, hh, :], scalar1=acum_tok[:, ch, h:h + 1], scalar2=0.0,
                                                          op0=ALU.subtract, op1=ALU.min), reads=[Arow[g], acum_tok], writes=[Arow[g]], sig=(hh == HPG - 1))
                k.op("act", lambda e: e.activation(out=segb[g][:].rearrange("p h t -> p (h t)"), in_=Arow[g][:].rearrange("p h t -> p (h t)"), func=AF.Exp),
                     reads=[Arow[g]], writes=[segb[g]])
            for half in range(2):
                pc = C.bank()
                for gi in range(4):
                    g = half * 4 + gi
                    k.op("pe", lambda e: e.matmul(pc[:, gi * 128:(gi + 1) * 128], lhsT=bT[:, g, :], rhs=cT[:, g, :], start=True, stop=True),
                         reads=[bT, cT], writes=[pc], sig=(gi == 3))
                k.op("dve", lambda e: e.tensor_tensor(out=cbm[:, half * 4:(half + 1) * 4, :], in0=pc[:].rearrange("p (g t) -> p g t", g=4),
                                                      in1=bc_mid(tri[:], 4), op=ALU.mult), reads=[pc, tri], writes=[cbm])
            for g in range(G):
                hs = slice(g * HPG, (g + 1) * HPG)
                gs = slice(g * GW, (g + 1) * GW)
                k.op("dve", lambda e: e.tensor_tensor(out=segb[g][:], in0=segb[g][:], in1=bc_mid(cbm[:, g, :], HPG), op=ALU.mult),
                     reads=[segb[g], cbm], writes=[segb[g]])
                yd = C.bank()
                for hh in range(HPG):
                    h = g * HPG + hh
                    k.op("pe", lambda e: e.matmul(yd[:, hh * 64:(hh + 1) * 64], lhsT=segb[g][:, hh, :], rhs=xdt[:, h * 64:(h + 1) * 64], start=True, stop=True),
                         reads=[segb[g], xdt], writes=[yd], sig=(hh == HPG - 1))
                yo = C.bank()
                k.op("pe", lambda e: e.matmul(yo[:, 0:GW], lhsT=cT[:, g, :], rhs=stb[g][:], start=True, stop=True), reads=[cT, stb[g]], writes=[yo])
                t_, u_, o_ = tt_[g % 3], uu_[g % 3], ob_[g % 3]
                k.op("dve", lambda e: e.tensor_tensor(out=t_[:].rearrange("p (h q) -> p h q", q=64), in0=yo[:, 0:GW].rearrange("p (h q) -> p h q", q=64),
                                                      in1=bc_last(eA[:, hs], 64), op=ALU.mult), reads=[yo, eA], writes=[t_])
                k.op("dve", lambda e: e.tensor_tensor(out=t_[:], in0=t_[:], in1=yd[:, 0:GW], op=ALU.add), reads=[t_, yd], writes=[t_])
                k.op("pool", lambda e: e.tensor_tensor(out=u_[:].rearrange("p (h q) -> p h q", q=64), in0=xs[:, gs].rearrange("p (h q) -> p h q", q=64),
                                                        in1=bc_last(rows[:, 2, hs], 64), op=ALU.mult), reads=[xs, rows], writes=[u_])
                k.op("pool", lambda e: e.tensor_tensor(out=u_[:], in0=u_[:], in1=t_[:], op=ALU.add), reads=[u_, t_], writes=[u_])
                k.op("pool", lambda e: e.tensor_tensor(out=u_[:], in0=u_[:], in1=zs[:, gs], op=ALU.mult), reads=[u_, zs], writes=[u_])
                sq_ = ssq[g % 2]
                k.op("act", lambda e: e.activation(out=junk[:], in_=u_[:], func=AF.Square, accum_out=sq_[:]), reads=[u_], writes=[junk, sq_])
                k.op("act", lambda e: e.activation(out=sq_[:], in_=sq_[:], func=AF.Sqrt, bias=Kc["eps"][:], scale=1.0 / GW),
                     reads=[sq_, Kc["eps"]], writes=[sq_])
                k.op("dve", lambda e: e.reciprocal(out=sq_[:], in_=sq_[:]), reads=[sq_], writes=[sq_])
                k.op("dve", lambda e: e.scalar_tensor_tensor(out=o_[:], in0=u_[:], scalar=sq_[:], in1=ngr[:, gs], op0=ALU.mult, op1=ALU.mult),
                     reads=[u_, sq_, ngr], writes=[o_])
                for j in range(GW // 128):
                    pt = C.bank()
                    ptb = pt[:].bitcast(BF16)
                    k.op("pe", lambda e: e.transpose(out=ptb[:, 0:128], in_=o_[:, j * 128:(j + 1) * 128], identity=ident[:]), reads=[o_, ident], writes=[pt])
                    k.op("act", lambda e: e.activation(out=oT[:, g * (GW // 128) + j, :], in_=ptb[:, 0:128], func=AF.Copy), reads=[pt], writes=[oT])
                pd = C.bank()
                k.op("pe", lambda e: e.matmul(pd[:, 0:GW], lhsT=bm[:, g * 128:(g + 1) * 128], rhs=xdd[:, gs], start=True, stop=True),
                     reads=[bm, xdd], writes=[pd])
                k.op("dve", lambda e: e.tensor_tensor(out=st32[g][:].rearrange("p (h q) -> p h q", q=64), in0=st32[g][:].rearrange("p (h q) -> p h q", q=64),
                                                      in1=bc_last(eAl[:, hs], 64), op=ALU.mult), reads=[st32[g], eAl], writes=[st32[g]])
                k.op("dve", lambda e: e.tensor_tensor(out=st32[g][:], in0=st32[g][:], in1=pd[:, 0:GW], op=ALU.add), reads=[st32[g], pd], writes=[st32[g]])
                k.op("act", lambda e: e.activation(out=stb[g][:], in_=st32[g][:], func=AF.Copy), reads=[st32[g]], writes=[stb[g]])
            base = c.DV + c.NW
            k.dma("sp", C.mixT[base:base + DI, sl].rearrange("(k p) t -> p k t", p=128), oT[:], reads=[oT], sbuf=oT)


def norm_sbuf(C, x, gcol, sq, rs, rs2, out_fn, Dn):
    k, c, Kc = C.k, C.c, C.K
    KT = c.KT
    ps = C.bank()
    for kt in range(KT):
        s_ = sq[kt % 2]
        k.op("act", lambda e: e.activation(out=s_[:], in_=x[:, kt, :], func=AF.Square), reads=[x], writes=[s_])
        k.op("pe", lambda e: e.matmul(ps[:], lhsT=Kc["ones_bf"][:], rhs=s_[:], start=(kt == 0), stop=(kt == KT - 1)),
             reads=[s_, Kc["ones_bf"]], writes=[ps])
    k.op("act", lambda e: e.activation(out=rs[:], in_=ps[:], func=AF.Sqrt, bias=Kc["eps"][:], scale=1.0 / Dn),
         reads=[ps, Kc["eps"]], writes=[rs])
    k.op("dve", lambda e: e.reciprocal(out=rs2[:], in_=rs[:]), reads=[rs], writes=[rs2])
    for kt in range(KT):
        dst, dbuf = out_fn(kt)
        k.op("dve", lambda e: e.scalar_tensor_tensor(out=dst, in0=x[:, kt, :], scalar=gcol[:, kt:kt + 1], in1=rs2[:],
                                                     op0=ALU.mult, op1=ALU.mult), reads=[x, gcol, rs2], writes=[dbuf])


def phase_merge_ffn(C, l):
    k, c, I, Kc = C.k, C.c, C.I, C.K
    D, L, KT, NT, NB, FT = c.D, c.L, c.KT, c.NT, c.NB, c.FT
    w_in = I["w_in"][l]
    with k.scope() as S:
        k.dma_fence("sp")
        wr = WRing(C, S, "mw", [128, KT, 512], q="sp")
        wd = WRing(C, S, "mwd", [128, FT, 128], q="sp")
        g1 = S.sb("mg1", [128, KT], F32)
        g2 = S.sb("mg2", [128, KT], F32)
        k.dma("sp", g1[:], I["norm_mix"][l], writes=[g1], sbuf=g1)
        k.dma("sp", g2[:], I["norm_ffn"][l], writes=[g2], sbuf=g2)
        xt = S.sb("mx", [128, KT, 512], F32)
        sq = [S.sb(f"msq{i}", [128, 512], BF16) for i in range(2)]
        rs = S.sb("mrs", [128, 512], F32)
        rs2 = S.sb("mrs2", [128, 512], F32)
        hb = S.sb("mh", [128, KT, 512], BF16)
        for tb in range(NB):
            tsl = slice(tb * 512, (tb + 1) * 512)
            k.dma("sp", xt[:], _w3(C.xres[:, tsl]), writes=[xt], sbuf=xt)
            norm_sbuf(C, xt, g1, sq, rs, rs2, lambda kt: (hb[:, kt, :], hb), D)
            with k.scope() as S1:
                mix = S1.sb("mmix", [128, 2 * KT, 512], BF16)
                m32 = S1.sb("mm32", [128, KT, 512], F32)
                mbf = S1.sb("mmbf", [128, KT, 512], BF16)
                gsb = [S1.sb(f"mgs{i}", [128, 512], F32) for i in range(4)]
                tmp = [S1.sb(f"mtp{i}", [128, 512], F32) for i in range(2)]
                for bi, (r0, nk) in enumerate(((0, KT), (D, KT), (2 * D, 2 * KT))):
                    k.dma("sp", mix[:, 0:nk, :], _w3(C.mixT[r0:r0 + nk * 128, tsl]), writes=[mix], sbuf=mix)
                    for d4 in range(KT // 4):
                        wg = wr.load(_w3(C.wb["gate"][:, bi * D + d4 * 512: bi * D + (d4 + 1) * 512]), KT, 512)
                        for j4 in range(4):
                            cs = slice(j4 * 128, (j4 + 1) * 128)
                            pg = C.bank()
                            mm_acc(C, pg[:], pg, [(wg[:, kt, cs], hb[:, kt, :]) for kt in range(KT)], [wg, hb])
                            gs = gsb[j4]
                            k.op("act", lambda e: e.activation(out=gs[:], in_=pg[:], func=AF.Sigmoid), reads=[pg], writes=[gs])
                        wbs = [wr.load(_w3(C.wb["branch"][r0 + j * D: r0 + (j + 1) * D, d4 * 512:(d4 + 1) * 512]), KT, 512)
                               for j in range(nk // KT)]
                        for j4 in range(4):
                            dmt = d4 * 4 + j4
                            cs = slice(j4 * 128, (j4 + 1) * 128)
                            gs = gsb[j4]
                            pu = C.bank()
                            mm_acc(C, pu[:], pu, [(wbs[kk // KT][:, kk % KT, cs], mix[:, kk, :]) for kk in range(nk)], wbs + [mix])
                            if bi == 0:
                                k.op("dve", lambda e: e.tensor_tensor(out=m32[:, dmt, :], in0=pu[:], in1=gs[:], op=ALU.mult),
                                     reads=[pu, gs], writes=[m32])
                            else:
                                t_ = tmp[dmt % 2]
                                k.op("dve", lambda e: e.tensor_tensor(out=t_[:], in0=pu[:], in1=gs[:], op=ALU.mult),
                                     reads=[pu, gs], writes=[t_])
                                k.op("pool", lambda e: e.tensor_tensor(out=m32[:, dmt, :], in0=m32[:, dmt, :], in1=t_[:], op=ALU.add),
                                     reads=[m32, t_], writes=[m32])
                for kt in range(KT):
                    k.op("act", lambda e: e.activation(out=mbf[:, kt, :], in_=m32[:, kt, :], func=AF.Copy), reads=[m32], writes=[mbf])
                for d4 in range(KT // 4):
                    wo = wr.load(_w3(C.wb["out"][:, d4 * 512:(d4 + 1) * 512]), KT, 512)
                    for j4 in range(4):
                        dmt = d4 * 4 + j4
                        px = C.bank()
                        mm_acc(C, px[:], px, [(wo[:, kt, j4 * 128:(j4 + 1) * 128], mbf[:, kt, :]) for kt in range(KT)], [wo, mbf])
                        k.op("dve", lambda e: e.tensor_tensor(out=xt[:, dmt, :], in0=xt[:, dmt, :], in1=px[:], op=ALU.add),
                             reads=[xt, px], writes=[xt])
                if C.debug:
                    k.dma("sp", _w3(C.d_x1[:, tsl]), xt[:], reads=[xt], sbuf=xt)
                    k.dma("sp", _w3(C.d_m[:, tsl]), mbf[:], reads=[mbf], sbuf=mbf)
                    k.dma("sp", _w3(C.d_h[:, tsl]), hb[:], reads=[hb], sbuf=hb)
            norm_sbuf(C, xt, g2, sq, rs, rs2, lambda kt: (hb[:, kt, :], hb), D)
            with k.scope() as S2:
                act = S2.sb("mact", [128, FT, 512], BF16)
                sg = [S2.sb(f"msg{i}", [128, 512], F32) for i in range(2)]
                for f4 in range((FT + 3) // 4):
                    nc_ = min(512, c.FF - f4 * 512)
                    wg = wr.load(_w3(C.wb["fg"][:, f4 * 512:f4 * 512 + nc_]), KT, nc_)
                    wu = wr.load(_w3(C.wb["fu"][:, f4 * 512:f4 * 512 + nc_]), KT, nc_)
                    for j4 in range(nc_ // 128):
                        ft = f4 * 4 + j4
                        cs = slice(j4 * 128, (j4 + 1) * 128)
                        pg = C.bank()
                        mm_acc(C, pg[:], pg, [(wg[:, kt, cs], hb[:, kt, :]) for kt in range(KT)], [wg, hb])
                        s_ = sg[ft % 2]
                        k.op("act", lambda e: e.activation(out=s_[:], in_=pg[:], func=AF.Silu), reads=[pg], writes=[s_])
                        pu = C.bank()
                        mm_acc(C, pu[:], pu, [(wu[:, kt, cs], hb[:, kt, :]) for kt in range(KT)], [wu, hb])
                        k.op("dve", lambda e: e.tensor_tensor(out=act[:, ft, :], in0=pu[:], in1=s_[:], op=ALU.mult),
                             reads=[pu, s_], writes=[act])
                for dmt in range(KT):
                    w = wd.load(_w3(C.wb["fd"][:, dmt * 128:(dmt + 1) * 128]), FT, 128)
                    py = C.bank()
                    mm_acc(C, py[:], py, [(w[:, ft, :], act[:, ft, :]) for ft in range(FT)], [w, act])
                    k.op("dve", lambda e: e.tensor_tensor(out=xt[:, dmt, :], in0=xt[:, dmt, :], in1=py[:], op=ALU.add),
                         reads=[xt, py], writes=[xt])
            k.dma("sp", _w3(C.xres[:, tsl]), xt[:], reads=[xt], sbuf=xt)


def phase_final(C):
    k, c, I, Kc = C.k, C.c, C.I, C.K
    KT, NB = c.KT, c.NB
    with k.scope() as S:
        k.dma_fence("sp")
        g = S.sb("fg", [128, KT], F32)
        k.dma("sp", g[:], I["norm_final"], writes=[g], sbuf=g)
        xt = [S.sb(f"fx{i}", [128, KT, 512], F32) for i in range(2)]
        ot = [S.sb(f"fo{i}", [128, KT, 512], F32) for i in range(2)]
        sq = [S.sb(f"fsq{i}", [128, 512], BF16) for i in range(2)]
        rs = S.sb("frs", [128, 512], F32)
        rs2 = S.sb("frs2", [128, 512], F32)
        for tb in range(NB):
            tsl = slice(tb * 512, (tb + 1) * 512)
            x, o = xt[tb % 2], ot[tb % 2]
            k.dma("sp", x[:], _w3(C.xres[:, tsl]), writes=[x], sbuf=x)
            norm_sbuf(C, x, g, sq, rs, rs2, lambda kt: (o[:, kt, :], o), c.D)
            k.dma("sp", _w3(C.out[:, tsl]), o[:], reads=[o], sbuf=o)


N_CORES = 4


def kernel(**inputs):
    c = Cfg()
    nc, C = build(c)
    shared = shared_inputs(c, inputs)
    x = np.asarray(inputs["x"], np.float32)
    in_maps = []
    for b in range(N_CORES):
        m = dict(shared)
        m["xT"] = np.ascontiguousarray(x[b].T)
        in_maps.append(m)
    res = run_bass_kernel_spmd(nc, in_maps, core_ids=list(range(N_CORES)))
    out = np.stack([np.asarray(res.results[b]["outT"]).T for b in range(N_CORES)])
    return np.ascontiguousarray(out.astype(np.float32))
```

```python
import numpy as np
import concourse.bass as bass
import concourse.mybir as mybir
from concourse.bass_utils import run_bass_kernel_spmd

F32 = mybir.dt.float32
BF16 = mybir.dt.bfloat16
AF = mybir.ActivationFunctionType
ALU = mybir.AluOpType
AX = mybir.AxisListType


class Buf:
    def __init__(self, k, t, name):
        self.k = k
        self.t = t
        self.name = name
        self.w = None
        self.r = {}
        self.dsem = None
        self.psum = False

    def __getitem__(self, idx):
        return self.t[idx]


class SemSlot:
    def __init__(self, h):
        self.h = h
        self.cnt = 0


class Scope:
    def __init__(self, k):
        import contextlib
        self.k = k
        self.st = contextlib.ExitStack()
        self.mine = []

    def __enter__(self):
        return self

    def sb(self, name, shape, dt):
        self.k.uid += 1
        name = f"{name}_{self.k.uid}"
        t = self.st.enter_context(self.k.nc.sbuf_tensor(name, list(shape), dt))
        b = Buf(self.k, t, name)
        self.mine.append(b)
        return b

    def __exit__(self, *a):
        self.k.barrier()
        for b in self.mine:
            if b.dsem is not None:
                self.k.free_slots.append(b.dsem)
        self.st.close()
        return False


class K:
    ENGS = ("pe", "act", "dve", "pool", "sp")

    def __init__(self, nc):
        self.nc = nc
        self.eng = {"pe": nc.tensor, "act": nc.scalar, "dve": nc.vector,
                    "pool": nc.gpsimd, "sp": nc.sync}
        self.sem = {}
        self.cnt = {}
        self.known = {}
        self.epoch = 0
        self.bufs = []
        self.slots = []
        self.free_slots = []
        self.nsem = 0
        self.uid = 0
        self._new_sems()
        self.n_ins = 0

    def _new_sems(self):
        for e in self.ENGS:
            self.sem[e] = self.nc.alloc_semaphore(name=f"s_{e}_{self.epoch}")
            self.cnt[e] = 0
            self.nsem += 1
        self.known = {e: {} for e in self.ENGS}

    def sb(self, name, shape, dt):
        t = self.nc.alloc_sbuf_tensor(name, list(shape), dt)
        b = Buf(self, t, name)
        self.bufs.append(b)
        return b

    def ps(self, name, shape, dt=F32):
        t = self.nc.alloc_psum_tensor(name, list(shape), dt)
        b = Buf(self, t, name)
        b.psum = True
        self.bufs.append(b)
        return b

    def _need(self, e, deps, b, is_write):
        if b.w is not None:
            kk, v = b.w
            if kk == "dma":
                deps[("d", b)] = max(deps.get(("d", b), 0), v)
            else:
                deps[kk] = max(deps.get(kk, 0), v)
        if is_write or b.psum:
            for kk, v in b.r.items():
                if not is_write and kk == e:
                    continue
                if kk == "dma":
                    deps[("d", b)] = max(deps.get(("d", b), 0), v)
                else:
                    deps[kk] = max(deps.get(kk, 0), v)

    def _emit_waits(self, e, deps):
        eng = self.eng[e]
        kn = self.known[e]
        for kk, v in deps.items():
            if isinstance(kk, tuple):
                b = kk[1]
                key = ("d", id(b.dsem))
                if kn.get(key, 0) >= v:
                    continue
                eng.wait_ge(b.dsem.h, 16 * v)
                kn[key] = v
            else:
                if kk == e and (e == "pe" or v > self.cnt[e]):
                    continue
                if kn.get(kk, 0) >= v:
                    continue
                eng.wait_ge(self.sem[kk], v)
                kn[kk] = v

    def op(self, e, fn, reads=(), writes=(), sig=True):
        deps = {}
        for b in reads:
            self._need(e, deps, b, False)
        for b in writes:
            self._need(e, deps, b, True)
        self._emit_waits(e, deps)
        ins = fn(self.eng[e])
        self.n_ins += 1
        if sig:
            self.cnt[e] += 1
            ins.then_inc(self.sem[e], 1)
            v = self.cnt[e]
        else:
            v = self.cnt[e] + 1
        for b in reads:
            b.r[e] = max(b.r.get(e, 0), v)
        for b in writes:
            b.w = (e, v)
            b.r = {}
        return ins

    def dma(self, q, out, in_, reads=(), writes=(), sbuf=None, **kw):
        deps = {}
        for b in reads:
            self._need(q, deps, b, False)
        for b in writes:
            self._need(q, deps, b, True)
        self._emit_waits(q, deps)
        b = sbuf
        if b.dsem is None:
            b.dsem = self._slot("sw" if q == "pool" else "hw")
        assert b.dsem.kind == ("sw" if q == "pool" else "hw"), b.name
        ins = self.eng[q].dma_start(out=out, in_=in_, **kw)
        ins.then_inc(b.dsem.h, 16)
        self.n_ins += 1
        b.dsem.cnt += 1
        for x in reads:
            if x is not b:
                raise ValueError("dma reads must be the tracked sbuf")
            x.r["dma"] = b.dsem.cnt
        for x in writes:
            if x is not b:
                raise ValueError("dma writes must be the tracked sbuf")
            x.w = ("dma", b.dsem.cnt)
            x.r = {}
        return ins

    def _slot(self, kind):
        for i, sl in enumerate(self.free_slots):
            if sl.kind == kind:
                return self.free_slots.pop(i)
        sl = SemSlot(self.nc.alloc_semaphore(name=f"d_{self.nsem}"))
        sl.kind = kind
        self.nsem += 1
        self.slots.append(sl)
        return sl

    def scope(self):
        return Scope(self)

    def dma_fence(self, q="sp"):
        kn = self.known[q]
        for sl in self.slots:
            if sl.cnt > 0:
                key = ("d", id(sl))
                if kn.get(key, 0) < sl.cnt:
                    self.eng[q].wait_ge(sl.h, 16 * sl.cnt)
                    kn[key] = sl.cnt

    def barrier(self):
        for e in self.ENGS:
            kn = self.known[e]
            for o in self.ENGS:
                if (o == e and e == "pe") or self.cnt[o] == 0:
                    continue
                if kn.get(o, 0) < self.cnt[o]:
                    self.eng[e].wait_ge(self.sem[o], self.cnt[o])
                    kn[o] = self.cnt[o]
        for e in self.ENGS:
            self.dma_fence(e)

    def finish(self):
        self.barrier()


class Cfg:
    def __init__(s, D=2048, L=2048, DEPTH=2):
        s.D, s.L, s.DEPTH = D, L, DEPTH
        s.KT, s.NT, s.NB = D // 128, L // 128, L // 512
        s.GH, s.DK, s.DV, s.LOW = 4, D // 2, D, 16
        s.HK, s.HV = s.DK // 4, s.DV // 4
        s.KC, s.VC = s.HK // 128, s.HV // 128
        s.HD, s.NH, s.NG = 128, D // 128, 4
        s.R, s.NW, s.KVW = s.NH // 4, D, 512
        s.CL, s.CS, s.CH, s.SB, s.WIN = 32, 16, 256, 64, 512
        s.NCMP, s.NSEL = (L - 32) // 16 + 1, L // 64
        s.TOPK = min(16, s.NSEL)
        s.DI, s.P, s.SG, s.N, s.CONV = 2 * D, 64, 8, 128, 4
        s.SH = s.DI // 64
        s.HPG = s.SH // 8
        s.CD = s.DI + 2 * 8 * 128
        s.FF = ((8 * D + 3 * 256 - 1) // (3 * 256)) * 256
        s.FT = s.FF // 128
        s.MIX = s.DV + s.NW + s.DI
        sizes = (3 * D, s.DK, s.DK, s.DV, 16, s.DV, s.NW, 6 * s.KVW, 3 * s.NH, s.DI, s.CD, s.SH)
        names = ("gate", "q", "k", "v", "low", "r", "nq", "nkv", "ng", "z", "xbc", "dt")
        s.off = {}
        o = 0
        for n, z in zip(names, sizes):
            s.off[n] = o
            o += z
        s.IN_COLS = o


def host_consts(c):
    import ml_dtypes
    bf = ml_dtypes.bfloat16
    L = c.L
    i = np.arange(128)
    tri = (i[:, None] <= i[None, :]).astype(np.float32)
    upper = (i[:, None] > i[None, :]).astype(np.float32)
    half = 16
    inv = (500000.0 ** (-np.arange(half, dtype=np.float32) / half)).astype(np.float32)
    ang = np.arange(L, dtype=np.float32)[None, :] * inv[:, None]
    cos, sin = np.cos(ang).astype(np.float32), np.sin(ang).astype(np.float32)
    ropeC = np.concatenate([cos, cos, np.ones((96, L), np.float32)], 0)
    ropeS = np.concatenate([sin, sin, np.zeros((96, L), np.float32)], 0)
    Rm = np.zeros((128, 128), np.float32)
    for d in range(16):
        Rm[d + 16, d] = -1.0
        Rm[d, d + 16] = 1.0
    n = np.arange(128)
    cmpmask = ((16 * n[:, None] + 31) <= np.arange(L)[None, :]).astype(np.float32)
    cmpmask[c.NCMP:] = 0
    cs = np.arange(c.NCMP) * 16
    ss = np.arange(c.NSEL) * 64
    ovl = np.clip(np.minimum(cs[:, None] + 32, ss[None, :] + 64) - np.maximum(cs[:, None], ss[None, :]), 0, None
                  ).astype(np.float32) / 32
    ovl_p = np.zeros((128, c.NSEL), np.float32)
    ovl_p[:c.NCMP] = ovl
    E = (np.arange(L)[None, :] // 64 == np.arange(c.NSEL)[:, None]).astype(np.float32)
    blk_t = (np.arange(L) // 64)[:, None]
    blk_j = np.arange(c.NSEL)[None, :]
    forced = (blk_j == 0) | (blk_j == blk_t) | (blk_j == blk_t - 1)
    valid = blk_j <= blk_t
    selmul = (valid & ~forced).astype(np.float32)
    selbias = np.where(forced, 1e9, np.where(valid, 0.0, -1.0)).astype(np.float32)
    def tm(a):
        return np.ascontiguousarray(a.reshape(c.NT, 128, -1).transpose(1, 0, 2))
    return {
        "c_ident": np.eye(128, dtype=np.float32).astype(bf),
        "c_tri": tri, "c_trib": tri.astype(bf), "c_upperb": upper.astype(bf),
        "c_ropeC": ropeC, "c_ropeS": ropeS, "c_Rm": Rm.astype(bf),
        "c_cmpmask": cmpmask.astype(bf), "c_ovl": ovl_p, "c_E": E.astype(bf),
        "c_selmul": tm(selmul), "c_selbias": tm(selbias),
    }


class Ctx:
    pass


def _w3(ap2, p=128):
    return ap2.rearrange("(k p) c -> p k c", p=p)


def build(c, debug=False, phases=None, depth=None):
    nc = bass.Bass("TRN2", target_bir_lowering=False)
    k = K(nc)
    C = Ctx()
    C.nc, C.k, C.c = nc, k, c
    D, L = c.D, c.L
    DEPTH = c.DEPTH if depth is None else depth

    def inp(name, shape, dt=F32):
        return nc.dram_tensor(name, list(shape), dt, kind="ExternalInput").ap()

    def scratch(name, shape, dt):
        kind = "ExternalOutput" if debug else "Internal"
        return nc.dram_tensor(name, list(shape), dt, kind=kind).ap()

    I = {}
    I["xT"] = inp("xT", [D, L])
    I["norm_mix"] = inp("norm_mix", [c.DEPTH, 128, c.KT])
    I["w_in"] = inp("w_in", [c.DEPTH, D, c.IN_COLS])
    I["gla_w2aug"] = inp("gla_w2aug", [c.DEPTH, 17, c.DK])
    I["gla_out_norm"] = inp("gla_out_norm", [c.DEPTH, 128, c.VC])
    I["nsa_peT_k"] = inp("nsa_peT_k", [c.DEPTH, 128, 32])
    I["nsa_peT_v"] = inp("nsa_peT_v", [c.DEPTH, 128, 32])
    I["nsa_k_w1"] = inp("nsa_k_w1", [c.DEPTH, 4096, 256])
    I["nsa_k_w2"] = inp("nsa_k_w2", [c.DEPTH, 256, 128])
    I["nsa_v_w1"] = inp("nsa_v_w1", [c.DEPTH, 4096, 256])
    I["nsa_v_w2"] = inp("nsa_v_w2", [c.DEPTH, 256, 128])
    I["ssd_conv_w"] = inp("ssd_conv_w", [c.DEPTH, 128, c.CD // 128, 4])
    I["ssd_conv_b"] = inp("ssd_conv_b", [c.DEPTH, 128, c.CD // 128])
    I["ssd_rows"] = inp("ssd_rows", [c.DEPTH, 3, c.SH])
    I["ssd_out_norm"] = inp("ssd_out_norm", [c.DEPTH, c.DI])
    I["w_branch"] = inp("w_branch", [c.DEPTH, c.MIX, D])
    I["w_out"] = inp("w_out", [c.DEPTH, D, D])
    I["norm_ffn"] = inp("norm_ffn", [c.DEPTH, 128, c.KT])
    I["w_ffn_gate"] = inp("w_ffn_gate", [c.DEPTH, D, c.FF])
    I["w_ffn_up"] = inp("w_ffn_up", [c.DEPTH, D, c.FF])
    I["w_ffn_down"] = inp("w_ffn_down", [c.DEPTH, c.FF, D])
    I["norm_final"] = inp("norm_final", [128, c.KT])
    hc = host_consts(c)
    for n_, a_ in hc.items():
        I[n_] = inp(n_, a_.shape, BF16 if a_.dtype != np.float32 else F32)
    C.I = I
    C.out = nc.dram_tensor("outT", [D, L], F32, kind="ExternalOutput").ap()
    C.xres = scratch("xres", [D, L], F32)
    C.mixT = scratch("mixT", [c.MIX, L], BF16)
    C.s_xs = scratch("s_xs", [L, c.DI], BF16)
    C.s_zs = scratch("s_zs", [L, c.DI], BF16)
    C.s_bm = scratch("s_bm", [L, 1024], BF16)
    C.s_bmT = scratch("s_bmT", [1024, L], BF16)
    C.s_cmT = scratch("s_cmT", [1024, L], BF16)
    C.s_acT = scratch("s_acT", [c.SH, L], F32)
    C.debug = debug
    C.wb = {
        "gate": nc.dram_tensor("wb_gate", [D, 3 * D], BF16, kind="Internal").ap(),
        "branch": nc.dram_tensor("wb_branch", [c.MIX, D], BF16, kind="Internal").ap(),
        "out": nc.dram_tensor("wb_out", [D, D], BF16, kind="Internal").ap(),
        "fg": nc.dram_tensor("wb_fg", [D, c.FF], BF16, kind="Internal").ap(),
        "fu": nc.dram_tensor("wb_fu", [D, c.FF], BF16, kind="Internal").ap(),
        "fd": nc.dram_tensor("wb_fd", [c.FF, D], BF16, kind="Internal").ap(),
    }
    C.pc = Buf(k, None, "precast")
    C.drip = []
    if debug:
        C.d_x1 = scratch("d_x1", [D, L], F32)
        C.d_m = scratch("d_m", [D, L], BF16)
        C.d_h = scratch("d_h", [D, L], BF16)

    C.pb = [k.ps(f"pb{i}", [128, 512], F32) for i in range(8)]
    C._bank = 0

    C.rot = list(range(8))

    def bank():
        b = C.pb[C.rot[C._bank % len(C.rot)]]
        C._bank += 1
        return b
    C.bank = bank
    K_ = {}
    C.hc = hc
    for n_, a_ in hc.items():
        if n_ not in ("c_ident", "c_tri", "c_trib", "c_upperb"):
            continue
        dt = BF16 if a_.dtype != np.float32 else F32
        K_[n_] = k.sb("s" + n_, list(a_.shape), dt)
        k.dma("sp", K_[n_][:], I[n_], writes=[K_[n_]], sbuf=K_[n_])
    K_["ones_bf"] = k.sb("ones_bf", [128, 128], BF16)
    k.op("dve", lambda e: e.memset(K_["ones_bf"][:], 1.0), writes=[K_["ones_bf"]])
    K_["eps"] = k.sb("eps_col", [128, 1], F32)
    k.op("dve", lambda e: e.memset(K_["eps"][:], 1e-6), writes=[K_["eps"]])
    K_["one"] = k.sb("one_col", [128, 1], F32)
    k.op("dve", lambda e: e.memset(K_["one"][:], 1.0), writes=[K_["one"]])
    C.K = K_

    run = phases or ("init", "gla", "nsa", "ssd", "merge", "final")
    if "init" in run:
        phase_init(C)
    for l in range(DEPTH):
        with k.scope() as S0:
            dt_tok = S0.sb("dt_tok", [128, c.NT, c.SH], F32)
            acum_tok = S0.sb("acum_tok", [128, c.NT, c.SH], F32)
            with k.scope() as S:
                hT = S.sb("hT", [128, c.KT, L], BF16)
                if "merge" in run:
                    precast_plan(C, l)
                phase_norm(C, S, C.xres, I["norm_mix"][l], hT)
                if "gla" in run:
                    phase_gla(C, l, hT)
                if "nsa" in run:
                    phase_nsa(C, l, hT)
                if "ssd" in run:
                    phase_ssd_prep(C, l, hT, dt_tok, acum_tok)
            if "ssd" in run:
                phase_ssd_loop(C, l, dt_tok, acum_tok)
        if "merge" in run:
            phase_merge_ffn(C, l)
    if "final" in run:
        phase_final(C)
    k.finish()
    C.n_ins = k.n_ins
    return nc, C


def phase_init(C):
    k, c = C.k, C.c
    with k.scope() as S:
        bufs = [S.sb(f"xi{i}", [128, c.L], F32) for i in range(2)]
        for kt in range(c.KT):
            b = bufs[kt % 2]
            k.dma("sp", b[:], C.I["xT"][kt * 128:(kt + 1) * 128, :], writes=[b], sbuf=b)
            k.dma("sp", C.xres[kt * 128:(kt + 1) * 128, :], b[:], reads=[b], sbuf=b)


def phase_norm(C, S, x_dram, g_ap, hT, tbs=None, hoff=0):
    k, c = C.k, C.c
    KT = c.KT
    with k.scope() as S2:
        gcol = S2.sb("gcol", [128, KT], F32)
        k.dma("sp", gcol[:], g_ap, writes=[gcol], sbuf=gcol)
        xb = [S2.sb(f"nx{i}", [128, KT, 512], F32) for i in range(2)]
        sq = [S2.sb(f"nsq{i}", [128, 512], BF16) for i in range(2)]
        rs = S2.sb("nrs", [128, 512], F32)
        rs2 = S2.sb("nrs2", [128, 512], F32)
        for i, tb in enumerate(tbs if tbs is not None else range(c.NB)):
            x = xb[i % 2]
            k.dma("sp", x[:], _w3(x_dram[:, tb * 512:(tb + 1) * 512]), writes=[x], sbuf=x)
            ps = C.bank()
            for kt in range(KT):
                s_ = sq[kt % 2]
                k.op("act", lambda e: e.activation(out=s_[:], in_=x[:, kt, :], func=AF.Square),
                     reads=[x], writes=[s_])
                k.op("pe", lambda e: e.matmul(ps[:], lhsT=C.K["ones_bf"][:], rhs=s_[:],
                                              start=(kt == 0), stop=(kt == KT - 1)),
                     reads=[s_, C.K["ones_bf"]], writes=[ps])
            k.op("act", lambda e: e.activation(out=rs[:], in_=ps[:], func=AF.Sqrt,
                                               bias=C.K["eps"][:], scale=1.0 / c.D),
                 reads=[ps, C.K["eps"]], writes=[rs])
            k.op("dve", lambda e: e.reciprocal(out=rs2[:], in_=rs[:]), reads=[rs], writes=[rs2])
            o = (hoff + i) * 512
            for kt in range(KT):
                k.op("dve", lambda e: e.scalar_tensor_tensor(
                    out=hT[:, kt, o:o + 512], in0=x[:, kt, :], scalar=gcol[:, kt:kt + 1],
                    in1=rs2[:], op0=ALU.mult, op1=ALU.mult), reads=[x, gcol, rs2], writes=[hT])


class WRing:
    def __init__(self, C, S, name, shape, n=2, q="pool"):
        self.C, self.q = C, q
        self.bufs = [S.sb(f"{name}{i}", shape, BF16) for i in range(n)]
        self.i = 0

    def load(self, src3, kt, ncols):
        b = self.bufs[self.i % len(self.bufs)]
        self.i += 1
        self.C.k.dma(self.q, b[:, 0:kt, 0:ncols], src3, writes=[b], sbuf=b)
        if self.q == "pool":
            drip(self.C, 2)
        return b


def mm_acc(C, ps_ap, ps_buf, pairs, rbufs):
    n = len(pairs)
    for i, (l_, r_) in enumerate(pairs):
        C.k.op("pe", lambda e: e.matmul(ps_ap, lhsT=l_, rhs=r_, start=(i == 0), stop=(i == n - 1)),
               reads=rbufs, writes=[ps_buf], sig=(i == n - 1))


def phase_gla(C, l, hT):
    k, c, I, Kc = C.k, C.c, C.I, C.K
    D, L, KT, NT, NB = c.D, c.L, c.KT, c.NT, c.NB
    HK, HV, KC, VC = c.HK, c.HV, c.KC, c.VC
    w_in = I["w_in"][l]
    with k.scope() as S:
        wr = WRing(C, S, "gw", [128, KT, 512])
        gaug = S.sb("gaug", [32, L], BF16)
        w2aug = S.sb("w2aug", [32, c.DK], BF16)
        k.op("dve", lambda e: e.memset(gaug[:], 1.0), writes=[gaug])
        k.dma("pool", w2aug[0:17, :], I["gla_w2aug"][l], writes=[w2aug], sbuf=w2aug)
        wl = wr.load(_w3(w_in[:, c.off["low"]:c.off["low"] + 16]), KT, 16)
        for tb in range(NB):
            ps = C.bank()
            mm_acc(C, ps[0:16, :], ps, [(wl[:, kt, 0:16], hT[:, kt, tb * 512:(tb + 1) * 512]) for kt in range(KT)],
                   [wl, hT])
            k.op("act", lambda e: e.activation(out=gaug[0:16, tb * 512:(tb + 1) * 512], in_=ps[0:16, :], func=AF.Copy),
                 reads=[ps], writes=[gaug])
        gn = S.sb("gnorm", [128, VC], F32)
        k.dma("sp", gn[:], I["gla_out_norm"][l], writes=[gn], sbuf=gn)
        for hd in range(c.GH):
            phase_gla_head(C, S, l, hT, hd, wr, gaug, w2aug, gn)


def phase_gla_head(C, S0, l, hT, hd, wr, gaug, w2aug, gn):
    k, c, I, Kc = C.k, C.c, C.I, C.K
    D, L, KT, NT, NB = c.D, c.L, c.KT, c.NT, c.NB
    HK, HV, KC, VC = c.HK, c.HV, c.KC, c.VC
    w_in = I["w_in"][l]
    tri = Kc["c_tri"]
    with k.scope() as S:
        qT = S.sb("qT", [128, KC, L], BF16)
        kT = S.sb("kT", [128, KC, L], BF16)
        ktok = S.sb("ktok", [128, NT, HK], BF16)
        vtok = S.sb("vtok", [128, NT, HV], BF16)
        srT = S.sb("srT", [128, VC, L], BF16)
        ebl = S.sb("ebl", [128, KC, NT], F32)
        ogT = S.sb("ogT", [128, VC, L], BF16)
        def fm(col0, ncol, dst, func):
            w = wr.load(_w3(w_in[:, col0:col0 + ncol]), KT, ncol)
            for ct in range(ncol // 128):
                for tb in range(NB):
                    ps = C.bank()
                    mm_acc(C, ps[:], ps, [(w[:, kt, ct * 128:(ct + 1) * 128], hT[:, kt, tb * 512:(tb + 1) * 512])
                                          for kt in range(KT)], [w, hT])
                    k.op("act", lambda e: e.activation(out=dst[:, ct, tb * 512:(tb + 1) * 512], in_=ps[:], func=func),
                         reads=[ps], writes=[dst])
            return w

        def tm(w, ncol, dst):
            for tt in range(NT):
                ps = C.bank()
                mm_acc(C, ps[:, 0:ncol], ps, [(hT[:, kt, tt * 128:(tt + 1) * 128], w[:, kt, 0:ncol])
                                              for kt in range(KT)], [w, hT])
                k.op("act", lambda e: e.activation(out=dst[:, tt, :], in_=ps[:, 0:ncol], func=AF.Copy),
                     reads=[ps], writes=[dst])

        fm(c.off["q"] + hd * HK, HK, qT, AF.Copy)
        wk = fm(c.off["k"] + hd * HK, HK, kT, AF.Copy)
        tm(wk, HK, ktok)
        wv = wr.load(_w3(w_in[:, c.off["v"] + hd * HV: c.off["v"] + (hd + 1) * HV]), KT, HV)
        tm(wv, HV, vtok)
        fm(c.off["r"] + hd * HV, HV, srT, AF.Silu)
        spb = [S.sb(f"sp{i}", [128, HK], F32) for i in range(2)]
        t1 = [S.sb(f"gt1_{i}", [128, HK], F32) for i in range(2)]
        t2 = [S.sb(f"gt2_{i}", [128, 128], F32) for i in range(2)]
        t3 = [S.sb(f"gt3_{i}", [128, 128], F32) for i in range(2)]
        for tt in range(NT):
            sl = slice(tt * 128, (tt + 1) * 128)
            sp, a1 = spb[tt % 2], t1[tt % 2]
            ps = C.bank()
            k.op("pe", lambda e: e.matmul(ps[:, 0:HK], lhsT=gaug[0:17, sl], rhs=w2aug[0:17, hd * HK:(hd + 1) * HK],
                                          start=True, stop=True), reads=[gaug, w2aug], writes=[ps])
            k.op("act", lambda e: e.activation(out=a1[:], in_=ps[:, 0:HK], func=AF.Exp, scale=-1.0),
                 reads=[ps], writes=[a1])
            k.op("act", lambda e: e.activation(out=sp[:], in_=a1[:], func=AF.Ln, bias=Kc["one"][:], scale=1.0),
                 reads=[a1, Kc["one"]], writes=[sp])
            ps2 = C.bank()
            k.op("pe", lambda e: e.matmul(ps2[:, 0:HK], lhsT=tri[:], rhs=sp[:], start=True, stop=True),
                 reads=[tri, sp], writes=[ps2])
            k.op("act", lambda e: e.activation(out=a1[:], in_=ps2[:, 0:HK], func=AF.Exp, scale=1.0 / 16),
                 reads=[ps2], writes=[a1])
            k.op("dve", lambda e: e.tensor_tensor(out=ktok[:, tt, :], in0=ktok[:, tt, :], in1=a1[:], op=ALU.mult),
                 reads=[ktok, a1], writes=[ktok])
            for kc in range(KC):
                ep, en = t2[kc % 2], t3[kc % 2]
                ps3 = C.bank()
                k.op("pe", lambda e: e.matmul(ps3[:, 0:128], lhsT=sp[:, kc * 128:(kc + 1) * 128], rhs=tri[:],
                                              start=True, stop=True), reads=[tri, sp], writes=[ps3])
                k.op("act", lambda e: e.activation(out=ep[:], in_=ps3[:, 0:128], func=AF.Exp, scale=1.0 / 16),
                     reads=[ps3], writes=[ep])
                k.op("act", lambda e: e.activation(out=en[:], in_=ps3[:, 0:128], func=AF.Exp, scale=-1.0 / 16),
                     reads=[ps3], writes=[en])
                k.op("dve", lambda e: e.tensor_tensor(out=kT[:, kc, sl], in0=kT[:, kc, sl], in1=ep[:], op=ALU.mult),
                     reads=[kT, ep], writes=[kT])
                k.op("dve", lambda e: e.scalar_tensor_tensor(out=qT[:, kc, sl], in0=qT[:, kc, sl],
                                                             scalar=float(HK) ** -0.5, in1=en[:],
                                                             op0=ALU.mult, op1=ALU.mult),
                     reads=[qT, en], writes=[qT])
                k.op("act", lambda e: e.activation(out=ebl[:, kc, tt:tt + 1], in_=en[:, 127:128], func=AF.Copy),
                     reads=[en], writes=[ebl])
        S32 = S.sb("S32", [128, KC, HV], F32)
        Sb = S.sb("Sb", [128, KC, HV], BF16)
        k.op("dve", lambda e: e.memset(S32[:], 0.0), writes=[S32])
        k.op("dve", lambda e: e.memset(Sb[:], 0.0), writes=[Sb])
        scm = [S.sb(f"scm{i}", [128, 128], BF16) for i in range(2)]
        sq = [S.sb(f"gsq{i}", [128, VC, 128], BF16) for i in range(2)]
        rsd = [S.sb(f"grs{i}", [128, 128], F32) for i in range(2)]
        tmp = [S.sb(f"gtm{i}", [128, 128], F32) for i in range(2)]
        tS = [S.sb(f"gtS{i}", [128, HV], F32) for i in range(2)]
        for ch in range(NT):
            sl = slice(ch * 128, (ch + 1) * 128)
            sc, sqb, rs = scm[ch % 2], sq[ch % 2], rsd[ch % 2]
            ps = C.bank()
            mm_acc(C, ps[:, 0:128], ps, [(kT[:, kc, sl], qT[:, kc, sl]) for kc in range(KC)], [kT, qT])
            k.op("dve", lambda e: e.tensor_tensor(out=sc[:], in0=ps[:, 0:128], in1=tri[:], op=ALU.mult),
                 reads=[ps, tri], writes=[sc])
            po = C.bank()
            for vc in range(VC):
                pairs = [(vtok[:, ch, vc * 128:(vc + 1) * 128], sc[:])]
                pairs += [(Sb[:, kc, vc * 128:(vc + 1) * 128], qT[:, kc, sl]) for kc in range(KC)]
                mm_acc(C, po[:, vc * 128:(vc + 1) * 128], po, pairs, [vtok, sc, Sb, qT])
            k.op("act", lambda e: e.activation(out=sqb[:].rearrange("p a b -> p (a b)"), in_=po[:, 0:VC * 128],
                                               func=AF.Square), reads=[po], writes=[sqb])
            pss = C.bank()
            mm_acc(C, pss[:, 0:128], pss, [(Kc["ones_bf"][:], sqb[:, vc, :]) for vc in range(VC)], [sqb, Kc["ones_bf"]])
            k.op("act", lambda e: e.activation(out=tmp[0][:], in_=pss[:, 0:128], func=AF.Sqrt, bias=Kc["eps"][:],
                                               scale=1.0 / HV), reads=[pss, Kc["eps"]], writes=[tmp[0]])
            k.op("dve", lambda e: e.reciprocal(out=rs[:], in_=tmp[0][:]), reads=[tmp[0]], writes=[rs])
            for vc in range(VC):
                k.op("dve", lambda e: e.scalar_tensor_tensor(out=tmp[1][:], in0=po[:, vc * 128:(vc + 1) * 128],
                                                             scalar=gn[:, vc:vc + 1], in1=rs[:],
                                                             op0=ALU.mult, op1=ALU.mult),
                     reads=[po, gn, rs], writes=[tmp[1]])
                k.op("dve", lambda e: e.tensor_tensor(out=ogT[:, vc, sl], in0=tmp[1][:], in1=srT[:, vc, sl], op=ALU.mult),
                     reads=[tmp[1], srT], writes=[ogT])
            for kc in range(KC):
                pd = C.bank()
                k.op("pe", lambda e: e.matmul(pd[:, 0:HV], lhsT=ktok[:, ch, kc * 128:(kc + 1) * 128], rhs=vtok[:, ch, :],
                                              start=True, stop=True), reads=[ktok, vtok], writes=[pd])
                ts = tS[kc % 2]
                k.op("dve", lambda e: e.tensor_scalar(out=ts[:], in0=pd[:, 0:HV], scalar1=ebl[:, kc, ch:ch + 1],
                                                      scalar2=None, op0=ALU.mult), reads=[pd, ebl], writes=[ts])
                k.op("dve", lambda e: e.scalar_tensor_tensor(out=S32[:, kc, :], in0=S32[:, kc, :],
                                                             scalar=ebl[:, kc, ch:ch + 1], in1=ts[:],
                                                             op0=ALU.mult, op1=ALU.add),
                     reads=[S32, ebl, ts], writes=[S32])
                k.op("act", lambda e: e.activation(out=Sb[:, kc, :], in_=S32[:, kc, :], func=AF.Copy),
                     reads=[S32], writes=[Sb])
        for vc in range(VC):
            r0 = hd * HV + vc * 128
            k.dma("sp", C.mixT[r0:r0 + 128, :], ogT[:, vc, :], reads=[ogT], sbuf=ogT)


def shared_inputs(c, inp):
    f = np.float32
    def colT(v, kt):
        v = np.asarray(v, f)
        return np.ascontiguousarray(v.reshape(v.shape[0], kt, 128).transpose(0, 2, 1))
    m = {}
    m["norm_mix"] = colT(inp["norm_mix"], c.KT)
    m["w_in"] = np.asarray(inp["w_in"], f)
    m["gla_w2aug"] = np.ascontiguousarray(np.concatenate(
        [np.asarray(inp["gla_gate_w2"], f), np.asarray(inp["gla_gate_b"], f)[:, None, :]], axis=1))
    m["gla_out_norm"] = colT(inp["gla_out_norm"], c.VC)
    m["nsa_peT_k"] = np.ascontiguousarray(np.asarray(inp["nsa_cmp_pos_k"], f).transpose(0, 2, 1))
    m["nsa_peT_v"] = np.ascontiguousarray(np.asarray(inp["nsa_cmp_pos_v"], f).transpose(0, 2, 1))
    m["nsa_k_w1"] = np.asarray(inp["nsa_cmp_k_w1"], f)
    m["nsa_k_w2"] = np.asarray(inp["nsa_cmp_k_w2"], f)
    m["nsa_v_w1"] = np.asarray(inp["nsa_cmp_v_w1"], f)
    m["nsa_v_w2"] = np.asarray(inp["nsa_cmp_v_w2"], f)
    cw = np.asarray(inp["ssd_conv_w"], f)
    CT = c.CD // 128
    m["ssd_conv_w"] = np.ascontiguousarray(cw.transpose(0, 2, 1).reshape(cw.shape[0], CT, 128, 4).transpose(0, 2, 1, 3))
    m["ssd_conv_b"] = colT(inp["ssd_conv_b"], CT)
    m["ssd_rows"] = np.ascontiguousarray(np.stack(
        [np.asarray(inp["ssd_dt_bias"], f), np.asarray(inp["ssd_a_log"], f), np.asarray(inp["ssd_d"], f)], axis=1))
    m["ssd_out_norm"] = np.asarray(inp["ssd_out_norm"], f)
    m["w_branch"] = np.asarray(inp["w_branch"], f)
    m["w_out"] = np.asarray(inp["w_out"], f)
    m["norm_ffn"] = colT(inp["norm_ffn"], c.KT)
    m["w_ffn_gate"] = np.asarray(inp["w_ffn_gate"], f)
    m["w_ffn_up"] = np.asarray(inp["w_ffn_up"], f)
    m["w_ffn_down"] = np.asarray(inp["w_ffn_down"], f)
    m["norm_final"] = colT(np.asarray(inp["norm_final"], f)[None], c.KT)[0]
    m.update(host_consts(c))
    return m


def bc_mid(ap2, n):
    return ap2.unsqueeze(1).to_broadcast([ap2.shape[0], n, ap2.shape[1]])


def phase_nsa(C, l, hT):
    k, c, I, Kc = C.k, C.c, C.I, C.K
    KT = c.KT
    with k.scope() as S:
        for n_, a_ in C.hc.items():
            if n_ in ("c_ident", "c_tri", "c_trib", "c_upperb"):
                continue
            dt = BF16 if a_.dtype != np.float32 else F32
            Kc[n_] = S.sb("s" + n_, list(a_.shape), dt)
            k.dma("sp", Kc[n_][:], I[n_], writes=[Kc[n_]], sbuf=Kc[n_])
        wr = WRing(C, S, "nw", [128, KT, 128], n=3)
        w1 = {}
        w2 = {}
        pe = {}
        for kind in ("k", "v"):
            w2[kind] = S.sb(f"w2{kind}", [128, 2, 128], BF16)
            k.dma("pool", w2[kind][:], _w3(I[f"nsa_{kind}_w2"][l]), writes=[w2[kind]], sbuf=w2[kind])
            pe[kind] = S.sb(f"pe{kind}", [128, 32], F32)
            k.dma("sp", pe[kind][:], I[f"nsa_peT_{kind}"][l], writes=[pe[kind]], sbuf=pe[kind])
        for g in range(c.NG):
            C.rot = [0, 1, 2, 3]
            nsa_group(C, l, hT, g, wr, w1, w2, pe)
            C.rot = list(range(8))


def nsa_group(C, l, hT, g, wr, w1, w2, pe):
    k, c, I, Kc = C.k, C.c, C.I, C.K
    D, L, KT, NT, NB, R = c.D, c.L, c.KT, c.NT, c.NB, c.R
    NCMP, NSEL = c.NCMP, c.NSEL
    NA = 129 + NSEL
    w_in = I["w_in"][l]
    scale = 128.0 ** -0.5
    trib, upb, ident = Kc["c_trib"], Kc["c_upperb"], Kc["c_ident"]
    with k.scope() as S:
        qT = S.sb("nqT", [128, R, L], BF16)
        kx = {n_: S.sb(f"n{n_}", [128, L], BF16) for n_ in ("ks", "kw")}
        vsl = S.sb("nvsl", [128, NT, 129], BF16)
        vw = S.sb("nvw", [128, NT, 129], BF16)
        gt = S.sb("ngt", [128, NT, 3 * R], F32)
        onT = S.sb("onT", [128, R, L], BF16)
        ra = [S.sb(f"nra{i}", [128, 512], F32) for i in range(2)]
        rb = [S.sb(f"nrb{i}", [128, 512], F32) for i in range(2)]

        def fm_rope(w, wc0, dst_ap_fn, dstbuf, rope):
            for tb in range(NB):
                tsl = slice(tb * 512, (tb + 1) * 512)
                ps = C.bank()
                mm_acc(C, ps[:], ps, [(w[:, kt, wc0:wc0 + 128], hT[:, kt, tsl]) for kt in range(KT)], [w, hT])
                dst = dst_ap_fn(tsl)
                k.op("act", lambda e: e.activation(out=dst, in_=ps[:], func=AF.Copy), reads=[ps], writes=[dstbuf])
                if rope:
                    pr = C.bank()
                    a_, b_ = ra[tb % 2], rb[tb % 2]
                    k.op("pe", lambda e: e.matmul(pr[:, :], lhsT=Kc["c_Rm"][:], rhs=dst, start=True, stop=True),
                         reads=[Kc["c_Rm"], dstbuf], writes=[pr])
                    k.op("dve", lambda e: e.tensor_tensor(out=a_[:], in0=ps[:, :], in1=Kc["c_ropeC"][:, tsl], op=ALU.mult),
                         reads=[ps, Kc["c_ropeC"]], writes=[a_])
                    k.op("dve", lambda e: e.tensor_tensor(out=b_[:], in0=pr[:, :], in1=Kc["c_ropeS"][:, tsl], op=ALU.mult),
                         reads=[pr, Kc["c_ropeS"]], writes=[b_])
                    k.op("dve", lambda e: e.tensor_tensor(out=dst, in0=a_[:], in1=b_[:], op=ALU.add),
                         reads=[a_, b_], writes=[dstbuf])

        q0 = c.off["nq"] + g * R * 128
        for r in range(R):
            w = wr.load(_w3(w_in[:, q0 + r * 128:q0 + (r + 1) * 128]), KT, 128)
            fm_rope(w, 0, lambda tsl, r=r: qT[:, r, tsl], qT, True)
        def kvcol(kind):
            return c.off["nkv"] + kind * c.KVW + g * 128
        for kind, name, rope in ((2, "ks", True), (4, "kw", True)):
            w = wr.load(_w3(w_in[:, kvcol(kind):kvcol(kind) + 128]), KT, 128)
            fm_rope(w, 0, lambda tsl, name=name: kx[name][:, tsl], kx[name], rope)
        for kind, dst in ((3, vsl), (5, vw)):
            w = wr.load(_w3(w_in[:, kvcol(kind):kvcol(kind) + 128]), KT, 128)
            k.op("dve", lambda e: e.memset(dst[:], 1.0), writes=[dst])
            for tt in range(NT):
                ps = C.bank()
                mm_acc(C, ps[:, 0:128], ps, [(hT[:, kt, tt * 128:(tt + 1) * 128], w[:, kt, 0:128]) for kt in range(KT)], [w, hT])
                k.op("act", lambda e: e.activation(out=dst[:, tt, 0:128], in_=ps[:, 0:128], func=AF.Copy), reads=[ps], writes=[dst])
        g0 = c.off["ng"] + g * R * 3
        w = wr.load(_w3(w_in[:, g0:g0 + 3 * R]), KT, 3 * R)
        for tt in range(NT):
            ps = C.bank()
            mm_acc(C, ps[:, 0:3 * R], ps, [(hT[:, kt, tt * 128:(tt + 1) * 128], w[:, kt, 0:3 * R]) for kt in range(KT)], [w, hT])
            k.op("act", lambda e: e.activation(out=gt[:, tt, :], in_=ps[:, 0:3 * R], func=AF.Sigmoid), reads=[ps], writes=[gt])

        kcT = S.sb("kcT", [128, 128], BF16)
        vaug = S.sb("vaug", [128, NA], BF16)
        k.op("dve", lambda e: e.memset(vaug[:], 1.0), writes=[vaug])
        k.op("dve", lambda e: e.tensor_copy(out=vaug[:, 129:NA], in_=Kc["c_ovl"][:]), reads=[Kc["c_ovl"]], writes=[vaug])
        with k.scope() as S3:
            for n_ in ("kc", "vc"):
                kx[n_] = S3.sb(f"n{n_}", [128, L], BF16)
            kpe = S3.sb("kpe", [128, 32, NCMP], BF16)
            hs = S3.sb("nhs", [128, 2, 128], BF16)
            w1b = S3.sb("w1b", [128, 32, 256], BF16)
            for kind, name, rope in ((0, "kc", True), (1, "vc", False)):
                w = wr.load(_w3(w_in[:, kvcol(kind):kvcol(kind) + 128]), KT, 128)
                fm_rope(w, 0, lambda tsl, name=name: kx[name][:, tsl], kx[name], rope)
            for kind, src in (("k", kx["kc"]), ("v", kx["vc"])):
                k.dma("pool", w1b[:], _w3(I[f"nsa_{kind}_w1"][l]), writes=[w1b], sbuf=w1b)
                for l_ in range(32):
                    k.op("dve", lambda e: e.tensor_scalar(out=kpe[:, l_, :], in0=src[:, l_:l_ + 16 * (NCMP - 1) + 1:16],
                                                          scalar1=pe[kind][:, l_:l_ + 1], scalar2=None, op0=ALU.add),
                         reads=[src, pe[kind]], writes=[kpe])
                for cc in range(2):
                    ps = C.bank()
                    mm_acc(C, ps[:, 0:NCMP], ps, [(w1b[:, l_, cc * 128:(cc + 1) * 128], kpe[:, l_, :]) for l_ in range(32)],
                           [w1b, kpe])
                    k.op("act", lambda e: e.activation(out=hs[:, cc, 0:NCMP], in_=ps[:, 0:NCMP], func=AF.Silu), reads=[ps], writes=[hs])
                ps = C.bank()
                if kind == "k":
                    mm_acc(C, ps[:, 0:NCMP], ps, [(w2[kind][:, cc, :], hs[:, cc, 0:NCMP]) for cc in range(2)], [w2[kind], hs])
                    k.op("act", lambda e: e.activation(out=kcT[:, 0:NCMP], in_=ps[:, 0:NCMP], func=AF.Copy), reads=[ps], writes=[kcT])
                else:
                    mm_acc(C, ps[0:NCMP, 0:128], ps, [(hs[:, cc, 0:NCMP], w2[kind][:, cc, :]) for cc in range(2)], [w2[kind], hs])
                    k.op("act", lambda e: e.activation(out=vaug[0:NCMP, 0:128], in_=ps[0:NCMP, 0:128], func=AF.Copy), reads=[ps], writes=[vaug])

        oacc = [S.sb(f"oacc{i}", [128, R, 128], F32) for i in range(2)]
        ob = [S.sb(f"nob{i}", [128, R, 128], BF16) for i in range(2)]
        eb = [S.sb(f"neb{i}", [128, R, 128], BF16) for i in range(3)]
        pslc = S.sb("pslc", [128, NSEL], F32)
        score = S.sb("score", [128, NSEL], F32)
        sc2 = S.sb("sc2", [128, NSEL], F32)
        m8 = S.sb("m8", [128, 8], F32)
        m8b = S.sb("m8b", [128, 8], F32)
        selb = S.sb("selb", [128, NSEL], BF16)
        selT = S.sb("selT", [32, 128], BF16)
        rd = [S.sb(f"nrd{i}", [128, 1], F32) for i in range(4)]
        wv_ = [S.sb(f"nwv{i}", [128, 1], F32) for i in range(4)]
        pacc = [C.pb[4 + r] for r in range(R)]
        ebi = 0

        def finish_branch(T, br, first):
            for r in range(R):
                pa = pacc[r]
                k.op("dve", lambda e: e.tensor_scalar(out=rd[r][:], in0=pa[:, 128:129], scalar1=1e-30, scalar2=None, op0=ALU.max),
                     reads=[pa], writes=[rd[r]])
                k.op("dve", lambda e: e.reciprocal(out=rd[r][:], in_=rd[r][:]), reads=[rd[r]], writes=[rd[r]])
                k.op("dve", lambda e: e.tensor_tensor(out=wv_[r][:], in0=rd[r][:], in1=gt[:, T, 3 * r + br:3 * r + br + 1], op=ALU.mult),
                     reads=[rd[r], gt], writes=[wv_[r]])
                oa = oacc[T % 2]
                if first:
                    k.op("dve", lambda e: e.tensor_scalar(out=oa[:, r, :], in0=pa[:, 0:128], scalar1=wv_[r][:], scalar2=None, op0=ALU.mult),
                         reads=[pa, wv_[r]], writes=[oa])
                else:
                    k.op("dve", lambda e: e.scalar_tensor_tensor(out=oa[:, r, :], in0=pa[:, 0:128], scalar=wv_[r][:], in1=oa[:, r, :],
                                                                 op0=ALU.mult, op1=ALU.add), reads=[pa, wv_[r], oa], writes=[oa])

        for T in range(NT):
            sl = slice(T * 128, (T + 1) * 128)
            qv = qT[:, 0:R, sl]
            ps = C.bank()
            k.op("pe", lambda e: e.matmul(ps[0:NCMP, 0:R * 128].rearrange("p (r t) -> p r t", r=R), lhsT=kcT[:, 0:NCMP], rhs=qv,
                                          start=True, stop=True), reads=[kcT, qT], writes=[ps])
            e_ = eb[ebi % 3]; ebi += 1
            k.op("act", lambda e: e.activation(out=e_[0:NCMP].rearrange("p r t -> p (r t)"), in_=ps[0:NCMP, 0:R * 128], func=AF.Exp, scale=scale),
                 reads=[ps], writes=[e_])
            k.op("dve", lambda e: e.tensor_tensor(out=e_[0:NCMP], in0=e_[0:NCMP], in1=bc_mid(Kc["c_cmpmask"][0:NCMP, sl], R), op=ALU.mult),
                 reads=[e_, Kc["c_cmpmask"]], writes=[e_])
            for r in range(R):
                pa = pacc[r]
                k.op("pe", lambda e: e.matmul(pa[:, 0:NA], lhsT=e_[0:NCMP, r, :], rhs=vaug[0:NCMP, :], start=True, stop=True),
                     reads=[e_, vaug], writes=[pa])
            finish_branch(T, 0, True)
            for r in range(R):
                pa = pacc[r]
                if r == 0:
                    k.op("dve", lambda e: e.tensor_scalar(out=pslc[:], in0=pa[:, 129:NA], scalar1=rd[r][:], scalar2=None, op0=ALU.mult),
                         reads=[pa, rd[r]], writes=[pslc])
                else:
                    k.op("dve", lambda e: e.scalar_tensor_tensor(out=pslc[:], in0=pa[:, 129:NA], scalar=rd[r][:], in1=pslc[:],
                                                                 op0=ALU.mult, op1=ALU.add), reads=[pa, rd[r], pslc], writes=[pslc])
            if c.TOPK < NSEL:
                assert c.TOPK == 16
                k.op("dve", lambda e: e.tensor_tensor(out=score[:], in0=pslc[:], in1=Kc["c_selmul"][:, T, :], op=ALU.mult),
                     reads=[pslc, Kc["c_selmul"]], writes=[score])
                k.op("dve", lambda e: e.tensor_tensor(out=score[:], in0=score[:], in1=Kc["c_selbias"][:, T, :], op=ALU.add),
                     reads=[score, Kc["c_selbias"]], writes=[score])
                k.op("dve", lambda e: e.max(out=m8[:], in_=score[:]), reads=[score], writes=[m8])
                k.op("dve", lambda e: e.match_replace(out=sc2[:], in_to_replace=m8[:], in_values=score[:], imm_value=-2.0),
                     reads=[score, m8], writes=[sc2])
                k.op("dve", lambda e: e.max(out=m8b[:], in_=sc2[:]), reads=[sc2], writes=[m8b])
                k.op("dve", lambda e: e.tensor_scalar(out=selb[:], in0=score[:], scalar1=m8b[:, 7:8], scalar2=None, op0=ALU.is_ge),
                     reads=[score, m8b], writes=[selb])
            else:
                k.op("dve", lambda e: e.memset(selb[:], 1.0), writes=[selb])
            pt = C.bank()
            ptb = pt[:].bitcast(BF16)
            k.op("pe", lambda e: e.transpose(out=ptb[0:NSEL, 0:128], in_=selb[:], identity=ident[:]), reads=[selb, ident], writes=[pt])
            k.op("act", lambda e: e.activation(out=selT[0:NSEL, :], in_=ptb[0:NSEL, 0:128], func=AF.Copy), reads=[pt], writes=[selT])
            for br, kT_, vT_ in ((1, kx["ks"], vsl), (2, kx["kw"], vw)):
                kts = list(range(0, T + 1)) if br == 1 else list(range(max(0, T - c.WIN // 128), T + 1))
                for i, kt in enumerate(kts):
                    ksl = slice(kt * 128, (kt + 1) * 128)
                    ps = C.bank()
                    k.op("pe", lambda e: e.matmul(ps[:, 0:R * 128].rearrange("p (r t) -> p r t", r=R), lhsT=kT_[:, ksl], rhs=qv,
                                                  start=True, stop=True), reads=[kT_, qT], writes=[ps])
                    e_ = eb[ebi % 3]; ebi += 1
                    k.op("act", lambda e: e.activation(out=e_[:].rearrange("p r t -> p (r t)"), in_=ps[:, 0:R * 128], func=AF.Exp, scale=scale),
                         reads=[ps], writes=[e_])
                    if kt == T:
                        k.op("dve", lambda e: e.tensor_tensor(out=e_[:], in0=e_[:], in1=bc_mid(trib[:], R), op=ALU.mult),
                             reads=[e_, trib], writes=[e_])
                    elif br == 1:
                        pm = C.bank()
                        k.op("pe", lambda e: e.matmul(pm[:, 0:128], lhsT=Kc["c_E"][0:NSEL, ksl], rhs=selT[0:NSEL, :], start=True, stop=True),
                             reads=[Kc["c_E"], selT], writes=[pm])
                        k.op("dve", lambda e: e.tensor_tensor(out=e_[:], in0=e_[:], in1=bc_mid(pm[:, 0:128], R), op=ALU.mult),
                             reads=[e_, pm], writes=[e_])
                    elif kt == T - c.WIN // 128:
                        k.op("dve", lambda e: e.tensor_tensor(out=e_[:], in0=e_[:], in1=bc_mid(upb[:], R), op=ALU.mult),
                             reads=[e_, upb], writes=[e_])
                    for r in range(R):
                        pa = pacc[r]
                        k.op("pe", lambda e: e.matmul(pa[:, 0:129], lhsT=e_[:, r, :], rhs=vT_[:, kt, :], start=(i == 0), stop=(i == len(kts) - 1)),
                             reads=[e_, vT_], writes=[pa], sig=(i == len(kts) - 1))
                finish_branch(T, br, False)
            oa, o_ = oacc[T % 2], ob[T % 2]
            k.op("act", lambda e: e.activation(out=o_[:].rearrange("p r t -> p (r t)"), in_=oa[:].rearrange("p r t -> p (r t)"), func=AF.Copy),
                 reads=[oa], writes=[o_])
            for r in range(R):
                pt = C.bank()
                ptb = pt[:].bitcast(BF16)
                k.op("pe", lambda e: e.transpose(out=ptb[:, 0:128], in_=o_[:, r, :], identity=ident[:]), reads=[o_, ident], writes=[pt])
                k.op("act", lambda e: e.activation(out=onT[:, r, sl], in_=ptb[:, 0:128], func=AF.Copy), reads=[pt], writes=[onT])
        for r in range(R):
            r0 = c.DV + (g * R + r) * 128
            k.dma("sp", C.mixT[r0:r0 + 128, :], onT[:, r, :], reads=[onT], sbuf=onT)


def bc_last(ap2, n):
    return ap2.unsqueeze(2).to_broadcast([ap2.shape[0], ap2.shape[1], n])


def phase_ssd_prep(C, l, hT, dt_tok, acum_tok):
    k, c, I, Kc = C.k, C.c, C.I, C.K
    D, L, KT, NT, NB = c.D, c.L, c.KT, c.NT, c.NB
    DI, SH, CD = c.DI, c.SH, c.CD
    w_in = I["w_in"][l]
    tri, ident = Kc["c_tri"], Kc["c_ident"]
    XT = DI // 128
    with k.scope() as S:
        wr = WRing(C, S, "sw", [128, KT, 512])
        stg = [S.sb(f"sstg{i}", [128, NT, 512], BF16) for i in range(2)]
        si = 0
        for cb in range(DI // 512):
            w = wr.load(_w3(w_in[:, c.off["z"] + cb * 512: c.off["z"] + (cb + 1) * 512]), KT, 512)
            st = stg[si % 2]; si += 1
            for tt in range(NT):
                ps = C.bank()
                mm_acc(C, ps[:], ps, [(hT[:, kt, tt * 128:(tt + 1) * 128], w[:, kt, :]) for kt in range(KT)], [w, hT])
                k.op("act", lambda e: e.activation(out=st[:, tt, :], in_=ps[:], func=AF.Silu), reads=[ps], writes=[st])
            k.dma("sp", C.s_zs[:, cb * 512:(cb + 1) * 512].rearrange("(t p) c -> p t c", p=128), st[:], reads=[st], sbuf=st)
        cw = S.sb("scw", [128, CD // 128, 4], F32)
        cbias = S.sb("scb", [128, CD // 128], F32)
        k.dma("sp", cw[:], I["ssd_conv_w"][l], writes=[cw], sbuf=cw)
        k.dma("sp", cbias[:], I["ssd_conv_b"][l], writes=[cbias], sbuf=cbias)
        xc = [S.sb(f"sxc{i}", [128, L + 4], F32) for i in range(2)]
        acc = [S.sb(f"sacc{i}", [128, L], F32) for i in range(2)]
        yT = [S.sb(f"syT{i}", [128, L], BF16) for i in range(2)]
        for b_ in xc:
            k.op("dve", lambda e: e.memset(b_[:, 0:4], 0.0), writes=[b_])
        for cb in range(CD // 512):
            w = wr.load(_w3(w_in[:, c.off["xbc"] + cb * 512: c.off["xbc"] + (cb + 1) * 512]), KT, 512)
            st = None
            for ci in range(4):
                ct = cb * 4 + ci
                x_, a_, y_ = xc[ct % 2], acc[ct % 2], yT[ct % 2]
                for tb in range(NB):
                    ps = C.bank()
                    mm_acc(C, ps[:], ps, [(w[:, kt, ci * 128:(ci + 1) * 128], hT[:, kt, tb * 512:(tb + 1) * 512]) for kt in range(KT)], [w, hT])
                    k.op("act", lambda e: e.activation(out=x_[:, 3 + tb * 512: 3 + (tb + 1) * 512], in_=ps[:], func=AF.Copy),
                         reads=[ps], writes=[x_])
                k.op("dve", lambda e: e.tensor_scalar(out=a_[:], in0=x_[:, 0:L], scalar1=cw[:, ct, 0:1], scalar2=None, op0=ALU.mult),
                     reads=[x_, cw], writes=[a_])
                for j in range(1, 4):
                    k.op("dve", lambda e: e.scalar_tensor_tensor(out=a_[:], in0=x_[:, j:j + L], scalar=cw[:, ct, j:j + 1], in1=a_[:],
                                                                 op0=ALU.mult, op1=ALU.add), reads=[x_, cw, a_], writes=[a_])
                k.op("act", lambda e: e.activation(out=y_[:], in_=a_[:], func=AF.Silu, bias=cbias[:, ct:ct + 1], scale=1.0),
                     reads=[a_, cbias], writes=[y_])
                is_x = ct < XT
                is_b = XT <= ct < XT + 8
                if is_x or is_b:
                    if st is None:
                        st = stg[si % 2]; si += 1
                    for tt in range(NT):
                        pt = C.bank()
                        ptb = pt[:].bitcast(BF16)
                        k.op("pe", lambda e: e.transpose(out=ptb[:, 0:128], in_=y_[:, tt * 128:(tt + 1) * 128], identity=ident[:]),
                             reads=[y_, ident], writes=[pt])
                        k.op("act", lambda e: e.activation(out=st[:, tt, ci * 128:(ci + 1) * 128], in_=ptb[:, 0:128], func=AF.Copy),
                             reads=[pt], writes=[st])
                if not is_x:
                    g = ct - XT
                    dst = C.s_bmT if g < 8 else C.s_cmT
                    g = g % 8
                    k.dma("sp", dst[g * 128:(g + 1) * 128, :], y_[:], reads=[y_], sbuf=y_)
            if st is not None:
                if cb * 4 < XT:
                    k.dma("sp", C.s_xs[:, cb * 512:(cb + 1) * 512].rearrange("(t p) c -> p t c", p=128), st[:], reads=[st], sbuf=st)
                else:
                    o = cb * 512 - DI
                    k.dma("sp", C.s_bm[:, o:o + 512].rearrange("(t p) c -> p t c", p=128), st[:], reads=[st], sbuf=st)
        rows = S.sb("srows", [128, 3, SH], F32)
        k.dma("sp", rows[:], I["ssd_rows"][l].partition_broadcast(128), writes=[rows], sbuf=rows)
        arow = S.sb("sarow", [128, SH], F32)
        k.op("act", lambda e: e.activation(out=arow[:], in_=rows[:, 1, :], func=AF.Exp), reads=[rows], writes=[arow])
        k.op("dve", lambda e: e.tensor_scalar(out=arow[:], in0=arow[:], scalar1=-1.0, scalar2=None, op0=ALU.mult), reads=[arow], writes=[arow])
        acT = S.sb("sacT", [SH, L], F32)
        t1 = [S.sb(f"sdt{i}", [128, SH], F32) for i in range(2)]
        adt = [S.sb(f"sadt{i}", [128, SH], F32) for i in range(2)]
        w = wr.load(_w3(w_in[:, c.off["dt"]: c.off["dt"] + SH]), KT, SH)
        for tt in range(NT):
            a1, a2 = t1[tt % 2], adt[tt % 2]
            ps = C.bank()
            mm_acc(C, ps[:, 0:SH], ps, [(hT[:, kt, tt * 128:(tt + 1) * 128], w[:, kt, 0:SH]) for kt in range(KT)], [w, hT])
            k.op("dve", lambda e: e.tensor_tensor(out=a1[:], in0=ps[:, 0:SH], in1=rows[:, 0, :], op=ALU.add), reads=[ps, rows], writes=[a1])
            k.op("act", lambda e: e.activation(out=a1[:], in_=a1[:], func=AF.Exp), reads=[a1], writes=[a1])
            k.op("act", lambda e: e.activation(out=dt_tok[:, tt, :], in_=a1[:], func=AF.Ln, bias=Kc["one"][:], scale=1.0),
                 reads=[a1, Kc["one"]], writes=[dt_tok])
            k.op("dve", lambda e: e.tensor_tensor(out=a2[:], in0=dt_tok[:, tt, :], in1=arow[:], op=ALU.mult), reads=[dt_tok, arow], writes=[a2])
            ps2 = C.bank()
            k.op("pe", lambda e: e.matmul(ps2[:, 0:SH], lhsT=tri[:], rhs=a2[:], start=True, stop=True), reads=[tri, a2], writes=[ps2])
            k.op("act", lambda e: e.activation(out=acum_tok[:, tt, :], in_=ps2[:, 0:SH], func=AF.Copy), reads=[ps2], writes=[acum_tok])
            ps3 = C.bank()
            k.op("pe", lambda e: e.matmul(ps3[0:SH, 0:128], lhsT=a2[:], rhs=tri[:], start=True, stop=True), reads=[tri, a2], writes=[ps3])
            k.op("act", lambda e: e.activation(out=acT[:, tt * 128:(tt + 1) * 128], in_=ps3[0:SH, 0:128], func=AF.Copy), reads=[ps3], writes=[acT])
        k.dma("sp", C.s_acT[:, :], acT[:], reads=[acT], sbuf=acT)


def precast_plan(C, l):
    c, I = C.c, C.I
    D = c.D
    jobs = [(C.wb["gate"], I["w_in"][l][:, 0:3 * D], 16), (C.wb["branch"], I["w_branch"][l], 16), (C.wb["out"], I["w_out"][l], 4),
            (C.wb["fg"], I["w_ffn_gate"][l], 16), (C.wb["fu"], I["w_ffn_up"][l], 16), (C.wb["fd"], I["w_ffn_down"][l], 16)]
    for dst, src, n in jobs:
        rows = src.shape[0]
        step = (rows + n - 1) // n
        for r0 in range(0, rows, step):
            r1 = min(rows, r0 + step)
            C.drip.append((dst[r0:r1, :], src[r0:r1, :]))


def drip(C, n):
    for _ in range(n):
        if not C.drip:
            return
        dst, src = C.drip.pop(0)
        C.k.dma("pool", dst, src, sbuf=C.pc)


def phase_ssd_loop(C, l, dt_tok, acum_tok):
    k, c, I, Kc = C.k, C.c, C.I, C.K
    D, L, KT, NT, NB = c.D, c.L, c.KT, c.NT, c.NB
    DI, SH, HPG, G = c.DI, c.SH, c.HPG, c.SG
    GW = HPG * 64
    tri, trib, ident = Kc["c_tri"], Kc["c_trib"], Kc["c_ident"]
    with k.scope() as S:
        k.dma_fence("sp")
        drip(C, len(C.drip))
        ngr = S.sb("ngrow", [128, DI], F32)
        k.dma("sp", ngr[:], I["ssd_out_norm"][l].partition_broadcast(128), writes=[ngr], sbuf=ngr)
        rows = S.sb("lrows", [128, 3, SH], F32)
        k.dma("sp", rows[:], I["ssd_rows"][l].partition_broadcast(128), writes=[rows], sbuf=rows)
        st32 = S.sb("st32", [128, DI], F32)
        stb = S.sb("stb", [128, DI], BF16)
        k.op("dve", lambda e: e.memset(st32[:], 0.0), writes=[st32])
        k.op("dve", lambda e: e.memset(stb[:], 0.0), writes=[stb])
        xs_ = [S.sb(f"lxs{i}", [128, DI], BF16) for i in range(2)]
        zs_ = [S.sb(f"lzs{i}", [128, DI], BF16) for i in range(2)]
        bm_ = [S.sb(f"lbm{i}", [128, G * 128], BF16) for i in range(2)]
        bT_ = [S.sb(f"lbT{i}", [128, G, 128], BF16) for i in range(2)]
        cT_ = [S.sb(f"lcT{i}", [128, G, 128], BF16) for i in range(2)]
        Arow = S.sb("Arow", [128, SH, 128], F32)
        segb = S.sb("segb", [128, SH, 128], BF16)
        xdt = S.sb("xdt", [128, DI], BF16)
        xdd = S.sb("xdd", [128, DI], BF16)
        oT_ = [S.sb(f"loT{i}", [128, DI // 128, 128], BF16) for i in range(2)]
        cbm = S.sb("cbm", [128, G, 128], BF16)
        dec = S.sb("dec", [128, SH], F32)
        eAl = S.sb("eAl", [128, SH], F32)
        eA = S.sb("eA", [128, SH], F32)
        tt_ = [S.sb(f"lt{i}", [128, GW], F32) for i in range(2)]
        uu_ = [S.sb(f"lu{i}", [128, GW], F32) for i in range(2)]
        ob_ = [S.sb(f"lob{i}", [128, GW], BF16) for i in range(2)]
        junk = S.sb("ljunk", [128, GW], BF16)
        ssq = [S.sb(f"lssq{i}", [128, 1], F32) for i in range(2)]
        for ch in range(NT):
            sl = slice(ch * 128, (ch + 1) * 128)
            xs, zs, bm, bT, cT, oT = xs_[ch % 2], zs_[ch % 2], bm_[ch % 2], bT_[ch % 2], cT_[ch % 2], oT_[ch % 2]
            k.dma("sp", xs[:], C.s_xs[sl, :], writes=[xs], sbuf=xs)
            k.dma("sp", zs[:], C.s_zs[sl, :], writes=[zs], sbuf=zs)
            k.dma("sp", bm[:], C.s_bm[sl, :], writes=[bm], sbuf=bm)
            k.dma("sp", bT[:], C.s_bmT[:, sl].rearrange("(g p) t -> p g t", p=128), writes=[bT], sbuf=bT)
            k.dma("sp", cT[:], C.s_cmT[:, sl].rearrange("(g p) t -> p g t", p=128), writes=[cT], sbuf=cT)
            k.dma("sp", Arow[:], C.s_acT[:, sl].partition_broadcast(128), writes=[Arow], sbuf=Arow)
            k.op("dve", lambda e: e.tensor_tensor(out=dec[:], in0=Arow[:, :, 127], in1=acum_tok[:, ch, :], op=ALU.subtract),
                 reads=[Arow, acum_tok], writes=[dec])
            k.op("act", lambda e: e.activation(out=dec[:], in_=dec[:], func=AF.Exp), reads=[dec], writes=[dec])
            k.op("act", lambda e: e.activation(out=eAl[:], in_=Arow[:, :, 127], func=AF.Exp), reads=[Arow], writes=[eAl])
            k.op("act", lambda e: e.activation(out=eA[:], in_=acum_tok[:, ch, :], func=AF.Exp), reads=[acum_tok], writes=[eA])
            xs3 = xs[:].rearrange("p (h q) -> p h q", q=64)
            k.op("dve", lambda e: e.tensor_tensor(out=xdt[:].rearrange("p (h q) -> p h q", q=64), in0=xs3, in1=bc_last(dt_tok[:, ch, :], 64), op=ALU.mult),
                 reads=[xs, dt_tok], writes=[xdt])
            k.op("pool", lambda e: e.tensor_tensor(out=xdd[:].rearrange("p (h q) -> p h q", q=64), in0=xdt[:].rearrange("p (h q) -> p h q", q=64),
                                                    in1=bc_last(dec[:], 64), op=ALU.mult), reads=[xdt, dec], writes=[xdd])
            for h in range(SH):
                k.op("dve", lambda e: e.tensor_scalar(out=Arow[:, h, :], in0=Arow[:, h, :], scalar1=acum_tok[:, ch, h:h + 1], scalar2=0.0,
                                                      op0=ALU.subtract, op1=ALU.min), reads=[Arow, acum_tok], writes=[Arow], sig=(h == SH - 1))
            k.op("act", lambda e: e.activation(out=segb[:].rearrange("p h t -> p (h t)"), in_=Arow[:].rearrange("p h t -> p (h t)"), func=AF.Exp),
                 reads=[Arow], writes=[segb])
            for half in range(2):
                pc = C.bank()
                for gi in range(4):
                    g = half * 4 + gi
                    k.op("pe", lambda e: e.matmul(pc[:, gi * 128:(gi + 1) * 128], lhsT=bT[:, g, :], rhs=cT[:, g, :], start=True, stop=True),
                         reads=[bT, cT], writes=[pc], sig=(gi == 3))
                k.op("dve", lambda e: e.tensor_tensor(out=cbm[:, half * 4:(half + 1) * 4, :], in0=pc[:].rearrange("p (g t) -> p g t", g=4),
                                                      in1=bc_mid(tri[:], 4), op=ALU.mult), reads=[pc, tri], writes=[cbm])
            for g in range(G):
                hs = slice(g * HPG, (g + 1) * HPG)
                gs = slice(g * GW, (g + 1) * GW)
                k.op("dve", lambda e: e.tensor_tensor(out=segb[:, hs, :], in0=segb[:, hs, :], in1=bc_mid(cbm[:, g, :], HPG), op=ALU.mult),
                     reads=[segb, cbm], writes=[segb])
                yd = C.bank()
                for hh in range(HPG):
                    h = g * HPG + hh
                    k.op("pe", lambda e: e.matmul(yd[:, hh * 64:(hh + 1) * 64], lhsT=segb[:, h, :], rhs=xdt[:, h * 64:(h + 1) * 64], start=True, stop=True),
                         reads=[segb, xdt], writes=[yd], sig=(hh == HPG - 1))
                yo = C.bank()
                k.op("pe", lambda e: e.matmul(yo[:, 0:GW], lhsT=cT[:, g, :], rhs=stb[:, gs], start=True, stop=True), reads=[cT, stb], writes=[yo])
                t_, u_, o_ = tt_[g % 2], uu_[g % 2], ob_[g % 2]
                k.op("dve", lambda e: e.tensor_tensor(out=t_[:].rearrange("p (h q) -> p h q", q=64), in0=yo[:, 0:GW].rearrange("p (h q) -> p h q", q=64),
                                                      in1=bc_last(eA[:, hs], 64), op=ALU.mult), reads=[yo, eA], writes=[t_])
                k.op("dve", lambda e: e.tensor_tensor(out=t_[:], in0=t_[:], in1=yd[:, 0:GW], op=ALU.add), reads=[t_, yd], writes=[t_])
                k.op("pool", lambda e: e.tensor_tensor(out=u_[:].rearrange("p (h q) -> p h q", q=64), in0=xs[:, gs].rearrange("p (h q) -> p h q", q=64),
                                                        in1=bc_last(rows[:, 2, hs], 64), op=ALU.mult), reads=[xs, rows], writes=[u_])
                k.op("pool", lambda e: e.tensor_tensor(out=u_[:], in0=u_[:], in1=t_[:], op=ALU.add), reads=[u_, t_], writes=[u_])
                k.op("pool", lambda e: e.tensor_tensor(out=u_[:], in0=u_[:], in1=zs[:, gs], op=ALU.mult), reads=[u_, zs], writes=[u_])
                sq_ = ssq[g % 2]
                k.op("act", lambda e: e.activation(out=junk[:], in_=u_[:], func=AF.Square, accum_out=sq_[:]), reads=[u_], writes=[junk, sq_])
                k.op("act", lambda e: e.activation(out=sq_[:], in_=sq_[:], func=AF.Sqrt, bias=Kc["eps"][:], scale=1.0 / GW),
                     reads=[sq_, Kc["eps"]], writes=[sq_])
                k.op("dve", lambda e: e.reciprocal(out=sq_[:], in_=sq_[:]), reads=[sq_], writes=[sq_])
                k.op("dve", lambda e: e.scalar_tensor_tensor(out=o_[:], in0=u_[:], scalar=sq_[:], in1=ngr[:, gs], op0=ALU.mult, op1=ALU.mult),
                     reads=[u_, sq_, ngr], writes=[o_])
                for j in range(GW // 128):
                    pt = C.bank()
                    ptb = pt[:].bitcast(BF16)
                    k.op("pe", lambda e: e.transpose(out=ptb[:, 0:128], in_=o_[:, j * 128:(j + 1) * 128], identity=ident[:]), reads=[o_, ident], writes=[pt])
                    k.op("act", lambda e: e.activation(out=oT[:, g * (GW // 128) + j, :], in_=ptb[:, 0:128], func=AF.Copy), reads=[pt], writes=[oT])
                pd = C.bank()
                k.op("pe", lambda e: e.matmul(pd[:, 0:GW], lhsT=bm[:, g * 128:(g + 1) * 128], rhs=xdd[:, gs], start=True, stop=True),
                     reads=[bm, xdd], writes=[pd])
                k.op("dve", lambda e: e.tensor_tensor(out=st32[:, gs].rearrange("p (h q) -> p h q", q=64), in0=st32[:, gs].rearrange("p (h q) -> p h q", q=64),
                                                      in1=bc_last(eAl[:, hs], 64), op=ALU.mult), reads=[st32, eAl], writes=[st32])
                k.op("dve", lambda e: e.tensor_tensor(out=st32[:, gs], in0=st32[:, gs], in1=pd[:, 0:GW], op=ALU.add), reads=[st32, pd], writes=[st32])
                k.op("act", lambda e: e.activation(out=stb[:, gs], in_=st32[:, gs], func=AF.Copy), reads=[st32], writes=[stb])
            base = c.DV + c.NW
            k.dma("sp", C.mixT[base:base + DI, sl].rearrange("(k p) t -> p k t", p=128), oT[:], reads=[oT], sbuf=oT)


def norm_sbuf(C, x, gcol, sq, rs, rs2, out_fn, Dn):
    k, c, Kc = C.k, C.c, C.K
    KT = c.KT
    ps = C.bank()
    for kt in range(KT):
        s_ = sq[kt % 2]
        k.op("act", lambda e: e.activation(out=s_[:], in_=x[:, kt, :], func=AF.Square), reads=[x], writes=[s_])
        k.op("pe", lambda e: e.matmul(ps[:], lhsT=Kc["ones_bf"][:], rhs=s_[:], start=(kt == 0), stop=(kt == KT - 1)),
             reads=[s_, Kc["ones_bf"]], writes=[ps])
    k.op("act", lambda e: e.activation(out=rs[:], in_=ps[:], func=AF.Sqrt, bias=Kc["eps"][:], scale=1.0 / Dn),
         reads=[ps, Kc["eps"]], writes=[rs])
    k.op("dve", lambda e: e.reciprocal(out=rs2[:], in_=rs[:]), reads=[rs], writes=[rs2])
    for kt in range(KT):
        dst, dbuf = out_fn(kt)
        k.op("dve", lambda e: e.scalar_tensor_tensor(out=dst, in0=x[:, kt, :], scalar=gcol[:, kt:kt + 1], in1=rs2[:],
                                                     op0=ALU.mult, op1=ALU.mult), reads=[x, gcol, rs2], writes=[dbuf])


def phase_merge_ffn(C, l):
    k, c, I, Kc = C.k, C.c, C.I, C.K
    D, L, KT, NT, NB, FT = c.D, c.L, c.KT, c.NT, c.NB, c.FT
    w_in = I["w_in"][l]
    with k.scope() as S:
        k.dma_fence("sp")
        drip(C, len(C.drip))
        k.dma_fence("sp")
        wr = WRing(C, S, "mw", [128, KT, 512], q="sp")
        wd = WRing(C, S, "mwd", [128, FT, 128], q="sp")
        g1 = S.sb("mg1", [128, KT], F32)
        g2 = S.sb("mg2", [128, KT], F32)
        k.dma("sp", g1[:], I["norm_mix"][l], writes=[g1], sbuf=g1)
        k.dma("sp", g2[:], I["norm_ffn"][l], writes=[g2], sbuf=g2)
        xt = S.sb("mx", [128, KT, 512], F32)
        sq = [S.sb(f"msq{i}", [128, 512], BF16) for i in range(2)]
        rs = S.sb("mrs", [128, 512], F32)
        rs2 = S.sb("mrs2", [128, 512], F32)
        hb = S.sb("mh", [128, KT, 512], BF16)
        for tb in range(NB):
            tsl = slice(tb * 512, (tb + 1) * 512)
            k.dma("sp", xt[:], _w3(C.xres[:, tsl]), writes=[xt], sbuf=xt)
            norm_sbuf(C, xt, g1, sq, rs, rs2, lambda kt: (hb[:, kt, :], hb), D)
            with k.scope() as S1:
                mix = S1.sb("mmix", [128, 2 * KT, 512], BF16)
                m32 = S1.sb("mm32", [128, KT, 512], F32)
                mbf = S1.sb("mmbf", [128, KT, 512], BF16)
                gsb = [S1.sb(f"mgs{i}", [128, 512], F32) for i in range(4)]
                tmp = [S1.sb(f"mtp{i}", [128, 512], F32) for i in range(2)]
                for bi, (r0, nk) in enumerate(((0, KT), (D, KT), (2 * D, 2 * KT))):
                    k.dma("sp", mix[:, 0:nk, :], _w3(C.mixT[r0:r0 + nk * 128, tsl]), writes=[mix], sbuf=mix)
                    for d4 in range(KT // 4):
                        wg = wr.load(_w3(C.wb["gate"][:, bi * D + d4 * 512: bi * D + (d4 + 1) * 512]), KT, 512)
                        for j4 in range(4):
                            cs = slice(j4 * 128, (j4 + 1) * 128)
                            pg = C.bank()
                            mm_acc(C, pg[:], pg, [(wg[:, kt, cs], hb[:, kt, :]) for kt in range(KT)], [wg, hb])
                            gs = gsb[j4]
                            k.op("act", lambda e: e.activation(out=gs[:], in_=pg[:], func=AF.Sigmoid), reads=[pg], writes=[gs])
                        wbs = [wr.load(_w3(C.wb["branch"][r0 + j * D: r0 + (j + 1) * D, d4 * 512:(d4 + 1) * 512]), KT, 512)
                               for j in range(nk // KT)]
                        for j4 in range(4):
                            dmt = d4 * 4 + j4
                            cs = slice(j4 * 128, (j4 + 1) * 128)
                            gs = gsb[j4]
                            pu = C.bank()
                            mm_acc(C, pu[:], pu, [(wbs[kk // KT][:, kk % KT, cs], mix[:, kk, :]) for kk in range(nk)], wbs + [mix])
                            if bi == 0:
                                k.op("dve", lambda e: e.tensor_tensor(out=m32[:, dmt, :], in0=pu[:], in1=gs[:], op=ALU.mult),
                                     reads=[pu, gs], writes=[m32])
                            else:
                                t_ = tmp[dmt % 2]
                                k.op("dve", lambda e: e.tensor_tensor(out=t_[:], in0=pu[:], in1=gs[:], op=ALU.mult),
                                     reads=[pu, gs], writes=[t_])
                                k.op("pool", lambda e: e.tensor_tensor(out=m32[:, dmt, :], in0=m32[:, dmt, :], in1=t_[:], op=ALU.add),
                                     reads=[m32, t_], writes=[m32])
                for kt in range(KT):
                    k.op("act", lambda e: e.activation(out=mbf[:, kt, :], in_=m32[:, kt, :], func=AF.Copy), reads=[m32], writes=[mbf])
                for d4 in range(KT // 4):
                    wo = wr.load(_w3(C.wb["out"][:, d4 * 512:(d4 + 1) * 512]), KT, 512)
                    for j4 in range(4):
                        dmt = d4 * 4 + j4
                        px = C.bank()
                        mm_acc(C, px[:], px, [(wo[:, kt, j4 * 128:(j4 + 1) * 128], mbf[:, kt, :]) for kt in range(KT)], [wo, mbf])
                        k.op("dve", lambda e: e.tensor_tensor(out=xt[:, dmt, :], in0=xt[:, dmt, :], in1=px[:], op=ALU.add),
                             reads=[xt, px], writes=[xt])
                if C.debug:
                    k.dma("sp", _w3(C.d_x1[:, tsl]), xt[:], reads=[xt], sbuf=xt)
                    k.dma("sp", _w3(C.d_m[:, tsl]), mbf[:], reads=[mbf], sbuf=mbf)
                    k.dma("sp", _w3(C.d_h[:, tsl]), hb[:], reads=[hb], sbuf=hb)
            norm_sbuf(C, xt, g2, sq, rs, rs2, lambda kt: (hb[:, kt, :], hb), D)
            with k.scope() as S2:
                act = S2.sb("mact", [128, FT, 512], BF16)
                sg = [S2.sb(f"msg{i}", [128, 512], F32) for i in range(2)]
                for f4 in range((FT + 3) // 4):
                    nc_ = min(512, c.FF - f4 * 512)
                    wg = wr.load(_w3(C.wb["fg"][:, f4 * 512:f4 * 512 + nc_]), KT, nc_)
                    wu = wr.load(_w3(C.wb["fu"][:, f4 * 512:f4 * 512 + nc_]), KT, nc_)
                    for j4 in range(nc_ // 128):
                        ft = f4 * 4 + j4
                        cs = slice(j4 * 128, (j4 + 1) * 128)
                        pg = C.bank()
                        mm_acc(C, pg[:], pg, [(wg[:, kt, cs], hb[:, kt, :]) for kt in range(KT)], [wg, hb])
                        s_ = sg[ft % 2]
                        k.op("act", lambda e: e.activation(out=s_[:], in_=pg[:], func=AF.Silu), reads=[pg], writes=[s_])
                        pu = C.bank()
                        mm_acc(C, pu[:], pu, [(wu[:, kt, cs], hb[:, kt, :]) for kt in range(KT)], [wu, hb])
                        k.op("dve", lambda e: e.tensor_tensor(out=act[:, ft, :], in0=pu[:], in1=s_[:], op=ALU.mult),
                             reads=[pu, s_], writes=[act])
                for dmt in range(KT):
                    w = wd.load(_w3(C.wb["fd"][:, dmt * 128:(dmt + 1) * 128]), FT, 128)
                    py = C.bank()
                    mm_acc(C, py[:], py, [(w[:, ft, :], act[:, ft, :]) for ft in range(FT)], [w, act])
                    k.op("dve", lambda e: e.tensor_tensor(out=xt[:, dmt, :], in0=xt[:, dmt, :], in1=py[:], op=ALU.add),
                         reads=[xt, py], writes=[xt])
            k.dma("sp", _w3(C.xres[:, tsl]), xt[:], reads=[xt], sbuf=xt)


def phase_final(C):
    k, c, I, Kc = C.k, C.c, C.I, C.K
    KT, NB = c.KT, c.NB
    with k.scope() as S:
        k.dma_fence("sp")
        g = S.sb("fg", [128, KT], F32)
        k.dma("sp", g[:], I["norm_final"], writes=[g], sbuf=g)
        xt = [S.sb(f"fx{i}", [128, KT, 512], F32) for i in range(2)]
        ot = [S.sb(f"fo{i}", [128, KT, 512], F32) for i in range(2)]
        sq = [S.sb(f"fsq{i}", [128, 512], BF16) for i in range(2)]
        rs = S.sb("frs", [128, 512], F32)
        rs2 = S.sb("frs2", [128, 512], F32)
        for tb in range(NB):
            tsl = slice(tb * 512, (tb + 1) * 512)
            x, o = xt[tb % 2], ot[tb % 2]
            k.dma("sp", x[:], _w3(C.xres[:, tsl]), writes=[x], sbuf=x)
            norm_sbuf(C, x, g, sq, rs, rs2, lambda kt: (o[:, kt, :], o), c.D)
            k.dma("sp", _w3(C.out[:, tsl]), o[:], reads=[o], sbuf=o)


N_CORES = 4


def kernel(**inputs):
    c = Cfg()
    nc, C = build(c)
    shared = shared_inputs(c, inputs)
    x = np.asarray(inputs["x"], np.float32)
    in_maps = []
    for b in range(N_CORES):
        m = dict(shared)
        m["xT"] = np.ascontiguousarray(x[b].T)
        in_maps.append(m)
    res = run_bass_kernel_spmd(nc, in_maps, core_ids=list(range(N_CORES)))
    out = np.stack([np.asarray(res.results[b]["outT"]).T for b in range(N_CORES)])
    return np.ascontiguousarray(out.astype(np.float32))
```

```python
import numpy as np
import concourse.bass as bass
import concourse.mybir as mybir
from concourse.bass_utils import run_bass_kernel_spmd

F32 = mybir.dt.float32
BF16 = mybir.dt.bfloat16
AF = mybir.ActivationFunctionType
ALU = mybir.AluOpType
AX = mybir.AxisListType


class Buf:
    def __init__(self, k, t, name):
        self.k = k
        self.t = t
        self.name = name
        self.w = None
        self.r = {}
        self.dsem = None
        self.psum = False

    def __getitem__(self, idx):
        return self.t[idx]


class SemSlot:
    def __init__(self, h):
        self.h = h
        self.cnt = 0


class Scope:
    def __init__(self, k):
        import contextlib
        self.k = k
        self.st = contextlib.ExitStack()
        self.mine = []

    def __enter__(self):
        return self

    def sb(self, name, shape, dt):
        self.k.uid += 1
        name = f"{name}_{self.k.uid}"
        t = self.st.enter_context(self.k.nc.sbuf_tensor(name, list(shape), dt))
        b = Buf(self.k, t, name)
        self.mine.append(b)
        return b

    def __exit__(self, *a):
        self.k.barrier()
        for b in self.mine:
            if b.dsem is not None:
                self.k.free_slots.append(b.dsem)
        self.st.close()
        return False


class K:
    ENGS = ("pe", "act", "dve", "pool", "sp")

    def __init__(self, nc):
        self.nc = nc
        self.eng = {"pe": nc.tensor, "act": nc.scalar, "dve": nc.vector,
                    "pool": nc.gpsimd, "sp": nc.sync}
        self.sem = {}
        self.cnt = {}
        self.known = {}
        self.epoch = 0
        self.bufs = []
        self.slots = []
        self.free_slots = []
        self.nsem = 0
        self.uid = 0
        self._new_sems()
        self.n_ins = 0

    def _new_sems(self):
        for e in self.ENGS:
            self.sem[e] = self.nc.alloc_semaphore(name=f"s_{e}_{self.epoch}")
            self.cnt[e] = 0
            self.nsem += 1
        self.known = {e: {} for e in self.ENGS}

    def sb(self, name, shape, dt):
        t = self.nc.alloc_sbuf_tensor(name, list(shape), dt)
        b = Buf(self, t, name)
        self.bufs.append(b)
        return b

    def ps(self, name, shape, dt=F32):
        t = self.nc.alloc_psum_tensor(name, list(shape), dt)
        b = Buf(self, t, name)
        b.psum = True
        self.bufs.append(b)
        return b

    def _need(self, e, deps, b, is_write):
        if b.w is not None:
            kk, v = b.w
            if kk == "dma":
                deps[("d", b)] = max(deps.get(("d", b), 0), v)
            else:
                deps[kk] = max(deps.get(kk, 0), v)
        if is_write or b.psum:
            for kk, v in b.r.items():
                if not is_write and kk == e:
                    continue
                if kk == "dma":
                    deps[("d", b)] = max(deps.get(("d", b), 0), v)
                else:
                    deps[kk] = max(deps.get(kk, 0), v)

    def _emit_waits(self, e, deps):
        eng = self.eng[e]
        kn = self.known[e]
        for kk, v in deps.items():
            if isinstance(kk, tuple):
                b = kk[1]
                key = ("d", id(b.dsem))
                if kn.get(key, 0) >= v:
                    continue
                eng.wait_ge(b.dsem.h, 16 * v)
                kn[key] = v
            else:
                if kk == e and (e == "pe" or v > self.cnt[e]):
                    continue
                if kn.get(kk, 0) >= v:
                    continue
                eng.wait_ge(self.sem[kk], v)
                kn[kk] = v

    def op(self, e, fn, reads=(), writes=(), sig=True):
        deps = {}
        for b in reads:
            self._need(e, deps, b, False)
        for b in writes:
            self._need(e, deps, b, True)
        self._emit_waits(e, deps)
        ins = fn(self.eng[e])
        self.n_ins += 1
        if sig:
            self.cnt[e] += 1
            ins.then_inc(self.sem[e], 1)
            v = self.cnt[e]
        else:
            v = self.cnt[e] + 1
        for b in reads:
            b.r[e] = max(b.r.get(e, 0), v)
        for b in writes:
            b.w = (e, v)
            b.r = {}
        return ins

    def dma(self, q, out, in_, reads=(), writes=(), sbuf=None, **kw):
        deps = {}
        for b in reads:
            self._need(q, deps, b, False)
        for b in writes:
            self._need(q, deps, b, True)
        self._emit_waits(q, deps)
        b = sbuf
        if b.dsem is None:
            b.dsem = self._slot("sw" if q == "pool" else "hw")
        assert b.dsem.kind == ("sw" if q == "pool" else "hw"), b.name
        ins = self.eng[q].dma_start(out=out, in_=in_, **kw)
        ins.then_inc(b.dsem.h, 16)
        self.n_ins += 1
        b.dsem.cnt += 1
        for x in reads:
            if x is not b:
                raise ValueError("dma reads must be the tracked sbuf")
            x.r["dma"] = b.dsem.cnt
        for x in writes:
            if x is not b:
                raise ValueError("dma writes must be the tracked sbuf")
            x.w = ("dma", b.dsem.cnt)
            x.r = {}
        return ins

    def _slot(self, kind):
        for i, sl in enumerate(self.free_slots):
            if sl.kind == kind:
                return self.free_slots.pop(i)
        sl = SemSlot(self.nc.alloc_semaphore(name=f"d_{self.nsem}"))
        sl.kind = kind
        self.nsem += 1
        self.slots.append(sl)
        return sl

    def scope(self):
        return Scope(self)

    def dma_fence(self, q="sp"):
        kn = self.known[q]
        for sl in self.slots:
            if sl.cnt > 0:
                key = ("d", id(sl))
                if kn.get(key, 0) < sl.cnt:
                    self.eng[q].wait_ge(sl.h, 16 * sl.cnt)
                    kn[key] = sl.cnt

    def barrier(self):
        for e in self.ENGS:
            kn = self.known[e]
            for o in self.ENGS:
                if (o == e and e == "pe") or self.cnt[o] == 0:
                    continue
                if kn.get(o, 0) < self.cnt[o]:
                    self.eng[e].wait_ge(self.sem[o], self.cnt[o])
                    kn[o] = self.cnt[o]
        for e in self.ENGS:
            self.dma_fence(e)

    def finish(self):
        self.barrier()


class Cfg:
    def __init__(s, D=2048, L=2048, DEPTH=2, split=1):
        s.D, s.L, s.DEPTH, s.SP = D, L, DEPTH, split
        s.KT, s.NT, s.NB = D // 128, L // 128, L // 512
        s.LOW = 16
        s.HK, s.HV = (D // 2) // 4, D // 4
        s.KC, s.VC = s.HK // 128, s.HV // 128
        s.GH = 4 // split
        s.DK, s.DV = s.GH * s.HK, s.GH * s.HV
        s.HD, s.R = 128, (D // 128) // 4
        s.NG = 4 // split
        s.NH = s.NG * s.R
        s.NW, s.KVW = s.NH * 128, s.NG * 128
        s.CL, s.CS, s.CH, s.SB, s.WIN = 32, 16, 256, 64, 512
        s.NCMP, s.NSEL = (L - 32) // 16 + 1, L // 64
        s.TOPK = min(16, s.NSEL)
        s.P, s.N, s.CONV = 64, 128, 4
        s.HPG = (2 * D // 64) // 8
        s.SG = 8 // split
        s.SH = s.SG * s.HPG
        s.DI = s.SH * 64
        s.CD = s.DI + 2 * s.SG * 128
        s.FFG = ((8 * D + 3 * 256 - 1) // (3 * 256)) * 256
        s.FF = s.FFG // split
        s.FT = s.FF // 128
        s.MIX = s.DV + s.NW + s.DI
        sizes = (3 * D, s.DK, s.DK, s.DV, 16, s.DV, s.NW, 6 * s.KVW, 3 * s.NH, s.DI, s.CD, s.SH)
        names = ("gate", "q", "k", "v", "low", "r", "nq", "nkv", "ng", "z", "xbc", "dt")
        s.off = {}
        o = 0
        for n, z in zip(names, sizes):
            s.off[n] = o
            o += z
        s.IN_COLS = o

    def global_cols(s, rank):
        g = Cfg(s.D, s.L, s.DEPTH, 1)
        SP = s.SP
        hs = [rank * s.GH + i for i in range(s.GH)]
        ngs = [rank * s.NG + i for i in range(s.NG)]
        sgs = [rank * s.SG + i for i in range(s.SG)]
        ar = np.arange
        cols = [ar(0, 3 * s.D)]
        cols += [g.off["q"] + h * s.HK + ar(s.HK) for h in hs]
        cols += [g.off["k"] + h * s.HK + ar(s.HK) for h in hs]
        cols += [g.off["v"] + h * s.HV + ar(s.HV) for h in hs]
        cols += [g.off["low"] + ar(16)]
        cols += [g.off["r"] + h * s.HV + ar(s.HV) for h in hs]
        cols += [g.off["nq"] + gg * s.R * 128 + ar(s.R * 128) for gg in ngs]
        for kind in range(6):
            cols += [g.off["nkv"] + kind * g.KVW + gg * 128 + ar(128) for gg in ngs]
        cols += [g.off["ng"] + gg * s.R * 3 + ar(s.R * 3) for gg in ngs]
        cols += [g.off["z"] + gg * s.HPG * 64 + ar(s.HPG * 64) for gg in sgs]
        xl = s.local_xbc(rank)
        cols += [g.off["xbc"] + xl]
        cols += [g.off["dt"] + gg * s.HPG + ar(s.HPG) for gg in sgs]
        cols = np.concatenate(cols)
        assert cols.size == s.IN_COLS, (cols.size, s.IN_COLS)
        return cols

    def local_xbc(s, rank):
        g = Cfg(s.D, s.L, s.DEPTH, 1)
        sgs = [rank * s.SG + i for i in range(s.SG)]
        ar = np.arange
        return np.concatenate([gg * s.HPG * 64 + ar(s.HPG * 64) for gg in sgs]
                              + [g.DI + gg * 128 + ar(128) for gg in sgs]
                              + [g.DI + 8 * 128 + gg * 128 + ar(128) for gg in sgs])

    def local_mix_rows(s, rank):
        g = Cfg(s.D, s.L, s.DEPTH, 1)
        hs = [rank * s.GH + i for i in range(s.GH)]
        ngs = [rank * s.NG + i for i in range(s.NG)]
        sgs = [rank * s.SG + i for i in range(s.SG)]
        ar = np.arange
        return np.concatenate([h * s.HV + ar(s.HV) for h in hs]
                              + [g.DV + gg * s.R * 128 + ar(s.R * 128) for gg in ngs]
                              + [g.DV + g.NW + gg * s.HPG * 64 + ar(s.HPG * 64) for gg in sgs])


def host_consts(c):
    import ml_dtypes
    bf = ml_dtypes.bfloat16
    L = c.L
    i = np.arange(128)
    tri = (i[:, None] <= i[None, :]).astype(np.float32)
    upper = (i[:, None] > i[None, :]).astype(np.float32)
    half = 16
    inv = (500000.0 ** (-np.arange(half, dtype=np.float32) / half)).astype(np.float32)
    ang = np.arange(L, dtype=np.float32)[None, :] * inv[:, None]
    cos, sin = np.cos(ang).astype(np.float32), np.sin(ang).astype(np.float32)
    ropeC = np.concatenate([cos, cos, np.ones((96, L), np.float32)], 0)
    ropeS = np.concatenate([sin, sin, np.zeros((96, L), np.float32)], 0)
    Rm = np.zeros((128, 128), np.float32)
    for d in range(16):
        Rm[d + 16, d] = -1.0
        Rm[d, d + 16] = 1.0
    n = np.arange(128)
    cmpmask = ((16 * n[:, None] + 31) <= np.arange(L)[None, :]).astype(np.float32)
    cmpmask[c.NCMP:] = 0
    cs = np.arange(c.NCMP) * 16
    ss = np.arange(c.NSEL) * 64
    ovl = np.clip(np.minimum(cs[:, None] + 32, ss[None, :] + 64) - np.maximum(cs[:, None], ss[None, :]), 0, None
                  ).astype(np.float32) / 32
    ovl_p = np.zeros((128, c.NSEL), np.float32)
    ovl_p[:c.NCMP] = ovl
    E = (np.arange(L)[None, :] // 64 == np.arange(c.NSEL)[:, None]).astype(np.float32)
    blk_t = (np.arange(L) // 64)[:, None]
    blk_j = np.arange(c.NSEL)[None, :]
    forced = (blk_j == 0) | (blk_j == blk_t) | (blk_j == blk_t - 1)
    valid = blk_j <= blk_t
    selmul = (valid & ~forced).astype(np.float32)
    selbias = np.where(forced, 1e9, np.where(valid, 0.0, -1.0)).astype(np.float32)
    def tm(a):
        return np.ascontiguousarray(a.reshape(c.NT, 128, -1).transpose(1, 0, 2))
    return {
        "c_ident": np.eye(128, dtype=np.float32).astype(bf),
        "c_tri": tri, "c_trib": tri.astype(bf), "c_upperb": upper.astype(bf),
        "c_ropeC": ropeC, "c_ropeS": ropeS, "c_Rm": Rm.astype(bf),
        "c_cmpmask": cmpmask.astype(bf), "c_ovl": ovl_p, "c_E": E.astype(bf),
        "c_selmul": tm(selmul), "c_selbias": tm(selbias),
    }


class Ctx:
    pass


def _w3(ap2, p=128):
    return ap2.rearrange("(k p) c -> p k c", p=p)


def pair_allreduce(C, buf):
    k = C.k
    if C.c.SP == 1:
        return
    k.dma("sp", _w3(C.cc_src.ap()), buf[:], reads=[buf], sbuf=buf)
    k.dma_fence("pool")
    C.cc_n += 1
    C.nc.gpsimd.collective_compute("AllReduce", ALU.add, replica_groups=C.groups,
                                   ins=[C.cc_src.ap().opt()], outs=[C.cc_dst.ap().opt()]).then_inc(C.cc_sem)
    C.nc.sync.wait_ge(C.cc_sem, C.cc_n)
    k.dma("sp", buf[:], _w3(C.cc_dst.ap()), writes=[buf], sbuf=buf)


def build(c, debug=False, phases=None, depth=None, groups=None):
    nc = bass.Bass("TRN2", target_bir_lowering=False)
    k = K(nc)
    C = Ctx()
    C.nc, C.k, C.c = nc, k, c
    D, L = c.D, c.L
    DEPTH = c.DEPTH if depth is None else depth

    def inp(name, shape, dt=F32):
        return nc.dram_tensor(name, list(shape), dt, kind="ExternalInput").ap()

    def scratch(name, shape, dt):
        kind = "ExternalOutput" if debug else "Internal"
        return nc.dram_tensor(name, list(shape), dt, kind=kind).ap()

    I = {}
    I["xT"] = inp("xT", [D, L])
    I["norm_mix"] = inp("norm_mix", [c.DEPTH, 128, c.KT])
    I["w_in"] = inp("w_in", [c.DEPTH, D, c.IN_COLS])
    I["gla_w2aug"] = inp("gla_w2aug", [c.DEPTH, 17, c.DK])
    I["gla_out_norm"] = inp("gla_out_norm", [c.DEPTH, 128, c.VC])
    I["nsa_peT_k"] = inp("nsa_peT_k", [c.DEPTH, 128, 32])
    I["nsa_peT_v"] = inp("nsa_peT_v", [c.DEPTH, 128, 32])
    I["nsa_k_w1"] = inp("nsa_k_w1", [c.DEPTH, 4096, 256])
    I["nsa_k_w2"] = inp("nsa_k_w2", [c.DEPTH, 256, 128])
    I["nsa_v_w1"] = inp("nsa_v_w1", [c.DEPTH, 4096, 256])
    I["nsa_v_w2"] = inp("nsa_v_w2", [c.DEPTH, 256, 128])
    I["ssd_conv_w"] = inp("ssd_conv_w", [c.DEPTH, 128, c.CD // 128, 4])
    I["ssd_conv_b"] = inp("ssd_conv_b", [c.DEPTH, 128, c.CD // 128])
    I["ssd_rows"] = inp("ssd_rows", [c.DEPTH, 3, c.SH])
    I["ssd_out_norm"] = inp("ssd_out_norm", [c.DEPTH, c.DI])
    I["w_branch"] = inp("w_branch", [c.DEPTH, c.MIX, D])
    I["w_out"] = inp("w_out", [c.DEPTH, D, D])
    I["norm_ffn"] = inp("norm_ffn", [c.DEPTH, 128, c.KT])
    I["w_ffn_gate"] = inp("w_ffn_gate", [c.DEPTH, D, c.FF])
    I["w_ffn_up"] = inp("w_ffn_up", [c.DEPTH, D, c.FF])
    I["w_ffn_down"] = inp("w_ffn_down", [c.DEPTH, c.FF, D])
    I["norm_final"] = inp("norm_final", [128, c.KT])
    hc = host_consts(c)
    for n_, a_ in hc.items():
        I[n_] = inp(n_, a_.shape, BF16 if a_.dtype != np.float32 else F32)
    C.I = I
    C.out = nc.dram_tensor("outT", [D, L], F32, kind="ExternalOutput").ap()
    C.xres = scratch("xres", [D, L], F32)
    C.mixT = scratch("mixT", [c.MIX, L], BF16)
    C.s_xs = scratch("s_xs", [L, c.DI], BF16)
    C.s_zs = scratch("s_zs", [L, c.DI], BF16)
    C.s_bm = scratch("s_bm", [L, c.SG * 128], BF16)
    C.s_bmT = scratch("s_bmT", [c.SG * 128, L], BF16)
    C.s_cmT = scratch("s_cmT", [c.SG * 128, L], BF16)
    C.s_acT = scratch("s_acT", [c.SH, L], F32)
    C.debug = debug
    C.wb = {
        "gate": nc.dram_tensor("wb_gate", [D, 3 * D], BF16, kind="Internal").ap(),
        "branch": nc.dram_tensor("wb_branch", [c.MIX, D], BF16, kind="Internal").ap(),
        "out": nc.dram_tensor("wb_out", [D, D], BF16, kind="Internal").ap(),
        "fg": nc.dram_tensor("wb_fg", [D, c.FF], BF16, kind="Internal").ap(),
        "fu": nc.dram_tensor("wb_fu", [D, c.FF], BF16, kind="Internal").ap(),
        "fd": nc.dram_tensor("wb_fd", [c.FF, D], BF16, kind="Internal").ap(),
    }
    C.pc = Buf(k, None, "precast")
    C.groups = groups
    if c.SP > 1:
        C.cc_src = nc.dram_tensor("cc_src", [D, 512], F32)
        C.cc_dst = nc.dram_tensor("cc_dst", [D, 512], F32)
        C.cc_sem = nc.alloc_semaphore(name="cc_sem")
        C.cc_n = 0
    C.drip = []
    if debug:
        C.d_x1 = scratch("d_x1", [D, L], F32)
        C.d_m = scratch("d_m", [D, L], BF16)
        C.d_h = scratch("d_h", [D, L], BF16)

    C.pb = [k.ps(f"pb{i}", [128, 512], F32) for i in range(8)]
    C._bank = 0

    C.rot = list(range(8))

    def bank():
        b = C.pb[C.rot[C._bank % len(C.rot)]]
        C._bank += 1
        return b
    C.bank = bank
    K_ = {}
    C.hc = hc
    for n_, a_ in hc.items():
        if n_ not in ("c_ident", "c_tri", "c_trib", "c_upperb"):
            continue
        dt = BF16 if a_.dtype != np.float32 else F32
        K_[n_] = k.sb("s" + n_, list(a_.shape), dt)
        k.dma("sp", K_[n_][:], I[n_], writes=[K_[n_]], sbuf=K_[n_])
    K_["ones_bf"] = k.sb("ones_bf", [128, 128], BF16)
    k.op("dve", lambda e: e.memset(K_["ones_bf"][:], 1.0), writes=[K_["ones_bf"]])
    K_["eps"] = k.sb("eps_col", [128, 1], F32)
    k.op("dve", lambda e: e.memset(K_["eps"][:], 1e-6), writes=[K_["eps"]])
    K_["one"] = k.sb("one_col", [128, 1], F32)
    k.op("dve", lambda e: e.memset(K_["one"][:], 1.0), writes=[K_["one"]])
    C.K = K_

    run = phases or ("init", "gla", "nsa", "ssd", "merge", "final")
    if "init" in run:
        phase_init(C)
    for l in range(DEPTH):
        with k.scope() as S0:
            dt_tok = S0.sb("dt_tok", [128, c.NT, c.SH], F32)
            acum_tok = S0.sb("acum_tok", [128, c.NT, c.SH], F32)
            with k.scope() as S:
                hT = S.sb("hT", [128, c.KT, L], BF16)
                if "merge" in run:
                    precast_plan(C, l)
                phase_norm(C, S, C.xres, I["norm_mix"][l], hT)
                if "gla" in run:
                    phase_gla(C, l, hT)
                if "nsa" in run:
                    phase_nsa(C, l, hT)
                if "ssd" in run:
                    phase_ssd_prep(C, l, hT, dt_tok, acum_tok)
            if "ssd" in run:
                phase_ssd_loop(C, l, dt_tok, acum_tok)
        if "merge" in run:
            phase_merge_ffn(C, l)
    if "final" in run:
        phase_final(C)
    k.finish()
    C.n_ins = k.n_ins
    return nc, C


def phase_init(C):
    k, c = C.k, C.c
    with k.scope() as S:
        bufs = [S.sb(f"xi{i}", [128, c.L], F32) for i in range(2)]
        for kt in range(c.KT):
            b = bufs[kt % 2]
            k.dma("sp", b[:], C.I["xT"][kt * 128:(kt + 1) * 128, :], writes=[b], sbuf=b)
            k.dma("sp", C.xres[kt * 128:(kt + 1) * 128, :], b[:], reads=[b], sbuf=b)


def phase_norm(C, S, x_dram, g_ap, hT, tbs=None, hoff=0):
    k, c = C.k, C.c
    KT = c.KT
    with k.scope() as S2:
        gcol = S2.sb("gcol", [128, KT], F32)
        k.dma("sp", gcol[:], g_ap, writes=[gcol], sbuf=gcol)
        xb = [S2.sb(f"nx{i}", [128, KT, 512], F32) for i in range(2)]
        sq = [S2.sb(f"nsq{i}", [128, 512], BF16) for i in range(2)]
        rs = S2.sb("nrs", [128, 512], F32)
        rs2 = S2.sb("nrs2", [128, 512], F32)
        for i, tb in enumerate(tbs if tbs is not None else range(c.NB)):
            x = xb[i % 2]
            k.dma("sp", x[:], _w3(x_dram[:, tb * 512:(tb + 1) * 512]), writes=[x], sbuf=x)
            ps = C.bank()
            for kt in range(KT):
                s_ = sq[kt % 2]
                k.op("act", lambda e: e.activation(out=s_[:], in_=x[:, kt, :], func=AF.Square),
                     reads=[x], writes=[s_])
                k.op("pe", lambda e: e.matmul(ps[:], lhsT=C.K["ones_bf"][:], rhs=s_[:],
                                              start=(kt == 0), stop=(kt == KT - 1)),
                     reads=[s_, C.K["ones_bf"]], writes=[ps])
            k.op("act", lambda e: e.activation(out=rs[:], in_=ps[:], func=AF.Sqrt,
                                               bias=C.K["eps"][:], scale=1.0 / c.D),
                 reads=[ps, C.K["eps"]], writes=[rs])
            k.op("dve", lambda e: e.reciprocal(out=rs2[:], in_=rs[:]), reads=[rs], writes=[rs2])
            o = (hoff + i) * 512
            for kt in range(KT):
                k.op("dve", lambda e: e.scalar_tensor_tensor(
                    out=hT[:, kt, o:o + 512], in0=x[:, kt, :], scalar=gcol[:, kt:kt + 1],
                    in1=rs2[:], op0=ALU.mult, op1=ALU.mult), reads=[x, gcol, rs2], writes=[hT])


class WRing:
    def __init__(self, C, S, name, shape, n=2, q="pool"):
        self.C, self.q = C, q
        self.bufs = [S.sb(f"{name}{i}", shape, BF16) for i in range(n)]
        self.i = 0

    def load(self, src3, kt, ncols):
        b = self.bufs[self.i % len(self.bufs)]
        self.i += 1
        self.C.k.dma(self.q, b[:, 0:kt, 0:ncols], src3, writes=[b], sbuf=b)
        if self.q == "pool":
            drip(self.C, 2)
        return b


def mm_acc(C, ps_ap, ps_buf, pairs, rbufs):
    n = len(pairs)
    for i, (l_, r_) in enumerate(pairs):
        C.k.op("pe", lambda e: e.matmul(ps_ap, lhsT=l_, rhs=r_, start=(i == 0), stop=(i == n - 1)),
               reads=rbufs, writes=[ps_buf], sig=(i == n - 1))


def phase_gla(C, l, hT):
    k, c, I, Kc = C.k, C.c, C.I, C.K
    D, L, KT, NT, NB = c.D, c.L, c.KT, c.NT, c.NB
    HK, HV, KC, VC = c.HK, c.HV, c.KC, c.VC
    w_in = I["w_in"][l]
    with k.scope() as S:
        wr = WRing(C, S, "gw", [128, KT, 512])
        gaug = S.sb("gaug", [32, L], BF16)
        w2aug = S.sb("w2aug", [32, c.DK], BF16)
        k.op("dve", lambda e: e.memset(gaug[:], 1.0), writes=[gaug])
        k.dma("pool", w2aug[0:17, :], I["gla_w2aug"][l], writes=[w2aug], sbuf=w2aug)
        wl = wr.load(_w3(w_in[:, c.off["low"]:c.off["low"] + 16]), KT, 16)
        for tb in range(NB):
            ps = C.bank()
            mm_acc(C, ps[0:16, :], ps, [(wl[:, kt, 0:16], hT[:, kt, tb * 512:(tb + 1) * 512]) for kt in range(KT)],
                   [wl, hT])
            k.op("act", lambda e: e.activation(out=gaug[0:16, tb * 512:(tb + 1) * 512], in_=ps[0:16, :], func=AF.Copy),
                 reads=[ps], writes=[gaug])
        gn = S.sb("gnorm", [128, VC], F32)
        k.dma("sp", gn[:], I["gla_out_norm"][l], writes=[gn], sbuf=gn)
        for hd in range(c.GH):
            phase_gla_head(C, S, l, hT, hd, wr, gaug, w2aug, gn)


def phase_gla_head(C, S0, l, hT, hd, wr, gaug, w2aug, gn):
    k, c, I, Kc = C.k, C.c, C.I, C.K
    D, L, KT, NT, NB = c.D, c.L, c.KT, c.NT, c.NB
    HK, HV, KC, VC = c.HK, c.HV, c.KC, c.VC
    w_in = I["w_in"][l]
    tri = Kc["c_tri"]
    with k.scope() as S:
        qT = S.sb("qT", [128, KC, L], BF16)
        kT = S.sb("kT", [128, KC, L], BF16)
        ktok = S.sb("ktok", [128, NT, HK], BF16)
        vtok = S.sb("vtok", [128, NT, HV], BF16)
        srT = S.sb("srT", [128, VC, L], BF16)
        ebl = S.sb("ebl", [128, KC, NT], F32)
        ogT = S.sb("ogT", [128, VC, L], BF16)
        def fm(col0, ncol, dst, func):
            w = wr.load(_w3(w_in[:, col0:col0 + ncol]), KT, ncol)
            for ct in range(ncol // 128):
                for tb in range(NB):
                    ps = C.bank()
                    mm_acc(C, ps[:], ps, [(w[:, kt, ct * 128:(ct + 1) * 128], hT[:, kt, tb * 512:(tb + 1) * 512])
                                          for kt in range(KT)], [w, hT])
                    k.op("act", lambda e: e.activation(out=dst[:, ct, tb * 512:(tb + 1) * 512], in_=ps[:], func=func),
                         reads=[ps], writes=[dst])
            return w

        def tm(w, ncol, dst):
            for tt in range(NT):
                ps = C.bank()
                mm_acc(C, ps[:, 0:ncol], ps, [(hT[:, kt, tt * 128:(tt + 1) * 128], w[:, kt, 0:ncol])
                                              for kt in range(KT)], [w, hT])
                k.op("act", lambda e: e.activation(out=dst[:, tt, :], in_=ps[:, 0:ncol], func=AF.Copy),
                     reads=[ps], writes=[dst])

        fm(c.off["q"] + hd * HK, HK, qT, AF.Copy)
        wk = fm(c.off["k"] + hd * HK, HK, kT, AF.Copy)
        tm(wk, HK, ktok)
        wv = wr.load(_w3(w_in[:, c.off["v"] + hd * HV: c.off["v"] + (hd + 1) * HV]), KT, HV)
        tm(wv, HV, vtok)
        fm(c.off["r"] + hd * HV, HV, srT, AF.Silu)
        spb = [S.sb(f"sp{i}", [128, HK], F32) for i in range(2)]
        t1 = [S.sb(f"gt1_{i}", [128, HK], F32) for i in range(2)]
        t2 = [S.sb(f"gt2_{i}", [128, 128], F32) for i in range(2)]
        t3 = [S.sb(f"gt3_{i}", [128, 128], F32) for i in range(2)]
        for tt in range(NT):
            sl = slice(tt * 128, (tt + 1) * 128)
            sp, a1 = spb[tt % 2], t1[tt % 2]
            ps = C.bank()
            k.op("pe", lambda e: e.matmul(ps[:, 0:HK], lhsT=gaug[0:17, sl], rhs=w2aug[0:17, hd * HK:(hd + 1) * HK],
                                          start=True, stop=True), reads=[gaug, w2aug], writes=[ps])
            k.op("act", lambda e: e.activation(out=a1[:], in_=ps[:, 0:HK], func=AF.Exp, scale=-1.0),
                 reads=[ps], writes=[a1])
            k.op("act", lambda e: e.activation(out=sp[:], in_=a1[:], func=AF.Ln, bias=Kc["one"][:], scale=1.0),
                 reads=[a1, Kc["one"]], writes=[sp])
            ps2 = C.bank()
            k.op("pe", lambda e: e.matmul(ps2[:, 0:HK], lhsT=tri[:], rhs=sp[:], start=True, stop=True),
                 reads=[tri, sp], writes=[ps2])
            k.op("act", lambda e: e.activation(out=a1[:], in_=ps2[:, 0:HK], func=AF.Exp, scale=1.0 / 16),
                 reads=[ps2], writes=[a1])
            k.op("dve", lambda e: e.tensor_tensor(out=ktok[:, tt, :], in0=ktok[:, tt, :], in1=a1[:], op=ALU.mult),
                 reads=[ktok, a1], writes=[ktok])
            for kc in range(KC):
                ep, en = t2[kc % 2], t3[kc % 2]
                ps3 = C.bank()
                k.op("pe", lambda e: e.matmul(ps3[:, 0:128], lhsT=sp[:, kc * 128:(kc + 1) * 128], rhs=tri[:],
                                              start=True, stop=True), reads=[tri, sp], writes=[ps3])
                k.op("act", lambda e: e.activation(out=ep[:], in_=ps3[:, 0:128], func=AF.Exp, scale=1.0 / 16),
                     reads=[ps3], writes=[ep])
                k.op("act", lambda e: e.activation(out=en[:], in_=ps3[:, 0:128], func=AF.Exp, scale=-1.0 / 16),
                     reads=[ps3], writes=[en])
                k.op("dve", lambda e: e.tensor_tensor(out=kT[:, kc, sl], in0=kT[:, kc, sl], in1=ep[:], op=ALU.mult),
                     reads=[kT, ep], writes=[kT])
                k.op("dve", lambda e: e.scalar_tensor_tensor(out=qT[:, kc, sl], in0=qT[:, kc, sl],
                                                             scalar=float(HK) ** -0.5, in1=en[:],
                                                             op0=ALU.mult, op1=ALU.mult),
                     reads=[qT, en], writes=[qT])
                k.op("act", lambda e: e.activation(out=ebl[:, kc, tt:tt + 1], in_=en[:, 127:128], func=AF.Copy),
                     reads=[en], writes=[ebl])
        S32 = S.sb("S32", [128, KC, HV], F32)
        Sb = S.sb("Sb", [128, KC, HV], BF16)
        k.op("dve", lambda e: e.memset(S32[:], 0.0), writes=[S32])
        k.op("dve", lambda e: e.memset(Sb[:], 0.0), writes=[Sb])
        scm = [S.sb(f"scm{i}", [128, 128], BF16) for i in range(2)]
        sq = [S.sb(f"gsq{i}", [128, VC, 128], BF16) for i in range(2)]
        rsd = [S.sb(f"grs{i}", [128, 128], F32) for i in range(2)]
        tmp = [S.sb(f"gtm{i}", [128, 128], F32) for i in range(2)]
        tS = [S.sb(f"gtS{i}", [128, HV], F32) for i in range(2)]
        for ch in range(NT):
            sl = slice(ch * 128, (ch + 1) * 128)
            sc, sqb, rs = scm[ch % 2], sq[ch % 2], rsd[ch % 2]
            ps = C.bank()
            mm_acc(C, ps[:, 0:128], ps, [(kT[:, kc, sl], qT[:, kc, sl]) for kc in range(KC)], [kT, qT])
            k.op("dve", lambda e: e.tensor_tensor(out=sc[:], in0=ps[:, 0:128], in1=tri[:], op=ALU.mult),
                 reads=[ps, tri], writes=[sc])
            po = C.bank()
            for vc in range(VC):
                pairs = [(vtok[:, ch, vc * 128:(vc + 1) * 128], sc[:])]
                pairs += [(Sb[:, kc, vc * 128:(vc + 1) * 128], qT[:, kc, sl]) for kc in range(KC)]
                mm_acc(C, po[:, vc * 128:(vc + 1) * 128], po, pairs, [vtok, sc, Sb, qT])
            k.op("act", lambda e: e.activation(out=sqb[:].rearrange("p a b -> p (a b)"), in_=po[:, 0:VC * 128],
                                               func=AF.Square), reads=[po], writes=[sqb])
            pss = C.bank()
            mm_acc(C, pss[:, 0:128], pss, [(Kc["ones_bf"][:], sqb[:, vc, :]) for vc in range(VC)], [sqb, Kc["ones_bf"]])
            k.op("act", lambda e: e.activation(out=tmp[0][:], in_=pss[:, 0:128], func=AF.Sqrt, bias=Kc["eps"][:],
                                               scale=1.0 / HV), reads=[pss, Kc["eps"]], writes=[tmp[0]])
            k.op("dve", lambda e: e.reciprocal(out=rs[:], in_=tmp[0][:]), reads=[tmp[0]], writes=[rs])
            for vc in range(VC):
                k.op("dve", lambda e: e.scalar_tensor_tensor(out=tmp[1][:], in0=po[:, vc * 128:(vc + 1) * 128],
                                                             scalar=gn[:, vc:vc + 1], in1=rs[:],
                                                             op0=ALU.mult, op1=ALU.mult),
                     reads=[po, gn, rs], writes=[tmp[1]])
                k.op("dve", lambda e: e.tensor_tensor(out=ogT[:, vc, sl], in0=tmp[1][:], in1=srT[:, vc, sl], op=ALU.mult),
                     reads=[tmp[1], srT], writes=[ogT])
            for kc in range(KC):
                pd = C.bank()
                k.op("pe", lambda e: e.matmul(pd[:, 0:HV], lhsT=ktok[:, ch, kc * 128:(kc + 1) * 128], rhs=vtok[:, ch, :],
                                              start=True, stop=True), reads=[ktok, vtok], writes=[pd])
                ts = tS[kc % 2]
                k.op("dve", lambda e: e.tensor_scalar(out=ts[:], in0=pd[:, 0:HV], scalar1=ebl[:, kc, ch:ch + 1],
                                                      scalar2=None, op0=ALU.mult), reads=[pd, ebl], writes=[ts])
                k.op("dve", lambda e: e.scalar_tensor_tensor(out=S32[:, kc, :], in0=S32[:, kc, :],
                                                             scalar=ebl[:, kc, ch:ch + 1], in1=ts[:],
                                                             op0=ALU.mult, op1=ALU.add),
                     reads=[S32, ebl, ts], writes=[S32])
                k.op("act", lambda e: e.activation(out=Sb[:, kc, :], in_=S32[:, kc, :], func=AF.Copy),
                     reads=[S32], writes=[Sb])
        for vc in range(VC):
            r0 = hd * HV + vc * 128
            k.dma("sp", C.mixT[r0:r0 + 128, :], ogT[:, vc, :], reads=[ogT], sbuf=ogT)


def shared_inputs(c, inp, rank=0):
    f = np.float32
    def colT(v, kt):
        v = np.asarray(v, f)
        return np.ascontiguousarray(v.reshape(v.shape[0], kt, 128).transpose(0, 2, 1))
    g = Cfg(c.D, c.L, c.DEPTH, 1)
    hs = [rank * c.GH + i for i in range(c.GH)]
    sgs = [rank * c.SG + i for i in range(c.SG)]
    m = {}
    m["norm_mix"] = colT(inp["norm_mix"], c.KT)
    m["w_in"] = np.ascontiguousarray(np.asarray(inp["w_in"], f)[:, :, c.global_cols(rank)])
    dkc = np.concatenate([h * c.HK + np.arange(c.HK) for h in hs])
    m["gla_w2aug"] = np.ascontiguousarray(np.concatenate(
        [np.asarray(inp["gla_gate_w2"], f), np.asarray(inp["gla_gate_b"], f)[:, None, :]], axis=1)[:, :, dkc])
    m["gla_out_norm"] = colT(inp["gla_out_norm"], c.VC)
    m["nsa_peT_k"] = np.ascontiguousarray(np.asarray(inp["nsa_cmp_pos_k"], f).transpose(0, 2, 1))
    m["nsa_peT_v"] = np.ascontiguousarray(np.asarray(inp["nsa_cmp_pos_v"], f).transpose(0, 2, 1))
    m["nsa_k_w1"] = np.asarray(inp["nsa_cmp_k_w1"], f)
    m["nsa_k_w2"] = np.asarray(inp["nsa_cmp_k_w2"], f)
    m["nsa_v_w1"] = np.asarray(inp["nsa_cmp_v_w1"], f)
    m["nsa_v_w2"] = np.asarray(inp["nsa_cmp_v_w2"], f)
    xl = c.local_xbc(rank)
    cw = np.asarray(inp["ssd_conv_w"], f)[:, :, xl]
    CT = c.CD // 128
    m["ssd_conv_w"] = np.ascontiguousarray(cw.transpose(0, 2, 1).reshape(cw.shape[0], CT, 128, 4).transpose(0, 2, 1, 3))
    m["ssd_conv_b"] = colT(np.asarray(inp["ssd_conv_b"], f)[:, xl], CT)
    hl = np.concatenate([gg * c.HPG + np.arange(c.HPG) for gg in sgs])
    m["ssd_rows"] = np.ascontiguousarray(np.stack(
        [np.asarray(inp["ssd_dt_bias"], f)[:, hl], np.asarray(inp["ssd_a_log"], f)[:, hl], np.asarray(inp["ssd_d"], f)[:, hl]], axis=1))
    dil = np.concatenate([gg * c.HPG * 64 + np.arange(c.HPG * 64) for gg in sgs])
    m["ssd_out_norm"] = np.ascontiguousarray(np.asarray(inp["ssd_out_norm"], f)[:, dil])
    m["w_branch"] = np.ascontiguousarray(np.asarray(inp["w_branch"], f)[:, c.local_mix_rows(rank), :])
    m["w_out"] = np.asarray(inp["w_out"], f)
    m["norm_ffn"] = colT(inp["norm_ffn"], c.KT)
    fsl = slice(rank * c.FF, (rank + 1) * c.FF)
    m["w_ffn_gate"] = np.ascontiguousarray(np.asarray(inp["w_ffn_gate"], f)[:, :, fsl])
    m["w_ffn_up"] = np.ascontiguousarray(np.asarray(inp["w_ffn_up"], f)[:, :, fsl])
    m["w_ffn_down"] = np.ascontiguousarray(np.asarray(inp["w_ffn_down"], f)[:, fsl, :])
    m["norm_final"] = colT(np.asarray(inp["norm_final"], f)[None], c.KT)[0]
    m.update(host_consts(c))
    return m


def bc_mid(ap2, n):
    return ap2.unsqueeze(1).to_broadcast([ap2.shape[0], n, ap2.shape[1]])


def phase_nsa(C, l, hT):
    k, c, I, Kc = C.k, C.c, C.I, C.K
    KT = c.KT
    with k.scope() as S:
        for n_, a_ in C.hc.items():
            if n_ in ("c_ident", "c_tri", "c_trib", "c_upperb"):
                continue
            dt = BF16 if a_.dtype != np.float32 else F32
            Kc[n_] = S.sb("s" + n_, list(a_.shape), dt)
            k.dma("sp", Kc[n_][:], I[n_], writes=[Kc[n_]], sbuf=Kc[n_])
        wr = WRing(C, S, "nw", [128, KT, 128], n=3)
        w1 = {}
        w2 = {}
        pe = {}
        for kind in ("k", "v"):
            w2[kind] = S.sb(f"w2{kind}", [128, 2, 128], BF16)
            k.dma("pool", w2[kind][:], _w3(I[f"nsa_{kind}_w2"][l]), writes=[w2[kind]], sbuf=w2[kind])
            pe[kind] = S.sb(f"pe{kind}", [128, 32], F32)
            k.dma("sp", pe[kind][:], I[f"nsa_peT_{kind}"][l], writes=[pe[kind]], sbuf=pe[kind])
        for g in range(c.NG):
            C.rot = [0, 1, 2, 3]
            nsa_group(C, l, hT, g, wr, w1, w2, pe)
            C.rot = list(range(8))


def nsa_group(C, l, hT, g, wr, w1, w2, pe):
    k, c, I, Kc = C.k, C.c, C.I, C.K
    D, L, KT, NT, NB, R = c.D, c.L, c.KT, c.NT, c.NB, c.R
    NCMP, NSEL = c.NCMP, c.NSEL
    NA = 129 + NSEL
    w_in = I["w_in"][l]
    scale = 128.0 ** -0.5
    trib, upb, ident = Kc["c_trib"], Kc["c_upperb"], Kc["c_ident"]
    with k.scope() as S:
        qT = S.sb("nqT", [128, R, L], BF16)
        kx = {n_: S.sb(f"n{n_}", [128, L], BF16) for n_ in ("ks", "kw")}
        vsl = S.sb("nvsl", [128, NT, 129], BF16)
        vw = S.sb("nvw", [128, NT, 129], BF16)
        gt = S.sb("ngt", [128, NT, 3 * R], F32)
        onT = S.sb("onT", [128, R, L], BF16)
        ra = [S.sb(f"nra{i}", [128, 512], F32) for i in range(2)]
        rb = [S.sb(f"nrb{i}", [128, 512], F32) for i in range(2)]

        def fm_rope(w, wc0, dst_ap_fn, dstbuf, rope):
            for tb in range(NB):
                tsl = slice(tb * 512, (tb + 1) * 512)
                ps = C.bank()
                mm_acc(C, ps[:], ps, [(w[:, kt, wc0:wc0 + 128], hT[:, kt, tsl]) for kt in range(KT)], [w, hT])
                dst = dst_ap_fn(tsl)
                k.op("act", lambda e: e.activation(out=dst, in_=ps[:], func=AF.Copy), reads=[ps], writes=[dstbuf])
                if rope:
                    pr = C.bank()
                    a_, b_ = ra[tb % 2], rb[tb % 2]
                    k.op("pe", lambda e: e.matmul(pr[:, :], lhsT=Kc["c_Rm"][:], rhs=dst, start=True, stop=True),
                         reads=[Kc["c_Rm"], dstbuf], writes=[pr])
                    k.op("dve", lambda e: e.tensor_tensor(out=a_[:], in0=ps[:, :], in1=Kc["c_ropeC"][:, tsl], op=ALU.mult),
                         reads=[ps, Kc["c_ropeC"]], writes=[a_])
                    k.op("dve", lambda e: e.tensor_tensor(out=b_[:], in0=pr[:, :], in1=Kc["c_ropeS"][:, tsl], op=ALU.mult),
                         reads=[pr, Kc["c_ropeS"]], writes=[b_])
                    k.op("dve", lambda e: e.tensor_tensor(out=dst, in0=a_[:], in1=b_[:], op=ALU.add),
                         reads=[a_, b_], writes=[dstbuf])

        q0 = c.off["nq"] + g * R * 128
        for r in range(R):
            w = wr.load(_w3(w_in[:, q0 + r * 128:q0 + (r + 1) * 128]), KT, 128)
            fm_rope(w, 0, lambda tsl, r=r: qT[:, r, tsl], qT, True)
        def kvcol(kind):
            return c.off["nkv"] + kind * c.KVW + g * 128
        for kind, name, rope in ((2, "ks", True), (4, "kw", True)):
            w = wr.load(_w3(w_in[:, kvcol(kind):kvcol(kind) + 128]), KT, 128)
            fm_rope(w, 0, lambda tsl, name=name: kx[name][:, tsl], kx[name], rope)
        for kind, dst in ((3, vsl), (5, vw)):
            w = wr.load(_w3(w_in[:, kvcol(kind):kvcol(kind) + 128]), KT, 128)
            k.op("dve", lambda e: e.memset(dst[:], 1.0), writes=[dst])
            for tt in range(NT):
                ps = C.bank()
                mm_acc(C, ps[:, 0:128], ps, [(hT[:, kt, tt * 128:(tt + 1) * 128], w[:, kt, 0:128]) for kt in range(KT)], [w, hT])
                k.op("act", lambda e: e.activation(out=dst[:, tt, 0:128], in_=ps[:, 0:128], func=AF.Copy), reads=[ps], writes=[dst])
        g0 = c.off["ng"] + g * R * 3
        w = wr.load(_w3(w_in[:, g0:g0 + 3 * R]), KT, 3 * R)
        for tt in range(NT):
            ps = C.bank()
            mm_acc(C, ps[:, 0:3 * R], ps, [(hT[:, kt, tt * 128:(tt + 1) * 128], w[:, kt, 0:3 * R]) for kt in range(KT)], [w, hT])
            k.op("act", lambda e: e.activation(out=gt[:, tt, :], in_=ps[:, 0:3 * R], func=AF.Sigmoid), reads=[ps], writes=[gt])

        kcT = S.sb("kcT", [128, 128], BF16)
        vaug = S.sb("vaug", [128, NA], BF16)
        k.op("dve", lambda e: e.memset(vaug[:], 1.0), writes=[vaug])
        k.op("dve", lambda e: e.tensor_copy(out=vaug[:, 129:NA], in_=Kc["c_ovl"][:]), reads=[Kc["c_ovl"]], writes=[vaug])
        with k.scope() as S3:
            for n_ in ("kc", "vc"):
                kx[n_] = S3.sb(f"n{n_}", [128, L], BF16)
            kpe = S3.sb("kpe", [128, 32, NCMP], BF16)
            hs = S3.sb("nhs", [128, 2, 128], BF16)
            w1b = S3.sb("w1b", [128, 32, 256], BF16)
            for kind, name, rope in ((0, "kc", True), (1, "vc", False)):
                w = wr.load(_w3(w_in[:, kvcol(kind):kvcol(kind) + 128]), KT, 128)
                fm_rope(w, 0, lambda tsl, name=name: kx[name][:, tsl], kx[name], rope)
            for kind, src in (("k", kx["kc"]), ("v", kx["vc"])):
                k.dma("pool", w1b[:], _w3(I[f"nsa_{kind}_w1"][l]), writes=[w1b], sbuf=w1b)
                for l_ in range(32):
                    k.op("dve", lambda e: e.tensor_scalar(out=kpe[:, l_, :], in0=src[:, l_:l_ + 16 * (NCMP - 1) + 1:16],
                                                          scalar1=pe[kind][:, l_:l_ + 1], scalar2=None, op0=ALU.add),
                         reads=[src, pe[kind]], writes=[kpe])
                for cc in range(2):
                    ps = C.bank()
                    mm_acc(C, ps[:, 0:NCMP], ps, [(w1b[:, l_, cc * 128:(cc + 1) * 128], kpe[:, l_, :]) for l_ in range(32)],
                           [w1b, kpe])
                    k.op("act", lambda e: e.activation(out=hs[:, cc, 0:NCMP], in_=ps[:, 0:NCMP], func=AF.Silu), reads=[ps], writes=[hs])
                ps = C.bank()
                if kind == "k":
                    mm_acc(C, ps[:, 0:NCMP], ps, [(w2[kind][:, cc, :], hs[:, cc, 0:NCMP]) for cc in range(2)], [w2[kind], hs])
                    k.op("act", lambda e: e.activation(out=kcT[:, 0:NCMP], in_=ps[:, 0:NCMP], func=AF.Copy), reads=[ps], writes=[kcT])
                else:
                    mm_acc(C, ps[0:NCMP, 0:128], ps, [(hs[:, cc, 0:NCMP], w2[kind][:, cc, :]) for cc in range(2)], [w2[kind], hs])
                    k.op("act", lambda e: e.activation(out=vaug[0:NCMP, 0:128], in_=ps[0:NCMP, 0:128], func=AF.Copy), reads=[ps], writes=[vaug])

        oacc = [S.sb(f"oacc{i}", [128, R, 128], F32) for i in range(2)]
        ob = [S.sb(f"nob{i}", [128, R, 128], BF16) for i in range(2)]
        eb = [S.sb(f"neb{i}", [128, R, 128], BF16) for i in range(3)]
        pslc = S.sb("pslc", [128, NSEL], F32)
        score = S.sb("score", [128, NSEL], F32)
        sc2 = S.sb("sc2", [128, NSEL], F32)
        m8 = S.sb("m8", [128, 8], F32)
        m8b = S.sb("m8b", [128, 8], F32)
        selb = S.sb("selb", [128, NSEL], BF16)
        selT = S.sb("selT", [32, 128], BF16)
        rd = [S.sb(f"nrd{i}", [128, 1], F32) for i in range(4)]
        wv_ = [S.sb(f"nwv{i}", [128, 1], F32) for i in range(4)]
        pacc = [C.pb[4 + r] for r in range(R)]
        ebi = 0

        def finish_branch(T, br, first):
            for r in range(R):
                pa = pacc[r]
                k.op("dve", lambda e: e.tensor_scalar(out=rd[r][:], in0=pa[:, 128:129], scalar1=1e-30, scalar2=None, op0=ALU.max),
                     reads=[pa], writes=[rd[r]])
                k.op("dve", lambda e: e.reciprocal(out=rd[r][:], in_=rd[r][:]), reads=[rd[r]], writes=[rd[r]])
                k.op("dve", lambda e: e.tensor_tensor(out=wv_[r][:], in0=rd[r][:], in1=gt[:, T, 3 * r + br:3 * r + br + 1], op=ALU.mult),
                     reads=[rd[r], gt], writes=[wv_[r]])
                oa = oacc[T % 2]
                if first:
                    k.op("dve", lambda e: e.tensor_scalar(out=oa[:, r, :], in0=pa[:, 0:128], scalar1=wv_[r][:], scalar2=None, op0=ALU.mult),
                         reads=[pa, wv_[r]], writes=[oa])
                else:
                    k.op("dve", lambda e: e.scalar_tensor_tensor(out=oa[:, r, :], in0=pa[:, 0:128], scalar=wv_[r][:], in1=oa[:, r, :],
                                                                 op0=ALU.mult, op1=ALU.add), reads=[pa, wv_[r], oa], writes=[oa])

        for T in range(NT):
            sl = slice(T * 128, (T + 1) * 128)
            qv = qT[:, 0:R, sl]
            ps = C.bank()
            k.op("pe", lambda e: e.matmul(ps[0:NCMP, 0:R * 128].rearrange("p (r t) -> p r t", r=R), lhsT=kcT[:, 0:NCMP], rhs=qv,
                                          start=True, stop=True), reads=[kcT, qT], writes=[ps])
            e_ = eb[ebi % 3]; ebi += 1
            k.op("act", lambda e: e.activation(out=e_[0:NCMP].rearrange("p r t -> p (r t)"), in_=ps[0:NCMP, 0:R * 128], func=AF.Exp, scale=scale),
                 reads=[ps], writes=[e_])
            k.op("dve", lambda e: e.tensor_tensor(out=e_[0:NCMP], in0=e_[0:NCMP], in1=bc_mid(Kc["c_cmpmask"][0:NCMP, sl], R), op=ALU.mult),
                 reads=[e_, Kc["c_cmpmask"]], writes=[e_])
            for r in range(R):
                pa = pacc[r]
                k.op("pe", lambda e: e.matmul(pa[:, 0:NA], lhsT=e_[0:NCMP, r, :], rhs=vaug[0:NCMP, :], start=True, stop=True),
                     reads=[e_, vaug], writes=[pa])
            finish_branch(T, 0, True)
            for r in range(R):
                pa = pacc[r]
                if r == 0:
                    k.op("dve", lambda e: e.tensor_scalar(out=pslc[:], in0=pa[:, 129:NA], scalar1=rd[r][:], scalar2=None, op0=ALU.mult),
                         reads=[pa, rd[r]], writes=[pslc])
                else:
                    k.op("dve", lambda e: e.scalar_tensor_tensor(out=pslc[:], in0=pa[:, 129:NA], scalar=rd[r][:], in1=pslc[:],
                                                                 op0=ALU.mult, op1=ALU.add), reads=[pa, rd[r], pslc], writes=[pslc])
            if c.TOPK < NSEL:
                assert c.TOPK == 16
                k.op("dve", lambda e: e.tensor_tensor(out=score[:], in0=pslc[:], in1=Kc["c_selmul"][:, T, :], op=ALU.mult),
                     reads=[pslc, Kc["c_selmul"]], writes=[score])
                k.op("dve", lambda e: e.tensor_tensor(out=score[:], in0=score[:], in1=Kc["c_selbias"][:, T, :], op=ALU.add),
                     reads=[score, Kc["c_selbias"]], writes=[score])
                k.op("dve", lambda e: e.max(out=m8[:], in_=score[:]), reads=[score], writes=[m8])
                k.op("dve", lambda e: e.match_replace(out=sc2[:], in_to_replace=m8[:], in_values=score[:], imm_value=-2.0),
                     reads=[score, m8], writes=[sc2])
                k.op("dve", lambda e: e.max(out=m8b[:], in_=sc2[:]), reads=[sc2], writes=[m8b])
                k.op("dve", lambda e: e.tensor_scalar(out=selb[:], in0=score[:], scalar1=m8b[:, 7:8], scalar2=None, op0=ALU.is_ge),
                     reads=[score, m8b], writes=[selb])
            else:
                k.op("dve", lambda e: e.memset(selb[:], 1.0), writes=[selb])
            pt = C.bank()
            ptb = pt[:].bitcast(BF16)
            k.op("pe", lambda e: e.transpose(out=ptb[0:NSEL, 0:128], in_=selb[:], identity=ident[:]), reads=[selb, ident], writes=[pt])
            k.op("act", lambda e: e.activation(out=selT[0:NSEL, :], in_=ptb[0:NSEL, 0:128], func=AF.Copy), reads=[pt], writes=[selT])
            for br, kT_, vT_ in ((1, kx["ks"], vsl), (2, kx["kw"], vw)):
                kts = list(range(0, T + 1)) if br == 1 else list(range(max(0, T - c.WIN // 128), T + 1))
                for i, kt in enumerate(kts):
                    ksl = slice(kt * 128, (kt + 1) * 128)
                    ps = C.bank()
                    k.op("pe", lambda e: e.matmul(ps[:, 0:R * 128].rearrange("p (r t) -> p r t", r=R), lhsT=kT_[:, ksl], rhs=qv,
                                                  start=True, stop=True), reads=[kT_, qT], writes=[ps])
                    e_ = eb[ebi % 3]; ebi += 1
                    k.op("act", lambda e: e.activation(out=e_[:].rearrange("p r t -> p (r t)"), in_=ps[:, 0:R * 128], func=AF.Exp, scale=scale),
                         reads=[ps], writes=[e_])
                    if kt == T:
                        k.op("dve", lambda e: e.tensor_tensor(out=e_[:], in0=e_[:], in1=bc_mid(trib[:], R), op=ALU.mult),
                             reads=[e_, trib], writes=[e_])
                    elif br == 1:
                        pm = C.bank()
                        k.op("pe", lambda e: e.matmul(pm[:, 0:128], lhsT=Kc["c_E"][0:NSEL, ksl], rhs=selT[0:NSEL, :], start=True, stop=True),
                             reads=[Kc["c_E"], selT], writes=[pm])
                        k.op("dve", lambda e: e.tensor_tensor(out=e_[:], in0=e_[:], in1=bc_mid(pm[:, 0:128], R), op=ALU.mult),
                             reads=[e_, pm], writes=[e_])
                    elif kt == T - c.WIN // 128:
                        k.op("dve", lambda e: e.tensor_tensor(out=e_[:], in0=e_[:], in1=bc_mid(upb[:], R), op=ALU.mult),
                             reads=[e_, upb], writes=[e_])
                    for r in range(R):
                        pa = pacc[r]
                        k.op("pe", lambda e: e.matmul(pa[:, 0:129], lhsT=e_[:, r, :], rhs=vT_[:, kt, :], start=(i == 0), stop=(i == len(kts) - 1)),
                             reads=[e_, vT_], writes=[pa], sig=(i == len(kts) - 1))
                finish_branch(T, br, False)
            oa, o_ = oacc[T % 2], ob[T % 2]
            k.op("act", lambda e: e.activation(out=o_[:].rearrange("p r t -> p (r t)"), in_=oa[:].rearrange("p r t -> p (r t)"), func=AF.Copy),
                 reads=[oa], writes=[o_])
            for r in range(R):
                pt = C.bank()
                ptb = pt[:].bitcast(BF16)
                k.op("pe", lambda e: e.transpose(out=ptb[:, 0:128], in_=o_[:, r, :], identity=ident[:]), reads=[o_, ident], writes=[pt])
                k.op("act", lambda e: e.activation(out=onT[:, r, sl], in_=ptb[:, 0:128], func=AF.Copy), reads=[pt], writes=[onT])
        for r in range(R):
            r0 = c.DV + (g * R + r) * 128
            k.dma("sp", C.mixT[r0:r0 + 128, :], onT[:, r, :], reads=[onT], sbuf=onT)


def bc_last(ap2, n):
    return ap2.unsqueeze(2).to_broadcast([ap2.shape[0], ap2.shape[1], n])


def phase_ssd_prep(C, l, hT, dt_tok, acum_tok):
    k, c, I, Kc = C.k, C.c, C.I, C.K
    D, L, KT, NT, NB = c.D, c.L, c.KT, c.NT, c.NB
    DI, SH, CD = c.DI, c.SH, c.CD
    w_in = I["w_in"][l]
    tri, ident = Kc["c_tri"], Kc["c_ident"]
    XT = DI // 128
    with k.scope() as S:
        wr = WRing(C, S, "sw", [128, KT, 512])
        stg = [S.sb(f"sstg{i}", [128, NT, 512], BF16) for i in range(2)]
        si = 0
        for cb in range(DI // 512):
            w = wr.load(_w3(w_in[:, c.off["z"] + cb * 512: c.off["z"] + (cb + 1) * 512]), KT, 512)
            st = stg[si % 2]; si += 1
            for tt in range(NT):
                ps = C.bank()
                mm_acc(C, ps[:], ps, [(hT[:, kt, tt * 128:(tt + 1) * 128], w[:, kt, :]) for kt in range(KT)], [w, hT])
                k.op("act", lambda e: e.activation(out=st[:, tt, :], in_=ps[:], func=AF.Silu), reads=[ps], writes=[st])
            k.dma("sp", C.s_zs[:, cb * 512:(cb + 1) * 512].rearrange("(t p) c -> p t c", p=128), st[:], reads=[st], sbuf=st)
        cw = S.sb("scw", [128, CD // 128, 4], F32)
        cbias = S.sb("scb", [128, CD // 128], F32)
        k.dma("sp", cw[:], I["ssd_conv_w"][l], writes=[cw], sbuf=cw)
        k.dma("sp", cbias[:], I["ssd_conv_b"][l], writes=[cbias], sbuf=cbias)
        xc = [S.sb(f"sxc{i}", [128, L + 4], F32) for i in range(2)]
        acc = [S.sb(f"sacc{i}", [128, L], F32) for i in range(2)]
        yT = [S.sb(f"syT{i}", [128, L], BF16) for i in range(2)]
        for b_ in xc:
            k.op("dve", lambda e: e.memset(b_[:, 0:4], 0.0), writes=[b_])
        for cb in range(CD // 512):
            w = wr.load(_w3(w_in[:, c.off["xbc"] + cb * 512: c.off["xbc"] + (cb + 1) * 512]), KT, 512)
            st = None
            for ci in range(4):
                ct = cb * 4 + ci
                x_, a_, y_ = xc[ct % 2], acc[ct % 2], yT[ct % 2]
                for tb in range(NB):
                    ps = C.bank()
                    mm_acc(C, ps[:], ps, [(w[:, kt, ci * 128:(ci + 1) * 128], hT[:, kt, tb * 512:(tb + 1) * 512]) for kt in range(KT)], [w, hT])
                    k.op("act", lambda e: e.activation(out=x_[:, 3 + tb * 512: 3 + (tb + 1) * 512], in_=ps[:], func=AF.Copy),
                         reads=[ps], writes=[x_])
                k.op("dve", lambda e: e.tensor_scalar(out=a_[:], in0=x_[:, 0:L], scalar1=cw[:, ct, 0:1], scalar2=None, op0=ALU.mult),
                     reads=[x_, cw], writes=[a_])
                for j in range(1, 4):
                    k.op("dve", lambda e: e.scalar_tensor_tensor(out=a_[:], in0=x_[:, j:j + L], scalar=cw[:, ct, j:j + 1], in1=a_[:],
                                                                 op0=ALU.mult, op1=ALU.add), reads=[x_, cw, a_], writes=[a_])
                k.op("act", lambda e: e.activation(out=y_[:], in_=a_[:], func=AF.Silu, bias=cbias[:, ct:ct + 1], scale=1.0),
                     reads=[a_, cbias], writes=[y_])
                is_x = ct < XT
                is_b = XT <= ct < XT + c.SG
                if is_x or is_b:
                    if st is None:
                        st = stg[si % 2]; si += 1
                    for tt in range(NT):
                        pt = C.bank()
                        ptb = pt[:].bitcast(BF16)
                        k.op("pe", lambda e: e.transpose(out=ptb[:, 0:128], in_=y_[:, tt * 128:(tt + 1) * 128], identity=ident[:]),
                             reads=[y_, ident], writes=[pt])
                        k.op("act", lambda e: e.activation(out=st[:, tt, ci * 128:(ci + 1) * 128], in_=ptb[:, 0:128], func=AF.Copy),
                             reads=[pt], writes=[st])
                if not is_x:
                    g = ct - XT
                    dst = C.s_bmT if g < c.SG else C.s_cmT
                    g = g % c.SG
                    k.dma("sp", dst[g * 128:(g + 1) * 128, :], y_[:], reads=[y_], sbuf=y_)
            if st is not None:
                if cb * 4 < XT:
                    k.dma("sp", C.s_xs[:, cb * 512:(cb + 1) * 512].rearrange("(t p) c -> p t c", p=128), st[:], reads=[st], sbuf=st)
                else:
                    o = cb * 512 - DI
                    k.dma("sp", C.s_bm[:, o:o + 512].rearrange("(t p) c -> p t c", p=128), st[:], reads=[st], sbuf=st)
        rows = S.sb("srows", [128, 3, SH], F32)
        k.dma("sp", rows[:], I["ssd_rows"][l].partition_broadcast(128), writes=[rows], sbuf=rows)
        arow = S.sb("sarow", [128, SH], F32)
        k.op("act", lambda e: e.activation(out=arow[:], in_=rows[:, 1, :], func=AF.Exp), reads=[rows], writes=[arow])
        k.op("dve", lambda e: e.tensor_scalar(out=arow[:], in0=arow[:], scalar1=-1.0, scalar2=None, op0=ALU.mult), reads=[arow], writes=[arow])
        acT = S.sb("sacT", [SH, L], F32)
        t1 = [S.sb(f"sdt{i}", [128, SH], F32) for i in range(2)]
        adt = [S.sb(f"sadt{i}", [128, SH], F32) for i in range(2)]
        w = wr.load(_w3(w_in[:, c.off["dt"]: c.off["dt"] + SH]), KT, SH)
        for tt in range(NT):
            a1, a2 = t1[tt % 2], adt[tt % 2]
            ps = C.bank()
            mm_acc(C, ps[:, 0:SH], ps, [(hT[:, kt, tt * 128:(tt + 1) * 128], w[:, kt, 0:SH]) for kt in range(KT)], [w, hT])
            k.op("dve", lambda e: e.tensor_tensor(out=a1[:], in0=ps[:, 0:SH], in1=rows[:, 0, :], op=ALU.add), reads=[ps, rows], writes=[a1])
            k.op("act", lambda e: e.activation(out=a1[:], in_=a1[:], func=AF.Exp), reads=[a1], writes=[a1])
            k.op("act", lambda e: e.activation(out=dt_tok[:, tt, :], in_=a1[:], func=AF.Ln, bias=Kc["one"][:], scale=1.0),
                 reads=[a1, Kc["one"]], writes=[dt_tok])
            k.op("dve", lambda e: e.tensor_tensor(out=a2[:], in0=dt_tok[:, tt, :], in1=arow[:], op=ALU.mult), reads=[dt_tok, arow], writes=[a2])
            ps2 = C.bank()
            k.op("pe", lambda e: e.matmul(ps2[:, 0:SH], lhsT=tri[:], rhs=a2[:], start=True, stop=True), reads=[tri, a2], writes=[ps2])
            k.op("act", lambda e: e.activation(out=acum_tok[:, tt, :], in_=ps2[:, 0:SH], func=AF.Copy), reads=[ps2], writes=[acum_tok])
            ps3 = C.bank()
            k.op("pe", lambda e: e.matmul(ps3[0:SH, 0:128], lhsT=a2[:], rhs=tri[:], start=True, stop=True), reads=[tri, a2], writes=[ps3])
            k.op("act", lambda e: e.activation(out=acT[:, tt * 128:(tt + 1) * 128], in_=ps3[0:SH, 0:128], func=AF.Copy), reads=[ps3], writes=[acT])
        k.dma("sp", C.s_acT[:, :], acT[:], reads=[acT], sbuf=acT)


def precast_plan(C, l):
    c, I = C.c, C.I
    D = c.D
    jobs = [(C.wb["gate"], I["w_in"][l][:, 0:3 * D], 16), (C.wb["branch"], I["w_branch"][l], 16), (C.wb["out"], I["w_out"][l], 4),
            (C.wb["fg"], I["w_ffn_gate"][l], 16), (C.wb["fu"], I["w_ffn_up"][l], 16), (C.wb["fd"], I["w_ffn_down"][l], 16)]
    for dst, src, n in jobs:
        rows = src.shape[0]
        step = (rows + n - 1) // n
        for r0 in range(0, rows, step):
            r1 = min(rows, r0 + step)
            C.drip.append((dst[r0:r1, :], src[r0:r1, :]))


def drip(C, n):
    for _ in range(n):
        if not C.drip:
            return
        dst, src = C.drip.pop(0)
        C.k.dma("pool", dst, src, sbuf=C.pc)


def phase_ssd_loop(C, l, dt_tok, acum_tok):
    k, c, I, Kc = C.k, C.c, C.I, C.K
    D, L, KT, NT, NB = c.D, c.L, c.KT, c.NT, c.NB
    DI, SH, HPG, G = c.DI, c.SH, c.HPG, c.SG
    GW = HPG * 64
    tri, trib, ident = Kc["c_tri"], Kc["c_trib"], Kc["c_ident"]
    with k.scope() as S:
        k.dma_fence("sp")
        drip(C, len(C.drip))
        ngr = S.sb("ngrow", [128, DI], F32)
        k.dma("sp", ngr[:], I["ssd_out_norm"][l].partition_broadcast(128), writes=[ngr], sbuf=ngr)
        rows = S.sb("lrows", [128, 3, SH], F32)
        k.dma("sp", rows[:], I["ssd_rows"][l].partition_broadcast(128), writes=[rows], sbuf=rows)
        st32 = S.sb("st32", [128, DI], F32)
        stb = S.sb("stb", [128, DI], BF16)
        k.op("dve", lambda e: e.memset(st32[:], 0.0), writes=[st32])
        k.op("dve", lambda e: e.memset(stb[:], 0.0), writes=[stb])
        xs_ = [S.sb(f"lxs{i}", [128, DI], BF16) for i in range(2)]
        zs_ = [S.sb(f"lzs{i}", [128, DI], BF16) for i in range(2)]
        bm_ = [S.sb(f"lbm{i}", [128, G * 128], BF16) for i in range(2)]
        bT_ = [S.sb(f"lbT{i}", [128, G, 128], BF16) for i in range(2)]
        cT_ = [S.sb(f"lcT{i}", [128, G, 128], BF16) for i in range(2)]
        Arow = S.sb("Arow", [128, SH, 128], F32)
        segb = S.sb("segb", [128, SH, 128], BF16)
        xdt = S.sb("xdt", [128, DI], BF16)
        xdd = S.sb("xdd", [128, DI], BF16)
        oT_ = [S.sb(f"loT{i}", [128, DI // 128, 128], BF16) for i in range(2)]
        cbm = S.sb("cbm", [128, G, 128], BF16)
        dec = S.sb("dec", [128, SH], F32)
        eAl = S.sb("eAl", [128, SH], F32)
        eA = S.sb("eA", [128, SH], F32)
        tt_ = [S.sb(f"lt{i}", [128, GW], F32) for i in range(2)]
        uu_ = [S.sb(f"lu{i}", [128, GW], F32) for i in range(2)]
        ob_ = [S.sb(f"lob{i}", [128, GW], BF16) for i in range(2)]
        junk = S.sb("ljunk", [128, GW], BF16)
        ssq = [S.sb(f"lssq{i}", [128, 1], F32) for i in range(2)]
        for ch in range(NT):
            sl = slice(ch * 128, (ch + 1) * 128)
            xs, zs, bm, bT, cT, oT = xs_[ch % 2], zs_[ch % 2], bm_[ch % 2], bT_[ch % 2], cT_[ch % 2], oT_[ch % 2]
            k.dma("sp", xs[:], C.s_xs[sl, :], writes=[xs], sbuf=xs)
            k.dma("sp", zs[:], C.s_zs[sl, :], writes=[zs], sbuf=zs)
            k.dma("sp", bm[:], C.s_bm[sl, :], writes=[bm], sbuf=bm)
            k.dma("sp", bT[:], C.s_bmT[:, sl].rearrange("(g p) t -> p g t", p=128), writes=[bT], sbuf=bT)
            k.dma("sp", cT[:], C.s_cmT[:, sl].rearrange("(g p) t -> p g t", p=128), writes=[cT], sbuf=cT)
            k.dma("sp", Arow[:], C.s_acT[:, sl].partition_broadcast(128), writes=[Arow], sbuf=Arow)
            k.op("dve", lambda e: e.tensor_tensor(out=dec[:], in0=Arow[:, :, 127], in1=acum_tok[:, ch, :], op=ALU.subtract),
                 reads=[Arow, acum_tok], writes=[dec])
            k.op("act", lambda e: e.activation(out=dec[:], in_=dec[:], func=AF.Exp), reads=[dec], writes=[dec])
            k.op("act", lambda e: e.activation(out=eAl[:], in_=Arow[:, :, 127], func=AF.Exp), reads=[Arow], writes=[eAl])
            k.op("act", lambda e: e.activation(out=eA[:], in_=acum_tok[:, ch, :], func=AF.Exp), reads=[acum_tok], writes=[eA])
            xs3 = xs[:].rearrange("p (h q) -> p h q", q=64)
            k.op("dve", lambda e: e.tensor_tensor(out=xdt[:].rearrange("p (h q) -> p h q", q=64), in0=xs3, in1=bc_last(dt_tok[:, ch, :], 64), op=ALU.mult),
                 reads=[xs, dt_tok], writes=[xdt])
            k.op("pool", lambda e: e.tensor_tensor(out=xdd[:].rearrange("p (h q) -> p h q", q=64), in0=xdt[:].rearrange("p (h q) -> p h q", q=64),
                                                    in1=bc_last(dec[:], 64), op=ALU.mult), reads=[xdt, dec], writes=[xdd])
            for h in range(SH):
                k.op("dve", lambda e: e.tensor_scalar(out=Arow[:, h, :], in0=Arow[:, h, :], scalar1=acum_tok[:, ch, h:h + 1], scalar2=0.0,
                                                      op0=ALU.subtract, op1=ALU.min), reads=[Arow, acum_tok], writes=[Arow], sig=(h == SH - 1))
            k.op("act", lambda e: e.activation(out=segb[:].rearrange("p h t -> p (h t)"), in_=Arow[:].rearrange("p h t -> p (h t)"), func=AF.Exp),
                 reads=[Arow], writes=[segb])
            for half in range(G // 4):
                pc = C.bank()
                for gi in range(4):
                    g = half * 4 + gi
                    k.op("pe", lambda e: e.matmul(pc[:, gi * 128:(gi + 1) * 128], lhsT=bT[:, g, :], rhs=cT[:, g, :], start=True, stop=True),
                         reads=[bT, cT], writes=[pc], sig=(gi == 3))
                k.op("dve", lambda e: e.tensor_tensor(out=cbm[:, half * 4:(half + 1) * 4, :], in0=pc[:].rearrange("p (g t) -> p g t", g=4),
                                                      in1=bc_mid(tri[:], 4), op=ALU.mult), reads=[pc, tri], writes=[cbm])
            for g in range(G):
                hs = slice(g * HPG, (g + 1) * HPG)
                gs = slice(g * GW, (g + 1) * GW)
                k.op("dve", lambda e: e.tensor_tensor(out=segb[:, hs, :], in0=segb[:, hs, :], in1=bc_mid(cbm[:, g, :], HPG), op=ALU.mult),
                     reads=[segb, cbm], writes=[segb])
                yd = C.bank()
                for hh in range(HPG):
                    h = g * HPG + hh
                    k.op("pe", lambda e: e.matmul(yd[:, hh * 64:(hh + 1) * 64], lhsT=segb[:, h, :], rhs=xdt[:, h * 64:(h + 1) * 64], start=True, stop=True),
                         reads=[segb, xdt], writes=[yd], sig=(hh == HPG - 1))
                yo = C.bank()
                k.op("pe", lambda e: e.matmul(yo[:, 0:GW], lhsT=cT[:, g, :], rhs=stb[:, gs], start=True, stop=True), reads=[cT, stb], writes=[yo])
                t_, u_, o_ = tt_[g % 2], uu_[g % 2], ob_[g % 2]
                k.op("dve", lambda e: e.tensor_tensor(out=t_[:].rearrange("p (h q) -> p h q", q=64), in0=yo[:, 0:GW].rearrange("p (h q) -> p h q", q=64),
                                                      in1=bc_last(eA[:, hs], 64), op=ALU.mult), reads=[yo, eA], writes=[t_])
                k.op("dve", lambda e: e.tensor_tensor(out=t_[:], in0=t_[:], in1=yd[:, 0:GW], op=ALU.add), reads=[t_, yd], writes=[t_])
                k.op("pool", lambda e: e.tensor_tensor(out=u_[:].rearrange("p (h q) -> p h q", q=64), in0=xs[:, gs].rearrange("p (h q) -> p h q", q=64),
                                                        in1=bc_last(rows[:, 2, hs], 64), op=ALU.mult), reads=[xs, rows], writes=[u_])
                k.op("pool", lambda e: e.tensor_tensor(out=u_[:], in0=u_[:], in1=t_[:], op=ALU.add), reads=[u_, t_], writes=[u_])
                k.op("pool", lambda e: e.tensor_tensor(out=u_[:], in0=u_[:], in1=zs[:, gs], op=ALU.mult), reads=[u_, zs], writes=[u_])
                sq_ = ssq[g % 2]
                k.op("act", lambda e: e.activation(out=junk[:], in_=u_[:], func=AF.Square, accum_out=sq_[:]), reads=[u_], writes=[junk, sq_])
                k.op("act", lambda e: e.activation(out=sq_[:], in_=sq_[:], func=AF.Sqrt, bias=Kc["eps"][:], scale=1.0 / GW),
                     reads=[sq_, Kc["eps"]], writes=[sq_])
                k.op("dve", lambda e: e.reciprocal(out=sq_[:], in_=sq_[:]), reads=[sq_], writes=[sq_])
                k.op("dve", lambda e: e.scalar_tensor_tensor(out=o_[:], in0=u_[:], scalar=sq_[:], in1=ngr[:, gs], op0=ALU.mult, op1=ALU.mult),
                     reads=[u_, sq_, ngr], writes=[o_])
                for j in range(GW // 128):
                    pt = C.bank()
                    ptb = pt[:].bitcast(BF16)
                    k.op("pe", lambda e: e.transpose(out=ptb[:, 0:128], in_=o_[:, j * 128:(j + 1) * 128], identity=ident[:]), reads=[o_, ident], writes=[pt])
                    k.op("act", lambda e: e.activation(out=oT[:, g * (GW // 128) + j, :], in_=ptb[:, 0:128], func=AF.Copy), reads=[pt], writes=[oT])
                pd = C.bank()
                k.op("pe", lambda e: e.matmul(pd[:, 0:GW], lhsT=bm[:, g * 128:(g + 1) * 128], rhs=xdd[:, gs], start=True, stop=True),
                     reads=[bm, xdd], writes=[pd])
                k.op("dve", lambda e: e.tensor_tensor(out=st32[:, gs].rearrange("p (h q) -> p h q", q=64), in0=st32[:, gs].rearrange("p (h q) -> p h q", q=64),
                                                      in1=bc_last(eAl[:, hs], 64), op=ALU.mult), reads=[st32, eAl], writes=[st32])
                k.op("dve", lambda e: e.tensor_tensor(out=st32[:, gs], in0=st32[:, gs], in1=pd[:, 0:GW], op=ALU.add), reads=[st32, pd], writes=[st32])
                k.op("act", lambda e: e.activation(out=stb[:, gs], in_=st32[:, gs], func=AF.Copy), reads=[st32], writes=[stb])
            base = c.DV + c.NW
            k.dma("sp", C.mixT[base:base + DI, sl].rearrange("(k p) t -> p k t", p=128), oT[:], reads=[oT], sbuf=oT)


def norm_sbuf(C, x, gcol, sq, rs, rs2, out_fn, Dn):
    k, c, Kc = C.k, C.c, C.K
    KT = c.KT
    ps = C.bank()
    for kt in range(KT):
        s_ = sq[kt % 2]
        k.op("act", lambda e: e.activation(out=s_[:], in_=x[:, kt, :], func=AF.Square), reads=[x], writes=[s_])
        k.op("pe", lambda e: e.matmul(ps[:], lhsT=Kc["ones_bf"][:], rhs=s_[:], start=(kt == 0), stop=(kt == KT - 1)),
             reads=[s_, Kc["ones_bf"]], writes=[ps])
    k.op("act", lambda e: e.activation(out=rs[:], in_=ps[:], func=AF.Sqrt, bias=Kc["eps"][:], scale=1.0 / Dn),
         reads=[ps, Kc["eps"]], writes=[rs])
    k.op("dve", lambda e: e.reciprocal(out=rs2[:], in_=rs[:]), reads=[rs], writes=[rs2])
    for kt in range(KT):
        dst, dbuf = out_fn(kt)
        k.op("dve", lambda e: e.scalar_tensor_tensor(out=dst, in0=x[:, kt, :], scalar=gcol[:, kt:kt + 1], in1=rs2[:],
                                                     op0=ALU.mult, op1=ALU.mult), reads=[x, gcol, rs2], writes=[dbuf])


def phase_merge_ffn(C, l):
    k, c, I, Kc = C.k, C.c, C.I, C.K
    D, L, KT, NT, NB, FT = c.D, c.L, c.KT, c.NT, c.NB, c.FT
    w_in = I["w_in"][l]
    with k.scope() as S:
        k.dma_fence("sp")
        drip(C, len(C.drip))
        k.dma_fence("sp")
        wr = WRing(C, S, "mw", [128, KT, 512], q="sp")
        wd = WRing(C, S, "mwd", [128, FT, 128], q="sp")
        g1 = S.sb("mg1", [128, KT], F32)
        g2 = S.sb("mg2", [128, KT], F32)
        k.dma("sp", g1[:], I["norm_mix"][l], writes=[g1], sbuf=g1)
        k.dma("sp", g2[:], I["norm_ffn"][l], writes=[g2], sbuf=g2)
        xt = S.sb("mx", [128, KT, 512], F32)
        sq = [S.sb(f"msq{i}", [128, 512], BF16) for i in range(2)]
        rs = S.sb("mrs", [128, 512], F32)
        rs2 = S.sb("mrs2", [128, 512], F32)
        hb = S.sb("mh", [128, KT, 512], BF16)
        for tb in range(NB):
            tsl = slice(tb * 512, (tb + 1) * 512)
            k.dma("sp", xt[:], _w3(C.xres[:, tsl]), writes=[xt], sbuf=xt)
            norm_sbuf(C, xt, g1, sq, rs, rs2, lambda kt: (hb[:, kt, :], hb), D)
            with k.scope() as S1:
                brs = ((0, c.DV // 128), (c.DV, c.NW // 128), (c.DV + c.NW, c.DI // 128))
                mix = S1.sb("mmix", [128, max(b_[1] for b_ in brs), 512], BF16)
                m32 = S1.sb("mm32", [128, KT, 512], F32)
                mbf = S1.sb("mmbf", [128, KT, 512], BF16)
                gsb = [S1.sb(f"mgs{i}", [128, 512], F32) for i in range(4)]
                tmp = [S1.sb(f"mtp{i}", [128, 512], F32) for i in range(2)]
                for bi, (r0, nk) in enumerate(brs):
                    k.dma("sp", mix[:, 0:nk, :], _w3(C.mixT[r0:r0 + nk * 128, tsl]), writes=[mix], sbuf=mix)
                    for d4 in range(KT // 4):
                        wg = wr.load(_w3(C.wb["gate"][:, bi * D + d4 * 512: bi * D + (d4 + 1) * 512]), KT, 512)
                        for j4 in range(4):
                            cs = slice(j4 * 128, (j4 + 1) * 128)
                            pg = C.bank()
                            mm_acc(C, pg[:], pg, [(wg[:, kt, cs], hb[:, kt, :]) for kt in range(KT)], [wg, hb])
                            gs = gsb[j4]
                            k.op("act", lambda e: e.activation(out=gs[:], in_=pg[:], func=AF.Sigmoid), reads=[pg], writes=[gs])
                        wbs = [wr.load(_w3(C.wb["branch"][r0 + j * 128: r0 + min(nk, j + KT) * 128, d4 * 512:(d4 + 1) * 512]), min(KT, nk - j), 512)
                               for j in range(0, nk, KT)]
                        for j4 in range(4):
                            dmt = d4 * 4 + j4
                            cs = slice(j4 * 128, (j4 + 1) * 128)
                            gs = gsb[j4]
                            pu = C.bank()
                            mm_acc(C, pu[:], pu, [(wbs[kk // KT][:, kk % KT, cs], mix[:, kk, :]) for kk in range(nk)], wbs + [mix])
                            if bi == 0:
                                k.op("dve", lambda e: e.tensor_tensor(out=m32[:, dmt, :], in0=pu[:], in1=gs[:], op=ALU.mult),
                                     reads=[pu, gs], writes=[m32])
                            else:
                                t_ = tmp[dmt % 2]
                                k.op("dve", lambda e: e.tensor_tensor(out=t_[:], in0=pu[:], in1=gs[:], op=ALU.mult),
                                     reads=[pu, gs], writes=[t_])
                                k.op("pool", lambda e: e.tensor_tensor(out=m32[:, dmt, :], in0=m32[:, dmt, :], in1=t_[:], op=ALU.add),
                                     reads=[m32, t_], writes=[m32])
                for kt in range(KT):
                    k.op("act", lambda e: e.activation(out=mbf[:, kt, :], in_=m32[:, kt, :], func=AF.Copy), reads=[m32], writes=[mbf])
                for d4 in range(KT // 4):
                    wo = wr.load(_w3(C.wb["out"][:, d4 * 512:(d4 + 1) * 512]), KT, 512)
                    for j4 in range(4):
                        dmt = d4 * 4 + j4
                        px = C.bank()
                        mm_acc(C, px[:], px, [(wo[:, kt, j4 * 128:(j4 + 1) * 128], mbf[:, kt, :]) for kt in range(KT)], [wo, mbf])
                        if c.SP == 1:
                            k.op("dve", lambda e: e.tensor_tensor(out=xt[:, dmt, :], in0=xt[:, dmt, :], in1=px[:], op=ALU.add),
                                 reads=[xt, px], writes=[xt])
                        else:
                            k.op("act", lambda e: e.activation(out=m32[:, dmt, :], in_=px[:], func=AF.Copy), reads=[px], writes=[m32])
                if c.SP > 1:
                    pair_allreduce(C, m32)
                    for dmt in range(KT):
                        k.op("dve", lambda e: e.tensor_tensor(out=xt[:, dmt, :], in0=xt[:, dmt, :], in1=m32[:, dmt, :], op=ALU.add),
                             reads=[xt, m32], writes=[xt])
                if C.debug:
                    k.dma("sp", _w3(C.d_x1[:, tsl]), xt[:], reads=[xt], sbuf=xt)
                    k.dma("sp", _w3(C.d_m[:, tsl]), mbf[:], reads=[mbf], sbuf=mbf)
                    k.dma("sp", _w3(C.d_h[:, tsl]), hb[:], reads=[hb], sbuf=hb)
            norm_sbuf(C, xt, g2, sq, rs, rs2, lambda kt: (hb[:, kt, :], hb), D)
            with k.scope() as S2:
                act = S2.sb("mact", [128, FT, 512], BF16)
                pp = S2.sb("mpp", [128, KT, 512], F32) if c.SP > 1 else None
                sg = [S2.sb(f"msg{i}", [128, 512], F32) for i in range(2)]
                for f4 in range((FT + 3) // 4):
                    nc_ = min(512, c.FF - f4 * 512)
                    wg = wr.load(_w3(C.wb["fg"][:, f4 * 512:f4 * 512 + nc_]), KT, nc_)
                    wu = wr.load(_w3(C.wb["fu"][:, f4 * 512:f4 * 512 + nc_]), KT, nc_)
                    for j4 in range(nc_ // 128):
                        ft = f4 * 4 + j4
                        cs = slice(j4 * 128, (j4 + 1) * 128)
                        pg = C.bank()
                        mm_acc(C, pg[:], pg, [(wg[:, kt, cs], hb[:, kt, :]) for kt in range(KT)], [wg, hb])
                        s_ = sg[ft % 2]
                        k.op("act", lambda e: e.activation(out=s_[:], in_=pg[:], func=AF.Silu), reads=[pg], writes=[s_])
                        pu = C.bank()
                        mm_acc(C, pu[:], pu, [(wu[:, kt, cs], hb[:, kt, :]) for kt in range(KT)], [wu, hb])
                        k.op("dve", lambda e: e.tensor_tensor(out=act[:, ft, :], in0=pu[:], in1=s_[:], op=ALU.mult),
                             reads=[pu, s_], writes=[act])
                for dmt in range(KT):
                    w = wd.load(_w3(C.wb["fd"][:, dmt * 128:(dmt + 1) * 128]), FT, 128)
                    py = C.bank()
                    mm_acc(C, py[:], py, [(w[:, ft, :], act[:, ft, :]) for ft in range(FT)], [w, act])
                    if c.SP == 1:
                        k.op("dve", lambda e: e.tensor_tensor(out=xt[:, dmt, :], in0=xt[:, dmt, :], in1=py[:], op=ALU.add),
                             reads=[xt, py], writes=[xt])
                    else:
                        k.op("act", lambda e: e.activation(out=pp[:, dmt, :], in_=py[:], func=AF.Copy), reads=[py], writes=[pp])
                if c.SP > 1:
                    pair_allreduce(C, pp)
                    for dmt in range(KT):
                        k.op("dve", lambda e: e.tensor_tensor(out=xt[:, dmt, :], in0=xt[:, dmt, :], in1=pp[:, dmt, :], op=ALU.add),
                             reads=[xt, pp], writes=[xt])
            k.dma("sp", _w3(C.xres[:, tsl]), xt[:], reads=[xt], sbuf=xt)


def phase_final(C):
    k, c, I, Kc = C.k, C.c, C.I, C.K
    KT, NB = c.KT, c.NB
    with k.scope() as S:
        k.dma_fence("sp")
        g = S.sb("fg", [128, KT], F32)
        k.dma("sp", g[:], I["norm_final"], writes=[g], sbuf=g)
        xt = [S.sb(f"fx{i}", [128, KT, 512], F32) for i in range(2)]
        ot = [S.sb(f"fo{i}", [128, KT, 512], F32) for i in range(2)]
        sq = [S.sb(f"fsq{i}", [128, 512], BF16) for i in range(2)]
        rs = S.sb("frs", [128, 512], F32)
        rs2 = S.sb("frs2", [128, 512], F32)
        for tb in range(NB):
            tsl = slice(tb * 512, (tb + 1) * 512)
            x, o = xt[tb % 2], ot[tb % 2]
            k.dma("sp", x[:], _w3(C.xres[:, tsl]), writes=[x], sbuf=x)
            norm_sbuf(C, x, g, sq, rs, rs2, lambda kt: (o[:, kt, :], o), c.D)
            k.dma("sp", _w3(C.out[:, tsl]), o[:], reads=[o], sbuf=o)


N_CORES = 8
SPLIT = 2


def kernel(**inputs):
    c = Cfg(split=SPLIT)
    groups = [[i * SPLIT + j for j in range(SPLIT)] for i in range(N_CORES // SPLIT)]
    nc, C = build(c, groups=groups)
    x = np.asarray(inputs["x"], np.float32)
    per_rank = [shared_inputs(c, inputs, r) for r in range(SPLIT)]
    in_maps = []
    for i in range(N_CORES):
        m = dict(per_rank[i % SPLIT])
        m["xT"] = np.ascontiguousarray(x[i // SPLIT].T)
        in_maps.append(m)
    res = run_bass_kernel_spmd(nc, in_maps, core_ids=list(range(N_CORES)))
    out = np.stack([np.asarray(res.results[b * SPLIT]["outT"]).T for b in range(N_CORES // SPLIT)])
    return np.ascontiguousarray(out.astype(np.float32))
```

```python
import numpy as np
import concourse.bass as bass
import concourse.mybir as mybir
from concourse.bass_utils import run_bass_kernel_spmd

F32 = mybir.dt.float32
BF16 = mybir.dt.bfloat16
AF = mybir.ActivationFunctionType
ALU = mybir.AluOpType
AX = mybir.AxisListType


class Buf:
    def __init__(self, k, t, name):
        self.k = k
        self.t = t
        self.name = name
        self.w = None
        self.r = {}
        self.dsem = None
        self.psum = False

    def __getitem__(self, idx):
        return self.t[idx]


class SemSlot:
    def __init__(self, h):
        self.h = h
        self.cnt = 0


class Scope:
    def __init__(self, k):
        import contextlib
        self.k = k
        self.st = contextlib.ExitStack()
        self.mine = []

    def __enter__(self):
        return self

    def sb(self, name, shape, dt):
        self.k.uid += 1
        name = f"{name}_{self.k.uid}"
        t = self.st.enter_context(self.k.nc.sbuf_tensor(name, list(shape), dt))
        b = Buf(self.k, t, name)
        self.mine.append(b)
        return b

    def __exit__(self, *a):
        self.k.barrier()
        for b in self.mine:
            if b.dsem is not None:
                self.k.free_slots.append(b.dsem)
        self.st.close()
        return False


class K:
    ENGS = ("pe", "act", "dve", "pool", "sp")

    def __init__(self, nc):
        self.nc = nc
        self.eng = {"pe": nc.tensor, "act": nc.scalar, "dve": nc.vector,
                    "pool": nc.gpsimd, "sp": nc.sync}
        self.sem = {}
        self.cnt = {}
        self.known = {}
        self.epoch = 0
        self.bufs = []
        self.slots = []
        self.free_slots = []
        self.nsem = 0
        self.uid = 0
        self._new_sems()
        self.n_ins = 0

    def _new_sems(self):
        for e in self.ENGS:
            self.sem[e] = self.nc.alloc_semaphore(name=f"s_{e}_{self.epoch}")
            self.cnt[e] = 0
            self.nsem += 1
        self.known = {e: {} for e in self.ENGS}

    def sb(self, name, shape, dt):
        t = self.nc.alloc_sbuf_tensor(name, list(shape), dt)
        b = Buf(self, t, name)
        self.bufs.append(b)
        return b

    def ps(self, name, shape, dt=F32):
        t = self.nc.alloc_psum_tensor(name, list(shape), dt)
        b = Buf(self, t, name)
        b.psum = True
        self.bufs.append(b)
        return b

    def _need(self, e, deps, b, is_write):
        if b.w is not None:
            kk, v = b.w
            if kk == "dma":
                deps[("d", b)] = max(deps.get(("d", b), 0), v)
            else:
                deps[kk] = max(deps.get(kk, 0), v)
        if is_write or b.psum:
            for kk, v in b.r.items():
                if not is_write and kk == e:
                    continue
                if kk == "dma":
                    deps[("d", b)] = max(deps.get(("d", b), 0), v)
                else:
                    deps[kk] = max(deps.get(kk, 0), v)

    def _emit_waits(self, e, deps):
        eng = self.eng[e]
        kn = self.known[e]
        for kk, v in deps.items():
            if isinstance(kk, tuple):
                b = kk[1]
                key = ("d", id(b.dsem))
                if kn.get(key, 0) >= v:
                    continue
                eng.wait_ge(b.dsem.h, 16 * v)
                kn[key] = v
            else:
                if kk == e and (e == "pe" or v > self.cnt[e]):
                    continue
                if kn.get(kk, 0) >= v:
                    continue
                eng.wait_ge(self.sem[kk], v)
                kn[kk] = v

    def op(self, e, fn, reads=(), writes=(), sig=True):
        deps = {}
        for b in reads:
            self._need(e, deps, b, False)
        for b in writes:
            self._need(e, deps, b, True)
        self._emit_waits(e, deps)
        ins = fn(self.eng[e])
        self.n_ins += 1
        if sig:
            self.cnt[e] += 1
            ins.then_inc(self.sem[e], 1)
            v = self.cnt[e]
        else:
            v = self.cnt[e] + 1
        for b in reads:
            b.r[e] = max(b.r.get(e, 0), v)
        for b in writes:
            b.w = (e, v)
            b.r = {}
        return ins

    def dma(self, q, out, in_, reads=(), writes=(), sbuf=None, **kw):
        deps = {}
        for b in reads:
            self._need(q, deps, b, False)
        for b in writes:
            self._need(q, deps, b, True)
        self._emit_waits(q, deps)
        b = sbuf
        if b.dsem is None:
            b.dsem = self._slot("sw" if q == "pool" else "hw")
        assert b.dsem.kind == ("sw" if q == "pool" else "hw"), b.name
        ins = self.eng[q].dma_start(out=out, in_=in_, **kw)
        ins.then_inc(b.dsem.h, 16)
        self.n_ins += 1
        b.dsem.cnt += 1
        for x in reads:
            if x is not b:
                raise ValueError("dma reads must be the tracked sbuf")
            x.r["dma"] = b.dsem.cnt
        for x in writes:
            if x is not b:
                raise ValueError("dma writes must be the tracked sbuf")
            x.w = ("dma", b.dsem.cnt)
            x.r = {}
        return ins

    def _slot(self, kind):
        for i, sl in enumerate(self.free_slots):
            if sl.kind == kind:
                return self.free_slots.pop(i)
        sl = SemSlot(self.nc.alloc_semaphore(name=f"d_{self.nsem}"))
        sl.kind = kind
        self.nsem += 1
        self.slots.append(sl)
        return sl

    def scope(self):
        return Scope(self)

    def dma_fence(self, q="sp"):
        kn = self.known[q]
        for sl in self.slots:
            if sl.cnt > 0:
                key = ("d", id(sl))
                if kn.get(key, 0) < sl.cnt:
                    self.eng[q].wait_ge(sl.h, 16 * sl.cnt)
                    kn[key] = sl.cnt

    def barrier(self):
        for e in self.ENGS:
            kn = self.known[e]
            for o in self.ENGS:
                if (o == e and e == "pe") or self.cnt[o] == 0:
                    continue
                if kn.get(o, 0) < self.cnt[o]:
                    self.eng[e].wait_ge(self.sem[o], self.cnt[o])
                    kn[o] = self.cnt[o]
        for e in self.ENGS:
            self.dma_fence(e)

    def finish(self):
        self.barrier()


class Cfg:
    def __init__(s, D=2048, L=2048, DEPTH=2, split=1):
        s.D, s.L, s.DEPTH, s.SP = D, L, DEPTH, split
        s.KT, s.NT, s.NB = D // 128, L // 128, L // 512
        s.LOW = 16
        s.HK, s.HV = (D // 2) // 4, D // 4
        s.KC, s.VC = s.HK // 128, s.HV // 128
        s.GH = 4 // split
        s.DK, s.DV = s.GH * s.HK, s.GH * s.HV
        s.HD, s.R = 128, (D // 128) // 4
        s.NG = 4 // split
        s.NH = s.NG * s.R
        s.NW, s.KVW = s.NH * 128, s.NG * 128
        s.CL, s.CS, s.CH, s.SB, s.WIN = 32, 16, 256, 64, 512
        s.NCMP, s.NSEL = (L - 32) // 16 + 1, L // 64
        s.TOPK = min(16, s.NSEL)
        s.P, s.N, s.CONV = 64, 128, 4
        s.HPG = (2 * D // 64) // 8
        s.SG = 8 // split
        s.SH = s.SG * s.HPG
        s.DI = s.SH * 64
        s.CD = s.DI + 2 * s.SG * 128
        s.FFG = ((8 * D + 3 * 256 - 1) // (3 * 256)) * 256
        s.FF = s.FFG // split
        s.FT = s.FF // 128
        s.MIX = s.DV + s.NW + s.DI
        sizes = (3 * D, s.DK, s.DK, s.DV, 16, s.DV, s.NW, 6 * s.KVW, 3 * s.NH, s.DI, s.CD, s.SH)
        names = ("gate", "q", "k", "v", "low", "r", "nq", "nkv", "ng", "z", "xbc", "dt")
        s.off = {}
        o = 0
        for n, z in zip(names, sizes):
            s.off[n] = o
            o += z
        s.IN_COLS = o

    def global_cols(s, rank):
        g = Cfg(s.D, s.L, s.DEPTH, 1)
        SP = s.SP
        hs = [rank * s.GH + i for i in range(s.GH)]
        ngs = [rank * s.NG + i for i in range(s.NG)]
        sgs = [rank * s.SG + i for i in range(s.SG)]
        ar = np.arange
        cols = [ar(0, 3 * s.D)]
        cols += [g.off["q"] + h * s.HK + ar(s.HK) for h in hs]
        cols += [g.off["k"] + h * s.HK + ar(s.HK) for h in hs]
        cols += [g.off["v"] + h * s.HV + ar(s.HV) for h in hs]
        cols += [g.off["low"] + ar(16)]
        cols += [g.off["r"] + h * s.HV + ar(s.HV) for h in hs]
        cols += [g.off["nq"] + gg * s.R * 128 + ar(s.R * 128) for gg in ngs]
        for kind in range(6):
            cols += [g.off["nkv"] + kind * g.KVW + gg * 128 + ar(128) for gg in ngs]
        cols += [g.off["ng"] + gg * s.R * 3 + ar(s.R * 3) for gg in ngs]
        cols += [g.off["z"] + gg * s.HPG * 64 + ar(s.HPG * 64) for gg in sgs]
        xl = s.local_xbc(rank)
        cols += [g.off["xbc"] + xl]
        cols += [g.off["dt"] + gg * s.HPG + ar(s.HPG) for gg in sgs]
        cols = np.concatenate(cols)
        assert cols.size == s.IN_COLS, (cols.size, s.IN_COLS)
        return cols

    def local_xbc(s, rank):
        g = Cfg(s.D, s.L, s.DEPTH, 1)
        sgs = [rank * s.SG + i for i in range(s.SG)]
        ar = np.arange
        return np.concatenate([gg * s.HPG * 64 + ar(s.HPG * 64) for gg in sgs]
                              + [g.DI + gg * 128 + ar(128) for gg in sgs]
                              + [g.DI + 8 * 128 + gg * 128 + ar(128) for gg in sgs])

    def local_mix_rows(s, rank):
        g = Cfg(s.D, s.L, s.DEPTH, 1)
        hs = [rank * s.GH + i for i in range(s.GH)]
        ngs = [rank * s.NG + i for i in range(s.NG)]
        sgs = [rank * s.SG + i for i in range(s.SG)]
        ar = np.arange
        return np.concatenate([h * s.HV + ar(s.HV) for h in hs]
                              + [g.DV + gg * s.R * 128 + ar(s.R * 128) for gg in ngs]
                              + [g.DV + g.NW + gg * s.HPG * 64 + ar(s.HPG * 64) for gg in sgs])


def host_consts(c):
    import ml_dtypes
    bf = ml_dtypes.bfloat16
    L = c.L
    i = np.arange(128)
    tri = (i[:, None] <= i[None, :]).astype(np.float32)
    upper = (i[:, None] > i[None, :]).astype(np.float32)
    half = 16
    inv = (500000.0 ** (-np.arange(half, dtype=np.float32) / half)).astype(np.float32)
    ang = np.arange(L, dtype=np.float32)[None, :] * inv[:, None]
    cos, sin = np.cos(ang).astype(np.float32), np.sin(ang).astype(np.float32)
    ropeC = np.concatenate([cos, cos, np.ones((96, L), np.float32)], 0)
    ropeS = np.concatenate([sin, sin, np.zeros((96, L), np.float32)], 0)
    Rm = np.zeros((128, 128), np.float32)
    for d in range(16):
        Rm[d + 16, d] = -1.0
        Rm[d, d + 16] = 1.0
    n = np.arange(128)
    cmpmask = ((16 * n[:, None] + 31) <= np.arange(L)[None, :]).astype(np.float32)
    cmpmask[c.NCMP:] = 0
    cs = np.arange(c.NCMP) * 16
    ss = np.arange(c.NSEL) * 64
    ovl = np.clip(np.minimum(cs[:, None] + 32, ss[None, :] + 64) - np.maximum(cs[:, None], ss[None, :]), 0, None
                  ).astype(np.float32) / 32
    ovl_p = np.zeros((128, c.NSEL), np.float32)
    ovl_p[:c.NCMP] = ovl
    E = (np.arange(L)[None, :] // 64 == np.arange(c.NSEL)[:, None]).astype(np.float32)
    blk_t = (np.arange(L) // 64)[:, None]
    blk_j = np.arange(c.NSEL)[None, :]
    forced = (blk_j == 0) | (blk_j == blk_t) | (blk_j == blk_t - 1)
    valid = blk_j <= blk_t
    selmul = (valid & ~forced).astype(np.float32)
    selbias = np.where(forced, 1e9, np.where(valid, 0.0, -1.0)).astype(np.float32)
    def tm(a):
        return np.ascontiguousarray(a.reshape(c.NT, 128, -1).transpose(1, 0, 2))
    return {
        "c_ident": np.eye(128, dtype=np.float32).astype(bf),
        "c_tri": tri, "c_trib": tri.astype(bf), "c_upperb": upper.astype(bf),
        "c_ropeC": ropeC, "c_ropeS": ropeS, "c_Rm": Rm.astype(bf),
        "c_cmpmask": cmpmask.astype(bf), "c_ovl": ovl_p, "c_E": E.astype(bf),
        "c_selmul": tm(selmul), "c_selbias": tm(selbias),
    }


class Ctx:
    pass


def _w3(ap2, p=128):
    return ap2.rearrange("(k p) c -> p k c", p=p)


def pair_allreduce(C, buf):
    k = C.k
    if C.c.SP == 1:
        return
    k.dma("sp", _w3(C.cc_src.ap()), buf[:], reads=[buf], sbuf=buf)
    k.dma_fence("pool")
    C.cc_n += 1
    C.nc.gpsimd.collective_compute("AllReduce", ALU.add, replica_groups=C.groups,
                                   ins=[C.cc_src.ap().opt()], outs=[C.cc_dst.ap().opt()]).then_inc(C.cc_sem)
    C.nc.sync.wait_ge(C.cc_sem, C.cc_n)
    k.dma("sp", buf[:], _w3(C.cc_dst.ap()), writes=[buf], sbuf=buf)


def build(c, debug=False, phases=None, depth=None, groups=None):
    nc = bass.Bass("TRN2", target_bir_lowering=False)
    k = K(nc)
    C = Ctx()
    C.nc, C.k, C.c = nc, k, c
    D, L = c.D, c.L
    DEPTH = c.DEPTH if depth is None else depth

    def inp(name, shape, dt=F32):
        return nc.dram_tensor(name, list(shape), dt, kind="ExternalInput").ap()

    def scratch(name, shape, dt):
        kind = "ExternalOutput" if debug else "Internal"
        return nc.dram_tensor(name, list(shape), dt, kind=kind).ap()

    I = {}
    I["xT"] = inp("xT", [D, L])
    I["norm_mix"] = inp("norm_mix", [c.DEPTH, 128, c.KT])
    I["w_in"] = inp("w_in", [c.DEPTH, D, c.IN_COLS])
    I["gla_w2aug"] = inp("gla_w2aug", [c.DEPTH, 17, c.DK])
    I["gla_out_norm"] = inp("gla_out_norm", [c.DEPTH, 128, c.VC])
    I["nsa_peT_k"] = inp("nsa_peT_k", [c.DEPTH, 128, 32])
    I["nsa_peT_v"] = inp("nsa_peT_v", [c.DEPTH, 128, 32])
    I["nsa_k_w1"] = inp("nsa_k_w1", [c.DEPTH, 4096, 256])
    I["nsa_k_w2"] = inp("nsa_k_w2", [c.DEPTH, 256, 128])
    I["nsa_v_w1"] = inp("nsa_v_w1", [c.DEPTH, 4096, 256])
    I["nsa_v_w2"] = inp("nsa_v_w2", [c.DEPTH, 256, 128])
    I["ssd_conv_w"] = inp("ssd_conv_w", [c.DEPTH, 128, c.CD // 128, 4])
    I["ssd_conv_b"] = inp("ssd_conv_b", [c.DEPTH, 128, c.CD // 128])
    I["ssd_rows"] = inp("ssd_rows", [c.DEPTH, 3, c.SH])
    I["ssd_out_norm"] = inp("ssd_out_norm", [c.DEPTH, c.DI])
    I["w_branch"] = inp("w_branch", [c.DEPTH, c.MIX, D])
    I["w_out"] = inp("w_out", [c.DEPTH, D, D])
    I["norm_ffn"] = inp("norm_ffn", [c.DEPTH, 128, c.KT])
    I["w_ffn_gate"] = inp("w_ffn_gate", [c.DEPTH, D, c.FF])
    I["w_ffn_up"] = inp("w_ffn_up", [c.DEPTH, D, c.FF])
    I["w_ffn_down"] = inp("w_ffn_down", [c.DEPTH, c.FF, D])
    I["norm_final"] = inp("norm_final", [128, c.KT])
    hc = host_consts(c)
    for n_, a_ in hc.items():
        I[n_] = inp(n_, a_.shape, BF16 if a_.dtype != np.float32 else F32)
    C.I = I
    C.out = nc.dram_tensor("outT", [D, L], F32, kind="ExternalOutput").ap()
    C.xres = scratch("xres", [D, L], F32)
    C.mixT = scratch("mixT", [c.MIX, L], BF16)
    C.s_xs = scratch("s_xs", [L, c.DI], BF16)
    C.s_zs = scratch("s_zs", [L, c.DI], BF16)
    C.s_bm = scratch("s_bm", [L, c.SG * 128], BF16)
    C.s_bmT = scratch("s_bmT", [c.SG * 128, L], BF16)
    C.s_cmT = scratch("s_cmT", [c.SG * 128, L], BF16)
    C.s_acT = scratch("s_acT", [c.SH, L], F32)
    C.debug = debug
    C.wb = {
        "gate": nc.dram_tensor("wb_gate", [D, 3 * D], BF16, kind="Internal").ap(),
        "branch": nc.dram_tensor("wb_branch", [c.MIX, D], BF16, kind="Internal").ap(),
        "out": nc.dram_tensor("wb_out", [D, D], BF16, kind="Internal").ap(),
        "fg": nc.dram_tensor("wb_fg", [D, c.FF], BF16, kind="Internal").ap(),
        "fu": nc.dram_tensor("wb_fu", [D, c.FF], BF16, kind="Internal").ap(),
        "fd": nc.dram_tensor("wb_fd", [c.FF, D], BF16, kind="Internal").ap(),
    }
    C.pc = Buf(k, None, "precast")
    C.groups = groups
    if c.SP > 1:
        C.cc_src = nc.dram_tensor("cc_src", [D, 512], F32)
        C.cc_dst = nc.dram_tensor("cc_dst", [D, 512], F32)
        C.cc_sem = nc.alloc_semaphore(name="cc_sem")
        C.cc_n = 0
    C.drip = []
    if debug:
        C.d_x1 = scratch("d_x1", [D, L], F32)
        C.d_m = scratch("d_m", [D, L], BF16)
        C.d_h = scratch("d_h", [D, L], BF16)

    C.pb = [k.ps(f"pb{i}", [128, 512], F32) for i in range(8)]
    C._bank = 0

    C.rot = list(range(8))

    def bank():
        b = C.pb[C.rot[C._bank % len(C.rot)]]
        C._bank += 1
        return b
    C.bank = bank
    K_ = {}
    C.hc = hc
    for n_, a_ in hc.items():
        if n_ not in ("c_ident", "c_tri", "c_trib", "c_upperb"):
            continue
        dt = BF16 if a_.dtype != np.float32 else F32
        K_[n_] = k.sb("s" + n_, list(a_.shape), dt)
        k.dma("sp", K_[n_][:], I[n_], writes=[K_[n_]], sbuf=K_[n_])
    K_["ones_bf"] = k.sb("ones_bf", [128, 128], BF16)
    k.op("dve", lambda e: e.memset(K_["ones_bf"][:], 1.0), writes=[K_["ones_bf"]])
    K_["eps"] = k.sb("eps_col", [128, 1], F32)
    k.op("dve", lambda e: e.memset(K_["eps"][:], 1e-6), writes=[K_["eps"]])
    K_["one"] = k.sb("one_col", [128, 1], F32)
    k.op("dve", lambda e: e.memset(K_["one"][:], 1.0), writes=[K_["one"]])
    C.K = K_

    run = phases or ("init", "gla", "nsa", "ssd", "merge", "final")
    if "init" in run:
        phase_init(C)
    for l in range(DEPTH):
        with k.scope() as S0:
            dt_tok = S0.sb("dt_tok", [128, c.NT, c.SH], F32)
            acum_tok = S0.sb("acum_tok", [128, c.NT, c.SH], F32)
            with k.scope() as S:
                hT = S.sb("hT", [128, c.KT, L], BF16)
                if "merge" in run:
                    precast_plan(C, l)
                phase_norm(C, S, C.xres, I["norm_mix"][l], hT)
                if "gla" in run:
                    phase_gla(C, l, hT)
                if "nsa" in run:
                    phase_nsa(C, l, hT)
                if "ssd" in run:
                    phase_ssd_prep(C, l, hT, dt_tok, acum_tok)
            if "ssd" in run:
                phase_ssd_loop(C, l, dt_tok, acum_tok)
        if "merge" in run:
            phase_merge_ffn(C, l)
    if "final" in run:
        phase_final(C)
    k.finish()
    C.n_ins = k.n_ins
    return nc, C


def phase_init(C):
    k, c = C.k, C.c
    with k.scope() as S:
        bufs = [S.sb(f"xi{i}", [128, c.L], F32) for i in range(2)]
        for kt in range(c.KT):
            b = bufs[kt % 2]
            k.dma("sp", b[:], C.I["xT"][kt * 128:(kt + 1) * 128, :], writes=[b], sbuf=b)
            k.dma("sp", C.xres[kt * 128:(kt + 1) * 128, :], b[:], reads=[b], sbuf=b)


def phase_norm(C, S, x_dram, g_ap, hT, tbs=None, hoff=0):
    k, c = C.k, C.c
    KT = c.KT
    with k.scope() as S2:
        gcol = S2.sb("gcol", [128, KT], F32)
        k.dma("sp", gcol[:], g_ap, writes=[gcol], sbuf=gcol)
        xb = [S2.sb(f"nx{i}", [128, KT, 512], F32) for i in range(2)]
        sq = [S2.sb(f"nsq{i}", [128, 512], BF16) for i in range(2)]
        rs = S2.sb("nrs", [128, 512], F32)
        rs2 = S2.sb("nrs2", [128, 512], F32)
        for i, tb in enumerate(tbs if tbs is not None else range(c.NB)):
            x = xb[i % 2]
            k.dma("sp", x[:], _w3(x_dram[:, tb * 512:(tb + 1) * 512]), writes=[x], sbuf=x)
            ps = C.bank()
            for kt in range(KT):
                s_ = sq[kt % 2]
                k.op("act", lambda e: e.activation(out=s_[:], in_=x[:, kt, :], func=AF.Square),
                     reads=[x], writes=[s_])
                k.op("pe", lambda e: e.matmul(ps[:], lhsT=C.K["ones_bf"][:], rhs=s_[:],
                                              start=(kt == 0), stop=(kt == KT - 1)),
                     reads=[s_, C.K["ones_bf"]], writes=[ps])
            k.op("act", lambda e: e.activation(out=rs[:], in_=ps[:], func=AF.Sqrt,
                                               bias=C.K["eps"][:], scale=1.0 / c.D),
                 reads=[ps, C.K["eps"]], writes=[rs])
            k.op("dve", lambda e: e.reciprocal(out=rs2[:], in_=rs[:]), reads=[rs], writes=[rs2])
            o = (hoff + i) * 512
            for kt in range(KT):
                k.op("dve", lambda e: e.scalar_tensor_tensor(
                    out=hT[:, kt, o:o + 512], in0=x[:, kt, :], scalar=gcol[:, kt:kt + 1],
                    in1=rs2[:], op0=ALU.mult, op1=ALU.mult), reads=[x, gcol, rs2], writes=[hT])


class WRing:
    def __init__(self, C, S, name, shape, n=2, q="pool"):
        self.C, self.q = C, q
        self.bufs = [S.sb(f"{name}{i}", shape, BF16) for i in range(n)]
        self.i = 0

    def load(self, src3, kt, ncols):
        b = self.bufs[self.i % len(self.bufs)]
        self.i += 1
        self.C.k.dma(self.q, b[:, 0:kt, 0:ncols], src3, writes=[b], sbuf=b)
        if self.q == "pool":
            drip(self.C, 2)
        return b


def mm_acc(C, ps_ap, ps_buf, pairs, rbufs):
    n = len(pairs)
    for i, (l_, r_) in enumerate(pairs):
        C.k.op("pe", lambda e: e.matmul(ps_ap, lhsT=l_, rhs=r_, start=(i == 0), stop=(i == n - 1)),
               reads=rbufs, writes=[ps_buf], sig=(i == n - 1))


def phase_gla(C, l, hT):
    k, c, I, Kc = C.k, C.c, C.I, C.K
    D, L, KT, NT, NB = c.D, c.L, c.KT, c.NT, c.NB
    HK, HV, KC, VC = c.HK, c.HV, c.KC, c.VC
    w_in = I["w_in"][l]
    with k.scope() as S:
        wr = WRing(C, S, "gw", [128, KT, 512])
        gaug = S.sb("gaug", [32, L], BF16)
        w2aug = S.sb("w2aug", [32, c.DK], BF16)
        k.op("dve", lambda e: e.memset(gaug[:], 1.0), writes=[gaug])
        k.dma("pool", w2aug[0:17, :], I["gla_w2aug"][l], writes=[w2aug], sbuf=w2aug)
        wl = wr.load(_w3(w_in[:, c.off["low"]:c.off["low"] + 16]), KT, 16)
        for tb in range(NB):
            ps = C.bank()
            mm_acc(C, ps[0:16, :], ps, [(wl[:, kt, 0:16], hT[:, kt, tb * 512:(tb + 1) * 512]) for kt in range(KT)],
                   [wl, hT])
            k.op("act", lambda e: e.activation(out=gaug[0:16, tb * 512:(tb + 1) * 512], in_=ps[0:16, :], func=AF.Copy),
                 reads=[ps], writes=[gaug])
        gn = S.sb("gnorm", [128, VC], F32)
        k.dma("sp", gn[:], I["gla_out_norm"][l], writes=[gn], sbuf=gn)
        for hd in range(c.GH):
            phase_gla_head(C, S, l, hT, hd, wr, gaug, w2aug, gn)


def phase_gla_head(C, S0, l, hT, hd, wr, gaug, w2aug, gn):
    k, c, I, Kc = C.k, C.c, C.I, C.K
    D, L, KT, NT, NB = c.D, c.L, c.KT, c.NT, c.NB
    HK, HV, KC, VC = c.HK, c.HV, c.KC, c.VC
    w_in = I["w_in"][l]
    tri = Kc["c_tri"]
    with k.scope() as S:
        qT = S.sb("qT", [128, KC, L], BF16)
        kT = S.sb("kT", [128, KC, L], BF16)
        ktok = S.sb("ktok", [128, NT, HK], BF16)
        vtok = S.sb("vtok", [128, NT, HV], BF16)
        srT = S.sb("srT", [128, VC, L], BF16)
        ebl = S.sb("ebl", [128, KC, NT], F32)
        ogT = S.sb("ogT", [128, VC, L], BF16)
        def fm(col0, ncol, dst, func):
            w = wr.load(_w3(w_in[:, col0:col0 + ncol]), KT, ncol)
            for ct in range(ncol // 128):
                for tb in range(NB):
                    ps = C.bank()
                    mm_acc(C, ps[:], ps, [(w[:, kt, ct * 128:(ct + 1) * 128], hT[:, kt, tb * 512:(tb + 1) * 512])
                                          for kt in range(KT)], [w, hT])
                    k.op("act", lambda e: e.activation(out=dst[:, ct, tb * 512:(tb + 1) * 512], in_=ps[:], func=func),
                         reads=[ps], writes=[dst])
            return w

        def tm(w, ncol, dst):
            for tt in range(NT):
                ps = C.bank()
                mm_acc(C, ps[:, 0:ncol], ps, [(hT[:, kt, tt * 128:(tt + 1) * 128], w[:, kt, 0:ncol])
                                              for kt in range(KT)], [w, hT])
                k.op("act", lambda e: e.activation(out=dst[:, tt, :], in_=ps[:, 0:ncol], func=AF.Copy),
                     reads=[ps], writes=[dst])

        fm(c.off["q"] + hd * HK, HK, qT, AF.Copy)
        wk = fm(c.off["k"] + hd * HK, HK, kT, AF.Copy)
        tm(wk, HK, ktok)
        wv = wr.load(_w3(w_in[:, c.off["v"] + hd * HV: c.off["v"] + (hd + 1) * HV]), KT, HV)
        tm(wv, HV, vtok)
        fm(c.off["r"] + hd * HV, HV, srT, AF.Silu)
        spb = [S.sb(f"sp{i}", [128, HK], F32) for i in range(2)]
        t1 = [S.sb(f"gt1_{i}", [128, HK], F32) for i in range(2)]
        t2 = [S.sb(f"gt2_{i}", [128, 128], F32) for i in range(2)]
        t3 = [S.sb(f"gt3_{i}", [128, 128], F32) for i in range(2)]
        for tt in range(NT):
            sl = slice(tt * 128, (tt + 1) * 128)
            sp, a1 = spb[tt % 2], t1[tt % 2]
            ps = C.bank()
            k.op("pe", lambda e: e.matmul(ps[:, 0:HK], lhsT=gaug[0:17, sl], rhs=w2aug[0:17, hd * HK:(hd + 1) * HK],
                                          start=True, stop=True), reads=[gaug, w2aug], writes=[ps])
            k.op("act", lambda e: e.activation(out=a1[:], in_=ps[:, 0:HK], func=AF.Exp, scale=-1.0),
                 reads=[ps], writes=[a1])
            k.op("act", lambda e: e.activation(out=sp[:], in_=a1[:], func=AF.Ln, bias=Kc["one"][:], scale=1.0),
                 reads=[a1, Kc["one"]], writes=[sp])
            ps2 = C.bank()
            k.op("pe", lambda e: e.matmul(ps2[:, 0:HK], lhsT=tri[:], rhs=sp[:], start=True, stop=True),
                 reads=[tri, sp], writes=[ps2])
            k.op("act", lambda e: e.activation(out=a1[:], in_=ps2[:, 0:HK], func=AF.Exp, scale=1.0 / 16),
                 reads=[ps2], writes=[a1])
            k.op("dve", lambda e: e.tensor_tensor(out=ktok[:, tt, :], in0=ktok[:, tt, :], in1=a1[:], op=ALU.mult),
                 reads=[ktok, a1], writes=[ktok])
            for kc in range(KC):
                ep, en = t2[kc % 2], t3[kc % 2]
                ps3 = C.bank()
                k.op("pe", lambda e: e.matmul(ps3[:, 0:128], lhsT=sp[:, kc * 128:(kc + 1) * 128], rhs=tri[:],
                                              start=True, stop=True), reads=[tri, sp], writes=[ps3])
                k.op("act", lambda e: e.activation(out=ep[:], in_=ps3[:, 0:128], func=AF.Exp, scale=1.0 / 16),
                     reads=[ps3], writes=[ep])
                k.op("act", lambda e: e.activation(out=en[:], in_=ps3[:, 0:128], func=AF.Exp, scale=-1.0 / 16),
                     reads=[ps3], writes=[en])
                k.op("dve", lambda e: e.tensor_tensor(out=kT[:, kc, sl], in0=kT[:, kc, sl], in1=ep[:], op=ALU.mult),
                     reads=[kT, ep], writes=[kT])
                k.op("dve", lambda e: e.scalar_tensor_tensor(out=qT[:, kc, sl], in0=qT[:, kc, sl],
                                                             scalar=float(HK) ** -0.5, in1=en[:],
                                                             op0=ALU.mult, op1=ALU.mult),
                     reads=[qT, en], writes=[qT])
                k.op("act", lambda e: e.activation(out=ebl[:, kc, tt:tt + 1], in_=en[:, 127:128], func=AF.Copy),
                     reads=[en], writes=[ebl])
        S32 = S.sb("S32", [128, KC, HV], F32)
        Sb = S.sb("Sb", [128, KC, HV], BF16)
        k.op("dve", lambda e: e.memset(S32[:], 0.0), writes=[S32])
        k.op("dve", lambda e: e.memset(Sb[:], 0.0), writes=[Sb])
        scm = [S.sb(f"scm{i}", [128, 128], BF16) for i in range(2)]
        sq = [S.sb(f"gsq{i}", [128, VC, 128], BF16) for i in range(2)]
        rsd = [S.sb(f"grs{i}", [128, 128], F32) for i in range(2)]
        tmp = [S.sb(f"gtm{i}", [128, 128], F32) for i in range(2)]
        tS = [S.sb(f"gtS{i}", [128, HV], F32) for i in range(2)]
        for ch in range(NT):
            sl = slice(ch * 128, (ch + 1) * 128)
            sc, sqb, rs = scm[ch % 2], sq[ch % 2], rsd[ch % 2]
            ps = C.bank()
            mm_acc(C, ps[:, 0:128], ps, [(kT[:, kc, sl], qT[:, kc, sl]) for kc in range(KC)], [kT, qT])
            k.op("dve", lambda e: e.tensor_tensor(out=sc[:], in0=ps[:, 0:128], in1=tri[:], op=ALU.mult),
                 reads=[ps, tri], writes=[sc])
            po = C.bank()
            for vc in range(VC):
                pairs = [(vtok[:, ch, vc * 128:(vc + 1) * 128], sc[:])]
                pairs += [(Sb[:, kc, vc * 128:(vc + 1) * 128], qT[:, kc, sl]) for kc in range(KC)]
                mm_acc(C, po[:, vc * 128:(vc + 1) * 128], po, pairs, [vtok, sc, Sb, qT])
            k.op("act", lambda e: e.activation(out=sqb[:].rearrange("p a b -> p (a b)"), in_=po[:, 0:VC * 128],
                                               func=AF.Square), reads=[po], writes=[sqb])
            pss = C.bank()
            mm_acc(C, pss[:, 0:128], pss, [(Kc["ones_bf"][:], sqb[:, vc, :]) for vc in range(VC)], [sqb, Kc["ones_bf"]])
            k.op("act", lambda e: e.activation(out=tmp[0][:], in_=pss[:, 0:128], func=AF.Sqrt, bias=Kc["eps"][:],
                                               scale=1.0 / HV), reads=[pss, Kc["eps"]], writes=[tmp[0]])
            k.op("dve", lambda e: e.reciprocal(out=rs[:], in_=tmp[0][:]), reads=[tmp[0]], writes=[rs])
            for vc in range(VC):
                k.op("dve", lambda e: e.scalar_tensor_tensor(out=tmp[1][:], in0=po[:, vc * 128:(vc + 1) * 128],
                                                             scalar=gn[:, vc:vc + 1], in1=rs[:],
                                                             op0=ALU.mult, op1=ALU.mult),
                     reads=[po, gn, rs], writes=[tmp[1]])
                k.op("dve", lambda e: e.tensor_tensor(out=ogT[:, vc, sl], in0=tmp[1][:], in1=srT[:, vc, sl], op=ALU.mult),
                     reads=[tmp[1], srT], writes=[ogT])
            for kc in range(KC):
                pd = C.bank()
                k.op("pe", lambda e: e.matmul(pd[:, 0:HV], lhsT=ktok[:, ch, kc * 128:(kc + 1) * 128], rhs=vtok[:, ch, :],
                                              start=True, stop=True), reads=[ktok, vtok], writes=[pd])
                ts = tS[kc % 2]
                k.op("dve", lambda e: e.tensor_scalar(out=ts[:], in0=pd[:, 0:HV], scalar1=ebl[:, kc, ch:ch + 1],
                                                      scalar2=None, op0=ALU.mult), reads=[pd, ebl], writes=[ts])
                k.op("dve", lambda e: e.scalar_tensor_tensor(out=S32[:, kc, :], in0=S32[:, kc, :],
                                                             scalar=ebl[:, kc, ch:ch + 1], in1=ts[:],
                                                             op0=ALU.mult, op1=ALU.add),
                     reads=[S32, ebl, ts], writes=[S32])
                k.op("act", lambda e: e.activation(out=Sb[:, kc, :], in_=S32[:, kc, :], func=AF.Copy),
                     reads=[S32], writes=[Sb])
        for vc in range(VC):
            r0 = hd * HV + vc * 128
            k.dma("sp", C.mixT[r0:r0 + 128, :], ogT[:, vc, :], reads=[ogT], sbuf=ogT)


def shared_inputs(c, inp, rank=0):
    f = np.float32
    def colT(v, kt):
        v = np.asarray(v, f)
        return np.ascontiguousarray(v.reshape(v.shape[0], kt, 128).transpose(0, 2, 1))
    g = Cfg(c.D, c.L, c.DEPTH, 1)
    hs = [rank * c.GH + i for i in range(c.GH)]
    sgs = [rank * c.SG + i for i in range(c.SG)]
    m = {}
    m["norm_mix"] = colT(inp["norm_mix"], c.KT)
    m["w_in"] = np.ascontiguousarray(np.asarray(inp["w_in"], f)[:, :, c.global_cols(rank)])
    dkc = np.concatenate([h * c.HK + np.arange(c.HK) for h in hs])
    m["gla_w2aug"] = np.ascontiguousarray(np.concatenate(
        [np.asarray(inp["gla_gate_w2"], f), np.asarray(inp["gla_gate_b"], f)[:, None, :]], axis=1)[:, :, dkc])
    m["gla_out_norm"] = colT(inp["gla_out_norm"], c.VC)
    m["nsa_peT_k"] = np.ascontiguousarray(np.asarray(inp["nsa_cmp_pos_k"], f).transpose(0, 2, 1))
    m["nsa_peT_v"] = np.ascontiguousarray(np.asarray(inp["nsa_cmp_pos_v"], f).transpose(0, 2, 1))
    m["nsa_k_w1"] = np.asarray(inp["nsa_cmp_k_w1"], f)
    m["nsa_k_w2"] = np.asarray(inp["nsa_cmp_k_w2"], f)
    m["nsa_v_w1"] = np.asarray(inp["nsa_cmp_v_w1"], f)
    m["nsa_v_w2"] = np.asarray(inp["nsa_cmp_v_w2"], f)
    xl = c.local_xbc(rank)
    cw = np.asarray(inp["ssd_conv_w"], f)[:, :, xl]
    CT = c.CD // 128
    m["ssd_conv_w"] = np.ascontiguousarray(cw.transpose(0, 2, 1).reshape(cw.shape[0], CT, 128, 4).transpose(0, 2, 1, 3))
    m["ssd_conv_b"] = colT(np.asarray(inp["ssd_conv_b"], f)[:, xl], CT)
    hl = np.concatenate([gg * c.HPG + np.arange(c.HPG) for gg in sgs])
    m["ssd_rows"] = np.ascontiguousarray(np.stack(
        [np.asarray(inp["ssd_dt_bias"], f)[:, hl], np.asarray(inp["ssd_a_log"], f)[:, hl], np.asarray(inp["ssd_d"], f)[:, hl]], axis=1))
    dil = np.concatenate([gg * c.HPG * 64 + np.arange(c.HPG * 64) for gg in sgs])
    m["ssd_out_norm"] = np.ascontiguousarray(np.asarray(inp["ssd_out_norm"], f)[:, dil])
    m["w_branch"] = np.ascontiguousarray(np.asarray(inp["w_branch"], f)[:, c.local_mix_rows(rank), :])
    m["w_out"] = np.asarray(inp["w_out"], f)
    m["norm_ffn"] = colT(inp["norm_ffn"], c.KT)
    fsl = slice(rank * c.FF, (rank + 1) * c.FF)
    m["w_ffn_gate"] = np.ascontiguousarray(np.asarray(inp["w_ffn_gate"], f)[:, :, fsl])
    m["w_ffn_up"] = np.ascontiguousarray(np.asarray(inp["w_ffn_up"], f)[:, :, fsl])
    m["w_ffn_down"] = np.ascontiguousarray(np.asarray(inp["w_ffn_down"], f)[:, fsl, :])
    m["norm_final"] = colT(np.asarray(inp["norm_final"], f)[None], c.KT)[0]
    m.update(host_consts(c))
    return m


def bc_mid(ap2, n):
    return ap2.unsqueeze(1).to_broadcast([ap2.shape[0], n, ap2.shape[1]])


def phase_nsa(C, l, hT):
    k, c, I, Kc = C.k, C.c, C.I, C.K
    KT = c.KT
    with k.scope() as S:
        for n_, a_ in C.hc.items():
            if n_ in ("c_ident", "c_tri", "c_trib", "c_upperb"):
                continue
            dt = BF16 if a_.dtype != np.float32 else F32
            Kc[n_] = S.sb("s" + n_, list(a_.shape), dt)
            k.dma("sp", Kc[n_][:], I[n_], writes=[Kc[n_]], sbuf=Kc[n_])
        wr = WRing(C, S, "nw", [128, KT, 128], n=3)
        w1 = {}
        w2 = {}
        pe = {}
        for kind in ("k", "v"):
            w2[kind] = S.sb(f"w2{kind}", [128, 2, 128], BF16)
            k.dma("pool", w2[kind][:], _w3(I[f"nsa_{kind}_w2"][l]), writes=[w2[kind]], sbuf=w2[kind])
            pe[kind] = S.sb(f"pe{kind}", [128, 32], F32)
            k.dma("sp", pe[kind][:], I[f"nsa_peT_{kind}"][l], writes=[pe[kind]], sbuf=pe[kind])
        for g in range(c.NG):
            C.rot = [0, 1, 2, 3]
            nsa_group(C, l, hT, g, wr, w1, w2, pe)
            C.rot = list(range(8))


def nsa_group(C, l, hT, g, wr, w1, w2, pe):
    k, c, I, Kc = C.k, C.c, C.I, C.K
    D, L, KT, NT, NB, R = c.D, c.L, c.KT, c.NT, c.NB, c.R
    NCMP, NSEL = c.NCMP, c.NSEL
    NA = 129 + NSEL
    w_in = I["w_in"][l]
    scale = 128.0 ** -0.5
    trib, upb, ident = Kc["c_trib"], Kc["c_upperb"], Kc["c_ident"]
    with k.scope() as S:
        qT = S.sb("nqT", [128, R, L], BF16)
        kx = {n_: S.sb(f"n{n_}", [128, L], BF16) for n_ in ("ks", "kw")}
        vsl = S.sb("nvsl", [128, NT, 129], BF16)
        vw = S.sb("nvw", [128, NT, 129], BF16)
        gt = S.sb("ngt", [128, NT, 3 * R], F32)
        onT = S.sb("onT", [128, R, L], BF16)
        ra = [S.sb(f"nra{i}", [128, 512], F32) for i in range(2)]
        rb = [S.sb(f"nrb{i}", [128, 512], F32) for i in range(2)]

        def fm_rope(w, wc0, dst_ap_fn, dstbuf, rope):
            for tb in range(NB):
                tsl = slice(tb * 512, (tb + 1) * 512)
                ps = C.bank()
                mm_acc(C, ps[:], ps, [(w[:, kt, wc0:wc0 + 128], hT[:, kt, tsl]) for kt in range(KT)], [w, hT])
                dst = dst_ap_fn(tsl)
                k.op("act", lambda e: e.activation(out=dst, in_=ps[:], func=AF.Copy), reads=[ps], writes=[dstbuf])
                if rope:
                    pr = C.bank()
                    a_, b_ = ra[tb % 2], rb[tb % 2]
                    k.op("pe", lambda e: e.matmul(pr[:, :], lhsT=Kc["c_Rm"][:], rhs=dst, start=True, stop=True),
                         reads=[Kc["c_Rm"], dstbuf], writes=[pr])
                    k.op("dve", lambda e: e.tensor_tensor(out=a_[:], in0=ps[:, :], in1=Kc["c_ropeC"][:, tsl], op=ALU.mult),
                         reads=[ps, Kc["c_ropeC"]], writes=[a_])
                    k.op("dve", lambda e: e.tensor_tensor(out=b_[:], in0=pr[:, :], in1=Kc["c_ropeS"][:, tsl], op=ALU.mult),
                         reads=[pr, Kc["c_ropeS"]], writes=[b_])
                    k.op("dve", lambda e: e.tensor_tensor(out=dst, in0=a_[:], in1=b_[:], op=ALU.add),
                         reads=[a_, b_], writes=[dstbuf])

        q0 = c.off["nq"] + g * R * 128
        for r in range(R):
            w = wr.load(_w3(w_in[:, q0 + r * 128:q0 + (r + 1) * 128]), KT, 128)
            fm_rope(w, 0, lambda tsl, r=r: qT[:, r, tsl], qT, True)
        def kvcol(kind):
            return c.off["nkv"] + kind * c.KVW + g * 128
        for kind, name, rope in ((2, "ks", True), (4, "kw", True)):
            w = wr.load(_w3(w_in[:, kvcol(kind):kvcol(kind) + 128]), KT, 128)
            fm_rope(w, 0, lambda tsl, name=name: kx[name][:, tsl], kx[name], rope)
        for kind, dst in ((3, vsl), (5, vw)):
            w = wr.load(_w3(w_in[:, kvcol(kind):kvcol(kind) + 128]), KT, 128)
            k.op("dve", lambda e: e.memset(dst[:], 1.0), writes=[dst])
            for tt in range(NT):
                ps = C.bank()
                mm_acc(C, ps[:, 0:128], ps, [(hT[:, kt, tt * 128:(tt + 1) * 128], w[:, kt, 0:128]) for kt in range(KT)], [w, hT])
                k.op("act", lambda e: e.activation(out=dst[:, tt, 0:128], in_=ps[:, 0:128], func=AF.Copy), reads=[ps], writes=[dst])
        g0 = c.off["ng"] + g * R * 3
        w = wr.load(_w3(w_in[:, g0:g0 + 3 * R]), KT, 3 * R)
        for tt in range(NT):
            ps = C.bank()
            mm_acc(C, ps[:, 0:3 * R], ps, [(hT[:, kt, tt * 128:(tt + 1) * 128], w[:, kt, 0:3 * R]) for kt in range(KT)], [w, hT])
            k.op("act", lambda e: e.activation(out=gt[:, tt, :], in_=ps[:, 0:3 * R], func=AF.Sigmoid), reads=[ps], writes=[gt])

        kcT = S.sb("kcT", [128, 128], BF16)
        vaug = S.sb("vaug", [128, NA], BF16)
        k.op("dve", lambda e: e.memset(vaug[:], 1.0), writes=[vaug])
        k.op("dve", lambda e: e.tensor_copy(out=vaug[:, 129:NA], in_=Kc["c_ovl"][:]), reads=[Kc["c_ovl"]], writes=[vaug])
        with k.scope() as S3:
            for n_ in ("kc", "vc"):
                kx[n_] = S3.sb(f"n{n_}", [128, L], BF16)
            kpe = S3.sb("kpe", [128, 32, NCMP], BF16)
            hs = S3.sb("nhs", [128, 2, 128], BF16)
            w1b = S3.sb("w1b", [128, 32, 256], BF16)
            for kind, name, rope in ((0, "kc", True), (1, "vc", False)):
                w = wr.load(_w3(w_in[:, kvcol(kind):kvcol(kind) + 128]), KT, 128)
                fm_rope(w, 0, lambda tsl, name=name: kx[name][:, tsl], kx[name], rope)
            for kind, src in (("k", kx["kc"]), ("v", kx["vc"])):
                k.dma("pool", w1b[:], _w3(I[f"nsa_{kind}_w1"][l]), writes=[w1b], sbuf=w1b)
                for l_ in range(32):
                    k.op("dve", lambda e: e.tensor_scalar(out=kpe[:, l_, :], in0=src[:, l_:l_ + 16 * (NCMP - 1) + 1:16],
                                                          scalar1=pe[kind][:, l_:l_ + 1], scalar2=None, op0=ALU.add),
                         reads=[src, pe[kind]], writes=[kpe])
                for cc in range(2):
                    ps = C.bank()
                    mm_acc(C, ps[:, 0:NCMP], ps, [(w1b[:, l_, cc * 128:(cc + 1) * 128], kpe[:, l_, :]) for l_ in range(32)],
                           [w1b, kpe])
                    k.op("act", lambda e: e.activation(out=hs[:, cc, 0:NCMP], in_=ps[:, 0:NCMP], func=AF.Silu), reads=[ps], writes=[hs])
                ps = C.bank()
                if kind == "k":
                    mm_acc(C, ps[:, 0:NCMP], ps, [(w2[kind][:, cc, :], hs[:, cc, 0:NCMP]) for cc in range(2)], [w2[kind], hs])
                    k.op("act", lambda e: e.activation(out=kcT[:, 0:NCMP], in_=ps[:, 0:NCMP], func=AF.Copy), reads=[ps], writes=[kcT])
                else:
                    mm_acc(C, ps[0:NCMP, 0:128], ps, [(hs[:, cc, 0:NCMP], w2[kind][:, cc, :]) for cc in range(2)], [w2[kind], hs])
                    k.op("act", lambda e: e.activation(out=vaug[0:NCMP, 0:128], in_=ps[0:NCMP, 0:128], func=AF.Copy), reads=[ps], writes=[vaug])

        oacc = [S.sb(f"oacc{i}", [128, R, 128], F32) for i in range(2)]
        ob = [S.sb(f"nob{i}", [128, R, 128], BF16) for i in range(2)]
        eb = [S.sb(f"neb{i}", [128, R, 128], BF16) for i in range(3)]
        pslc = S.sb("pslc", [128, NSEL], F32)
        score = S.sb("score", [128, NSEL], F32)
        sc2 = S.sb("sc2", [128, NSEL], F32)
        m8 = S.sb("m8", [128, 8], F32)
        m8b = S.sb("m8b", [128, 8], F32)
        selb = S.sb("selb", [128, NSEL], BF16)
        selT = S.sb("selT", [32, 128], BF16)
        rd = [S.sb(f"nrd{i}", [128, 1], F32) for i in range(4)]
        wv_ = [S.sb(f"nwv{i}", [128, 1], F32) for i in range(4)]
        pacc = [C.pb[4 + r] for r in range(R)]
        ebi = 0

        def finish_branch(T, br, first):
            for r in range(R):
                pa = pacc[r]
                k.op("dve", lambda e: e.tensor_scalar(out=rd[r][:], in0=pa[:, 128:129], scalar1=1e-30, scalar2=None, op0=ALU.max),
                     reads=[pa], writes=[rd[r]])
                k.op("dve", lambda e: e.reciprocal(out=rd[r][:], in_=rd[r][:]), reads=[rd[r]], writes=[rd[r]])
                k.op("dve", lambda e: e.tensor_tensor(out=wv_[r][:], in0=rd[r][:], in1=gt[:, T, 3 * r + br:3 * r + br + 1], op=ALU.mult),
                     reads=[rd[r], gt], writes=[wv_[r]])
                oa = oacc[T % 2]
                if first:
                    k.op("dve", lambda e: e.tensor_scalar(out=oa[:, r, :], in0=pa[:, 0:128], scalar1=wv_[r][:], scalar2=None, op0=ALU.mult),
                         reads=[pa, wv_[r]], writes=[oa])
                else:
                    k.op("dve", lambda e: e.scalar_tensor_tensor(out=oa[:, r, :], in0=pa[:, 0:128], scalar=wv_[r][:], in1=oa[:, r, :],
                                                                 op0=ALU.mult, op1=ALU.add), reads=[pa, wv_[r], oa], writes=[oa])

        for T in range(NT):
            sl = slice(T * 128, (T + 1) * 128)
            qv = qT[:, 0:R, sl]
            ps = C.bank()
            k.op("pe", lambda e: e.matmul(ps[0:NCMP, 0:R * 128].rearrange("p (r t) -> p r t", r=R), lhsT=kcT[:, 0:NCMP], rhs=qv,
                                          start=True, stop=True), reads=[kcT, qT], writes=[ps])
            e_ = eb[ebi % 3]; ebi += 1
            k.op("act", lambda e: e.activation(out=e_[0:NCMP].rearrange("p r t -> p (r t)"), in_=ps[0:NCMP, 0:R * 128], func=AF.Exp, scale=scale),
                 reads=[ps], writes=[e_])
            k.op("dve", lambda e: e.tensor_tensor(out=e_[0:NCMP], in0=e_[0:NCMP], in1=bc_mid(Kc["c_cmpmask"][0:NCMP, sl], R), op=ALU.mult),
                 reads=[e_, Kc["c_cmpmask"]], writes=[e_])
            for r in range(R):
                pa = pacc[r]
                k.op("pe", lambda e: e.matmul(pa[:, 0:NA], lhsT=e_[0:NCMP, r, :], rhs=vaug[0:NCMP, :], start=True, stop=True),
                     reads=[e_, vaug], writes=[pa])
            finish_branch(T, 0, True)
            for r in range(R):
                pa = pacc[r]
                if r == 0:
                    k.op("dve", lambda e: e.tensor_scalar(out=pslc[:], in0=pa[:, 129:NA], scalar1=rd[r][:], scalar2=None, op0=ALU.mult),
                         reads=[pa, rd[r]], writes=[pslc])
                else:
                    k.op("dve", lambda e: e.scalar_tensor_tensor(out=pslc[:], in0=pa[:, 129:NA], scalar=rd[r][:], in1=pslc[:],
                                                                 op0=ALU.mult, op1=ALU.add), reads=[pa, rd[r], pslc], writes=[pslc])
            if c.TOPK < NSEL:
                assert c.TOPK == 16
                k.op("dve", lambda e: e.tensor_tensor(out=score[:], in0=pslc[:], in1=Kc["c_selmul"][:, T, :], op=ALU.mult),
                     reads=[pslc, Kc["c_selmul"]], writes=[score])
                k.op("dve", lambda e: e.tensor_tensor(out=score[:], in0=score[:], in1=Kc["c_selbias"][:, T, :], op=ALU.add),
                     reads=[score, Kc["c_selbias"]], writes=[score])
                k.op("dve", lambda e: e.max(out=m8[:], in_=score[:]), reads=[score], writes=[m8])
                k.op("dve", lambda e: e.match_replace(out=sc2[:], in_to_replace=m8[:], in_values=score[:], imm_value=-2.0),
                     reads=[score, m8], writes=[sc2])
                k.op("dve", lambda e: e.max(out=m8b[:], in_=sc2[:]), reads=[sc2], writes=[m8b])
                k.op("dve", lambda e: e.tensor_scalar(out=selb[:], in0=score[:], scalar1=m8b[:, 7:8], scalar2=None, op0=ALU.is_ge),
                     reads=[score, m8b], writes=[selb])
            else:
                k.op("dve", lambda e: e.memset(selb[:], 1.0), writes=[selb])
            pt = C.bank()
            ptb = pt[:].bitcast(BF16)
            k.op("pe", lambda e: e.transpose(out=ptb[0:NSEL, 0:128], in_=selb[:], identity=ident[:]), reads=[selb, ident], writes=[pt])
            k.op("act", lambda e: e.activation(out=selT[0:NSEL, :], in_=ptb[0:NSEL, 0:128], func=AF.Copy), reads=[pt], writes=[selT])
            for br, kT_, vT_ in ((1, kx["ks"], vsl), (2, kx["kw"], vw)):
                kts = list(range(0, T + 1)) if br == 1 else list(range(max(0, T - c.WIN // 128), T + 1))
                for i, kt in enumerate(kts):
                    ksl = slice(kt * 128, (kt + 1) * 128)
                    ps = C.bank()
                    k.op("pe", lambda e: e.matmul(ps[:, 0:R * 128].rearrange("p (r t) -> p r t", r=R), lhsT=kT_[:, ksl], rhs=qv,
                                                  start=True, stop=True), reads=[kT_, qT], writes=[ps])
                    e_ = eb[ebi % 3]; ebi += 1
                    k.op("act", lambda e: e.activation(out=e_[:].rearrange("p r t -> p (r t)"), in_=ps[:, 0:R * 128], func=AF.Exp, scale=scale),
                         reads=[ps], writes=[e_])
                    if kt == T:
                        k.op("dve", lambda e: e.tensor_tensor(out=e_[:], in0=e_[:], in1=bc_mid(trib[:], R), op=ALU.mult),
                             reads=[e_, trib], writes=[e_])
                    elif br == 1:
                        pm = C.bank()
                        k.op("pe", lambda e: e.matmul(pm[:, 0:128], lhsT=Kc["c_E"][0:NSEL, ksl], rhs=selT[0:NSEL, :], start=True, stop=True),
                             reads=[Kc["c_E"], selT], writes=[pm])
                        k.op("dve", lambda e: e.tensor_tensor(out=e_[:], in0=e_[:], in1=bc_mid(pm[:, 0:128], R), op=ALU.mult),
                             reads=[e_, pm], writes=[e_])
                    elif kt == T - c.WIN // 128:
                        k.op("dve", lambda e: e.tensor_tensor(out=e_[:], in0=e_[:], in1=bc_mid(upb[:], R), op=ALU.mult),
                             reads=[e_, upb], writes=[e_])
                    for r in range(R):
                        pa = pacc[r]
                        k.op("pe", lambda e: e.matmul(pa[:, 0:129], lhsT=e_[:, r, :], rhs=vT_[:, kt, :], start=(i == 0), stop=(i == len(kts) - 1)),
                             reads=[e_, vT_], writes=[pa], sig=(i == len(kts) - 1))
                finish_branch(T, br, False)
            oa, o_ = oacc[T % 2], ob[T % 2]
            k.op("act", lambda e: e.activation(out=o_[:].rearrange("p r t -> p (r t)"), in_=oa[:].rearrange("p r t -> p (r t)"), func=AF.Copy),
                 reads=[oa], writes=[o_])
            for r in range(R):
                pt = C.bank()
                ptb = pt[:].bitcast(BF16)
                k.op("pe", lambda e: e.transpose(out=ptb[:, 0:128], in_=o_[:, r, :], identity=ident[:]), reads=[o_, ident], writes=[pt])
                k.op("act", lambda e: e.activation(out=onT[:, r, sl], in_=ptb[:, 0:128], func=AF.Copy), reads=[pt], writes=[onT])
        for r in range(R):
            r0 = c.DV + (g * R + r) * 128
            k.dma("sp", C.mixT[r0:r0 + 128, :], onT[:, r, :], reads=[onT], sbuf=onT)


def bc_last(ap2, n):
    return ap2.unsqueeze(2).to_broadcast([ap2.shape[0], ap2.shape[1], n])


def phase_ssd_prep(C, l, hT, dt_tok, acum_tok):
    k, c, I, Kc = C.k, C.c, C.I, C.K
    D, L, KT, NT, NB = c.D, c.L, c.KT, c.NT, c.NB
    DI, SH, CD = c.DI, c.SH, c.CD
    w_in = I["w_in"][l]
    tri, ident = Kc["c_tri"], Kc["c_ident"]
    XT = DI // 128
    with k.scope() as S:
        wr = WRing(C, S, "sw", [128, KT, 512])
        stg = [S.sb(f"sstg{i}", [128, NT, 512], BF16) for i in range(2)]
        si = 0
        for cb in range(DI // 512):
            w = wr.load(_w3(w_in[:, c.off["z"] + cb * 512: c.off["z"] + (cb + 1) * 512]), KT, 512)
            st = stg[si % 2]; si += 1
            for tt in range(NT):
                ps = C.bank()
                mm_acc(C, ps[:], ps, [(hT[:, kt, tt * 128:(tt + 1) * 128], w[:, kt, :]) for kt in range(KT)], [w, hT])
                k.op("act", lambda e: e.activation(out=st[:, tt, :], in_=ps[:], func=AF.Silu), reads=[ps], writes=[st])
            k.dma("sp", C.s_zs[:, cb * 512:(cb + 1) * 512].rearrange("(t p) c -> p t c", p=128), st[:], reads=[st], sbuf=st)
        cw = S.sb("scw", [128, CD // 128, 4], F32)
        cbias = S.sb("scb", [128, CD // 128], F32)
        k.dma("sp", cw[:], I["ssd_conv_w"][l], writes=[cw], sbuf=cw)
        k.dma("sp", cbias[:], I["ssd_conv_b"][l], writes=[cbias], sbuf=cbias)
        xc = [S.sb(f"sxc{i}", [128, L + 4], F32) for i in range(2)]
        acc = [S.sb(f"sacc{i}", [128, L], F32) for i in range(2)]
        yT = [S.sb(f"syT{i}", [128, L], BF16) for i in range(2)]
        for b_ in xc:
            k.op("dve", lambda e: e.memset(b_[:, 0:4], 0.0), writes=[b_])
        for cb in range(CD // 512):
            w = wr.load(_w3(w_in[:, c.off["xbc"] + cb * 512: c.off["xbc"] + (cb + 1) * 512]), KT, 512)
            st = None
            for ci in range(4):
                ct = cb * 4 + ci
                x_, a_, y_ = xc[ct % 2], acc[ct % 2], yT[ct % 2]
                for tb in range(NB):
                    ps = C.bank()
                    mm_acc(C, ps[:], ps, [(w[:, kt, ci * 128:(ci + 1) * 128], hT[:, kt, tb * 512:(tb + 1) * 512]) for kt in range(KT)], [w, hT])
                    k.op("act", lambda e: e.activation(out=x_[:, 3 + tb * 512: 3 + (tb + 1) * 512], in_=ps[:], func=AF.Copy),
                         reads=[ps], writes=[x_])
                k.op("dve", lambda e: e.tensor_scalar(out=a_[:], in0=x_[:, 0:L], scalar1=cw[:, ct, 0:1], scalar2=None, op0=ALU.mult),
                     reads=[x_, cw], writes=[a_])
                for j in range(1, 4):
                    k.op("dve", lambda e: e.scalar_tensor_tensor(out=a_[:], in0=x_[:, j:j + L], scalar=cw[:, ct, j:j + 1], in1=a_[:],
                                                                 op0=ALU.mult, op1=ALU.add), reads=[x_, cw, a_], writes=[a_])
                k.op("act", lambda e: e.activation(out=y_[:], in_=a_[:], func=AF.Silu, bias=cbias[:, ct:ct + 1], scale=1.0),
                     reads=[a_, cbias], writes=[y_])
                is_x = ct < XT
                is_b = XT <= ct < XT + c.SG
                if is_x or is_b:
                    if st is None:
                        st = stg[si % 2]; si += 1
                    for tt in range(NT):
                        pt = C.bank()
                        ptb = pt[:].bitcast(BF16)
                        k.op("pe", lambda e: e.transpose(out=ptb[:, 0:128], in_=y_[:, tt * 128:(tt + 1) * 128], identity=ident[:]),
                             reads=[y_, ident], writes=[pt])
                        k.op("act", lambda e: e.activation(out=st[:, tt, ci * 128:(ci + 1) * 128], in_=ptb[:, 0:128], func=AF.Copy),
                             reads=[pt], writes=[st])
                if not is_x:
                    g = ct - XT
                    dst = C.s_bmT if g < c.SG else C.s_cmT
                    g = g % c.SG
                    k.dma("sp", dst[g * 128:(g + 1) * 128, :], y_[:], reads=[y_], sbuf=y_)
            if st is not None:
                if cb * 4 < XT:
                    k.dma("sp", C.s_xs[:, cb * 512:(cb + 1) * 512].rearrange("(t p) c -> p t c", p=128), st[:], reads=[st], sbuf=st)
                else:
                    o = cb * 512 - DI
                    k.dma("sp", C.s_bm[:, o:o + 512].rearrange("(t p) c -> p t c", p=128), st[:], reads=[st], sbuf=st)
        rows = S.sb("srows", [128, 3, SH], F32)
        k.dma("sp", rows[:], I["ssd_rows"][l].partition_broadcast(128), writes=[rows], sbuf=rows)
        arow = S.sb("sarow", [128, SH], F32)
        k.op("act", lambda e: e.activation(out=arow[:], in_=rows[:, 1, :], func=AF.Exp), reads=[rows], writes=[arow])
        k.op("dve", lambda e: e.tensor_scalar(out=arow[:], in0=arow[:], scalar1=-1.0, scalar2=None, op0=ALU.mult), reads=[arow], writes=[arow])
        acT = S.sb("sacT", [SH, L], F32)
        t1 = [S.sb(f"sdt{i}", [128, SH], F32) for i in range(2)]
        adt = [S.sb(f"sadt{i}", [128, SH], F32) for i in range(2)]
        w = wr.load(_w3(w_in[:, c.off["dt"]: c.off["dt"] + SH]), KT, SH)
        for tt in range(NT):
            a1, a2 = t1[tt % 2], adt[tt % 2]
            ps = C.bank()
            mm_acc(C, ps[:, 0:SH], ps, [(hT[:, kt, tt * 128:(tt + 1) * 128], w[:, kt, 0:SH]) for kt in range(KT)], [w, hT])
            k.op("dve", lambda e: e.tensor_tensor(out=a1[:], in0=ps[:, 0:SH], in1=rows[:, 0, :], op=ALU.add), reads=[ps, rows], writes=[a1])
            k.op("act", lambda e: e.activation(out=a1[:], in_=a1[:], func=AF.Exp), reads=[a1], writes=[a1])
            k.op("act", lambda e: e.activation(out=dt_tok[:, tt, :], in_=a1[:], func=AF.Ln, bias=Kc["one"][:], scale=1.0),
                 reads=[a1, Kc["one"]], writes=[dt_tok])
            k.op("dve", lambda e: e.tensor_tensor(out=a2[:], in0=dt_tok[:, tt, :], in1=arow[:], op=ALU.mult), reads=[dt_tok, arow], writes=[a2])
            ps2 = C.bank()
            k.op("pe", lambda e: e.matmul(ps2[:, 0:SH], lhsT=tri[:], rhs=a2[:], start=True, stop=True), reads=[tri, a2], writes=[ps2])
            k.op("act", lambda e: e.activation(out=acum_tok[:, tt, :], in_=ps2[:, 0:SH], func=AF.Copy), reads=[ps2], writes=[acum_tok])
            ps3 = C.bank()
            k.op("pe", lambda e: e.matmul(ps3[0:SH, 0:128], lhsT=a2[:], rhs=tri[:], start=True, stop=True), reads=[tri, a2], writes=[ps3])
            k.op("act", lambda e: e.activation(out=acT[:, tt * 128:(tt + 1) * 128], in_=ps3[0:SH, 0:128], func=AF.Copy), reads=[ps3], writes=[acT])
        k.dma("sp", C.s_acT[:, :], acT[:], reads=[acT], sbuf=acT)


def precast_plan(C, l):
    c, I = C.c, C.I
    D = c.D
    jobs = [(C.wb["gate"], I["w_in"][l][:, 0:3 * D], 16), (C.wb["branch"], I["w_branch"][l], 16), (C.wb["out"], I["w_out"][l], 4),
            (C.wb["fg"], I["w_ffn_gate"][l], 16), (C.wb["fu"], I["w_ffn_up"][l], 16), (C.wb["fd"], I["w_ffn_down"][l], 16)]
    for dst, src, n in jobs:
        rows = src.shape[0]
        step = (rows + n - 1) // n
        for r0 in range(0, rows, step):
            r1 = min(rows, r0 + step)
            C.drip.append((dst[r0:r1, :], src[r0:r1, :]))


def drip(C, n):
    for _ in range(n):
        if not C.drip:
            return
        dst, src = C.drip.pop(0)
        C.k.dma("pool", dst, src, sbuf=C.pc)


def phase_ssd_loop(C, l, dt_tok, acum_tok):
    k, c, I, Kc = C.k, C.c, C.I, C.K
    D, L, KT, NT, NB = c.D, c.L, c.KT, c.NT, c.NB
    DI, SH, HPG, G = c.DI, c.SH, c.HPG, c.SG
    GW = HPG * 64
    tri, trib, ident = Kc["c_tri"], Kc["c_trib"], Kc["c_ident"]
    with k.scope() as S:
        k.dma_fence("sp")
        drip(C, len(C.drip))
        ngr = S.sb("ngrow", [128, DI], F32)
        k.dma("sp", ngr[:], I["ssd_out_norm"][l].partition_broadcast(128), writes=[ngr], sbuf=ngr)
        rows = S.sb("lrows", [128, 3, SH], F32)
        k.dma("sp", rows[:], I["ssd_rows"][l].partition_broadcast(128), writes=[rows], sbuf=rows)
        st32 = S.sb("st32", [128, DI], F32)
        stb = S.sb("stb", [128, DI], BF16)
        k.op("dve", lambda e: e.memset(st32[:], 0.0), writes=[st32])
        k.op("dve", lambda e: e.memset(stb[:], 0.0), writes=[stb])
        xs_ = [S.sb(f"lxs{i}", [128, DI], BF16) for i in range(2)]
        zs_ = [S.sb(f"lzs{i}", [128, DI], BF16) for i in range(2)]
        bm_ = [S.sb(f"lbm{i}", [128, G * 128], BF16) for i in range(2)]
        bT_ = [S.sb(f"lbT{i}", [128, G, 128], BF16) for i in range(2)]
        cT_ = [S.sb(f"lcT{i}", [128, G, 128], BF16) for i in range(2)]
        Arow = S.sb("Arow", [128, SH, 128], F32)
        segb = S.sb("segb", [128, SH, 128], BF16)
        xdt = S.sb("xdt", [128, DI], BF16)
        xdd = S.sb("xdd", [128, DI], BF16)
        oT_ = [S.sb(f"loT{i}", [128, DI // 128, 128], BF16) for i in range(2)]
        cbm = S.sb("cbm", [128, G, 128], BF16)
        dec = S.sb("dec", [128, SH], F32)
        eAl = S.sb("eAl", [128, SH], F32)
        eA = S.sb("eA", [128, SH], F32)
        tt_ = [S.sb(f"lt{i}", [128, GW], F32) for i in range(2)]
        uu_ = [S.sb(f"lu{i}", [128, GW], F32) for i in range(2)]
        ob_ = [S.sb(f"lob{i}", [128, GW], BF16) for i in range(2)]
        junk = S.sb("ljunk", [128, GW], BF16)
        ssq = [S.sb(f"lssq{i}", [128, 1], F32) for i in range(2)]
        for ch in range(NT):
            sl = slice(ch * 128, (ch + 1) * 128)
            xs, zs, bm, bT, cT, oT = xs_[ch % 2], zs_[ch % 2], bm_[ch % 2], bT_[ch % 2], cT_[ch % 2], oT_[ch % 2]
            k.dma("sp", xs[:], C.s_xs[sl, :], writes=[xs], sbuf=xs)
            k.dma("sp", zs[:], C.s_zs[sl, :], writes=[zs], sbuf=zs)
            k.dma("sp", bm[:], C.s_bm[sl, :], writes=[bm], sbuf=bm)
            k.dma("sp", bT[:], C.s_bmT[:, sl].rearrange("(g p) t -> p g t", p=128), writes=[bT], sbuf=bT)
            k.dma("sp", cT[:], C.s_cmT[:, sl].rearrange("(g p) t -> p g t", p=128), writes=[cT], sbuf=cT)
            k.dma("sp", Arow[:], C.s_acT[:, sl].partition_broadcast(128), writes=[Arow], sbuf=Arow)
            k.op("dve", lambda e: e.tensor_tensor(out=dec[:], in0=Arow[:, :, 127], in1=acum_tok[:, ch, :], op=ALU.subtract),
                 reads=[Arow, acum_tok], writes=[dec])
            k.op("act", lambda e: e.activation(out=dec[:], in_=dec[:], func=AF.Exp), reads=[dec], writes=[dec])
            k.op("act", lambda e: e.activation(out=eAl[:], in_=Arow[:, :, 127], func=AF.Exp), reads=[Arow], writes=[eAl])
            k.op("act", lambda e: e.activation(out=eA[:], in_=acum_tok[:, ch, :], func=AF.Exp), reads=[acum_tok], writes=[eA])
            xs3 = xs[:].rearrange("p (h q) -> p h q", q=64)
            k.op("dve", lambda e: e.tensor_tensor(out=xdt[:].rearrange("p (h q) -> p h q", q=64), in0=xs3, in1=bc_last(dt_tok[:, ch, :], 64), op=ALU.mult),
                 reads=[xs, dt_tok], writes=[xdt])
            k.op("pool", lambda e: e.tensor_tensor(out=xdd[:].rearrange("p (h q) -> p h q", q=64), in0=xdt[:].rearrange("p (h q) -> p h q", q=64),
                                                    in1=bc_last(dec[:], 64), op=ALU.mult), reads=[xdt, dec], writes=[xdd])
            for h in range(SH):
                k.op("dve", lambda e: e.tensor_scalar(out=Arow[:, h, :], in0=Arow[:, h, :], scalar1=acum_tok[:, ch, h:h + 1], scalar2=0.0,
                                                      op0=ALU.subtract, op1=ALU.min), reads=[Arow, acum_tok], writes=[Arow], sig=(h == SH - 1))
            k.op("act", lambda e: e.activation(out=segb[:].rearrange("p h t -> p (h t)"), in_=Arow[:].rearrange("p h t -> p (h t)"), func=AF.Exp),
                 reads=[Arow], writes=[segb])
            for half in range(G // 4):
                pc = C.bank()
                for gi in range(4):
                    g = half * 4 + gi
                    k.op("pe", lambda e: e.matmul(pc[:, gi * 128:(gi + 1) * 128], lhsT=bT[:, g, :], rhs=cT[:, g, :], start=True, stop=True),
                         reads=[bT, cT], writes=[pc], sig=(gi == 3))
                k.op("dve", lambda e: e.tensor_tensor(out=cbm[:, half * 4:(half + 1) * 4, :], in0=pc[:].rearrange("p (g t) -> p g t", g=4),
                                                      in1=bc_mid(tri[:], 4), op=ALU.mult), reads=[pc, tri], writes=[cbm])
            for g in range(G):
                hs = slice(g * HPG, (g + 1) * HPG)
                gs = slice(g * GW, (g + 1) * GW)
                k.op("dve", lambda e: e.tensor_tensor(out=segb[:, hs, :], in0=segb[:, hs, :], in1=bc_mid(cbm[:, g, :], HPG), op=ALU.mult),
                     reads=[segb, cbm], writes=[segb])
                yd = C.bank()
                for hh in range(HPG):
                    h = g * HPG + hh
                    k.op("pe", lambda e: e.matmul(yd[:, hh * 64:(hh + 1) * 64], lhsT=segb[:, h, :], rhs=xdt[:, h * 64:(h + 1) * 64], start=True, stop=True),
                         reads=[segb, xdt], writes=[yd], sig=(hh == HPG - 1))
                yo = C.bank()
                k.op("pe", lambda e: e.matmul(yo[:, 0:GW], lhsT=cT[:, g, :], rhs=stb[:, gs], start=True, stop=True), reads=[cT, stb], writes=[yo])
                t_, u_, o_ = tt_[g % 2], uu_[g % 2], ob_[g % 2]
                k.op("dve", lambda e: e.tensor_tensor(out=t_[:].rearrange("p (h q) -> p h q", q=64), in0=yo[:, 0:GW].rearrange("p (h q) -> p h q", q=64),
                                                      in1=bc_last(eA[:, hs], 64), op=ALU.mult), reads=[yo, eA], writes=[t_])
                k.op("dve", lambda e: e.tensor_tensor(out=t_[:], in0=t_[:], in1=yd[:, 0:GW], op=ALU.add), reads=[t_, yd], writes=[t_])
                k.op("pool", lambda e: e.tensor_tensor(out=u_[:].rearrange("p (h q) -> p h q", q=64), in0=xs[:, gs].rearrange("p (h q) -> p h q", q=64),
                                                        in1=bc_last(rows[:, 2, hs], 64), op=ALU.mult), reads=[xs, rows], writes=[u_])
                k.op("pool", lambda e: e.tensor_tensor(out=u_[:], in0=u_[:], in1=t_[:], op=ALU.add), reads=[u_, t_], writes=[u_])
                k.op("pool", lambda e: e.tensor_tensor(out=u_[:], in0=u_[:], in1=zs[:, gs], op=ALU.mult), reads=[u_, zs], writes=[u_])
                sq_ = ssq[g % 2]
                k.op("act", lambda e: e.activation(out=junk[:], in_=u_[:], func=AF.Square, accum_out=sq_[:]), reads=[u_], writes=[junk, sq_])
                k.op("act", lambda e: e.activation(out=sq_[:], in_=sq_[:], func=AF.Sqrt, bias=Kc["eps"][:], scale=1.0 / GW),
                     reads=[sq_, Kc["eps"]], writes=[sq_])
                k.op("dve", lambda e: e.reciprocal(out=sq_[:], in_=sq_[:]), reads=[sq_], writes=[sq_])
                k.op("dve", lambda e: e.scalar_tensor_tensor(out=o_[:], in0=u_[:], scalar=sq_[:], in1=ngr[:, gs], op0=ALU.mult, op1=ALU.mult),
                     reads=[u_, sq_, ngr], writes=[o_])
                for j in range(GW // 128):
                    pt = C.bank()
                    ptb = pt[:].bitcast(BF16)
                    k.op("pe", lambda e: e.transpose(out=ptb[:, 0:128], in_=o_[:, j * 128:(j + 1) * 128], identity=ident[:]), reads=[o_, ident], writes=[pt])
                    k.op("act", lambda e: e.activation(out=oT[:, g * (GW // 128) + j, :], in_=ptb[:, 0:128], func=AF.Copy), reads=[pt], writes=[oT])
                pd = C.bank()
                k.op("pe", lambda e: e.matmul(pd[:, 0:GW], lhsT=bm[:, g * 128:(g + 1) * 128], rhs=xdd[:, gs], start=True, stop=True),
                     reads=[bm, xdd], writes=[pd])
                k.op("dve", lambda e: e.tensor_tensor(out=st32[:, gs].rearrange("p (h q) -> p h q", q=64), in0=st32[:, gs].rearrange("p (h q) -> p h q", q=64),
                                                      in1=bc_last(eAl[:, hs], 64), op=ALU.mult), reads=[st32, eAl], writes=[st32])
                k.op("dve", lambda e: e.tensor_tensor(out=st32[:, gs], in0=st32[:, gs], in1=pd[:, 0:GW], op=ALU.add), reads=[st32, pd], writes=[st32])
                k.op("act", lambda e: e.activation(out=stb[:, gs], in_=st32[:, gs], func=AF.Copy), reads=[st32], writes=[stb])
            base = c.DV + c.NW
            k.dma("sp", C.mixT[base:base + DI, sl].rearrange("(k p) t -> p k t", p=128), oT[:], reads=[oT], sbuf=oT)


def norm_sbuf(C, x, gcol, sq, rs, rs2, out_fn, Dn):
    k, c, Kc = C.k, C.c, C.K
    KT = c.KT
    ps = C.bank()
    for kt in range(KT):
        s_ = sq[kt % 2]
        k.op("act", lambda e: e.activation(out=s_[:], in_=x[:, kt, :], func=AF.Square), reads=[x], writes=[s_])
        k.op("pe", lambda e: e.matmul(ps[:], lhsT=Kc["ones_bf"][:], rhs=s_[:], start=(kt == 0), stop=(kt == KT - 1)),
             reads=[s_, Kc["ones_bf"]], writes=[ps])
    k.op("act", lambda e: e.activation(out=rs[:], in_=ps[:], func=AF.Sqrt, bias=Kc["eps"][:], scale=1.0 / Dn),
         reads=[ps, Kc["eps"]], writes=[rs])
    k.op("dve", lambda e: e.reciprocal(out=rs2[:], in_=rs[:]), reads=[rs], writes=[rs2])
    for kt in range(KT):
        dst, dbuf = out_fn(kt)
        k.op("dve", lambda e: e.scalar_tensor_tensor(out=dst, in0=x[:, kt, :], scalar=gcol[:, kt:kt + 1], in1=rs2[:],
                                                     op0=ALU.mult, op1=ALU.mult), reads=[x, gcol, rs2], writes=[dbuf])


def phase_merge_ffn(C, l):
    k, c, I, Kc = C.k, C.c, C.I, C.K
    D, L, KT, NT, NB, FT = c.D, c.L, c.KT, c.NT, c.NB, c.FT
    w_in = I["w_in"][l]
    with k.scope() as S:
        k.dma_fence("sp")
        drip(C, len(C.drip))
        k.dma_fence("sp")
        wr = WRing(C, S, "mw", [128, KT, 512], n=3, q="sp")
        wd = WRing(C, S, "mwd", [128, FT, 128], n=3, q="sp")
        g1 = S.sb("mg1", [128, KT], F32)
        g2 = S.sb("mg2", [128, KT], F32)
        k.dma("sp", g1[:], I["norm_mix"][l], writes=[g1], sbuf=g1)
        k.dma("sp", g2[:], I["norm_ffn"][l], writes=[g2], sbuf=g2)
        xt = S.sb("mx", [128, KT, 512], F32)
        sq = [S.sb(f"msq{i}", [128, 512], BF16) for i in range(2)]
        rs = S.sb("mrs", [128, 512], F32)
        rs2 = S.sb("mrs2", [128, 512], F32)
        hb = S.sb("mh", [128, KT, 512], BF16)
        for tb in range(NB):
            tsl = slice(tb * 512, (tb + 1) * 512)
            k.dma("sp", xt[:], _w3(C.xres[:, tsl]), writes=[xt], sbuf=xt)
            norm_sbuf(C, xt, g1, sq, rs, rs2, lambda kt: (hb[:, kt, :], hb), D)
            with k.scope() as S1:
                brs = ((0, c.DV // 128), (c.DV, c.NW // 128), (c.DV + c.NW, c.DI // 128))
                mix = S1.sb("mmix", [128, max(b_[1] for b_ in brs), 512], BF16)
                m32 = S1.sb("mm32", [128, KT, 512], F32)
                mbf = S1.sb("mmbf", [128, KT, 512], BF16)
                gsb = [S1.sb(f"mgs{i}", [128, 512], F32) for i in range(4)]
                tmp = [S1.sb(f"mtp{i}", [128, 512], F32) for i in range(2)]
                for bi, (r0, nk) in enumerate(brs):
                    k.dma("sp", mix[:, 0:nk, :], _w3(C.mixT[r0:r0 + nk * 128, tsl]), writes=[mix], sbuf=mix)
                    for d4 in range(KT // 4):
                        wg = wr.load(_w3(C.wb["gate"][:, bi * D + d4 * 512: bi * D + (d4 + 1) * 512]), KT, 512)
                        for j4 in range(4):
                            cs = slice(j4 * 128, (j4 + 1) * 128)
                            pg = C.bank()
                            mm_acc(C, pg[:], pg, [(wg[:, kt, cs], hb[:, kt, :]) for kt in range(KT)], [wg, hb])
                            gs = gsb[j4]
                            k.op("act", lambda e: e.activation(out=gs[:], in_=pg[:], func=AF.Sigmoid), reads=[pg], writes=[gs])
                        wbs = [wr.load(_w3(C.wb["branch"][r0 + j * 128: r0 + min(nk, j + KT) * 128, d4 * 512:(d4 + 1) * 512]), min(KT, nk - j), 512)
                               for j in range(0, nk, KT)]
                        for j4 in range(4):
                            dmt = d4 * 4 + j4
                            cs = slice(j4 * 128, (j4 + 1) * 128)
                            gs = gsb[j4]
                            pu = C.bank()
                            mm_acc(C, pu[:], pu, [(wbs[kk // KT][:, kk % KT, cs], mix[:, kk, :]) for kk in range(nk)], wbs + [mix])
                            if bi == 0:
                                k.op("dve", lambda e: e.tensor_tensor(out=m32[:, dmt, :], in0=pu[:], in1=gs[:], op=ALU.mult),
                                     reads=[pu, gs], writes=[m32])
                            else:
                                t_ = tmp[dmt % 2]
                                k.op("dve", lambda e: e.tensor_tensor(out=t_[:], in0=pu[:], in1=gs[:], op=ALU.mult),
                                     reads=[pu, gs], writes=[t_])
                                k.op("pool", lambda e: e.tensor_tensor(out=m32[:, dmt, :], in0=m32[:, dmt, :], in1=t_[:], op=ALU.add),
                                     reads=[m32, t_], writes=[m32])
                for kt in range(KT):
                    k.op("act", lambda e: e.activation(out=mbf[:, kt, :], in_=m32[:, kt, :], func=AF.Copy), reads=[m32], writes=[mbf])
                for d4 in range(KT // 4):
                    wo = wr.load(_w3(C.wb["out"][:, d4 * 512:(d4 + 1) * 512]), KT, 512)
                    for j4 in range(4):
                        dmt = d4 * 4 + j4
                        px = C.bank()
                        mm_acc(C, px[:], px, [(wo[:, kt, j4 * 128:(j4 + 1) * 128], mbf[:, kt, :]) for kt in range(KT)], [wo, mbf])
                        if c.SP == 1:
                            k.op("dve", lambda e: e.tensor_tensor(out=xt[:, dmt, :], in0=xt[:, dmt, :], in1=px[:], op=ALU.add),
                                 reads=[xt, px], writes=[xt])
                        else:
                            k.op("act", lambda e: e.activation(out=m32[:, dmt, :], in_=px[:], func=AF.Copy), reads=[px], writes=[m32])
                if c.SP > 1:
                    pair_allreduce(C, m32)
                    for dmt in range(KT):
                        k.op("dve", lambda e: e.tensor_tensor(out=xt[:, dmt, :], in0=xt[:, dmt, :], in1=m32[:, dmt, :], op=ALU.add),
                             reads=[xt, m32], writes=[xt])
                if C.debug:
                    k.dma("sp", _w3(C.d_x1[:, tsl]), xt[:], reads=[xt], sbuf=xt)
                    k.dma("sp", _w3(C.d_m[:, tsl]), mbf[:], reads=[mbf], sbuf=mbf)
                    k.dma("sp", _w3(C.d_h[:, tsl]), hb[:], reads=[hb], sbuf=hb)
            norm_sbuf(C, xt, g2, sq, rs, rs2, lambda kt: (hb[:, kt, :], hb), D)
            with k.scope() as S2:
                act = S2.sb("mact", [128, FT, 512], BF16)
                pp = S2.sb("mpp", [128, KT, 512], F32) if c.SP > 1 else None
                sg = [S2.sb(f"msg{i}", [128, 512], F32) for i in range(2)]
                for f4 in range((FT + 3) // 4):
                    nc_ = min(512, c.FF - f4 * 512)
                    wg = wr.load(_w3(C.wb["fg"][:, f4 * 512:f4 * 512 + nc_]), KT, nc_)
                    wu = wr.load(_w3(C.wb["fu"][:, f4 * 512:f4 * 512 + nc_]), KT, nc_)
                    for j4 in range(nc_ // 128):
                        ft = f4 * 4 + j4
                        cs = slice(j4 * 128, (j4 + 1) * 128)
                        pg = C.bank()
                        mm_acc(C, pg[:], pg, [(wg[:, kt, cs], hb[:, kt, :]) for kt in range(KT)], [wg, hb])
                        s_ = sg[ft % 2]
                        k.op("act", lambda e: e.activation(out=s_[:], in_=pg[:], func=AF.Silu), reads=[pg], writes=[s_])
                        pu = C.bank()
                        mm_acc(C, pu[:], pu, [(wu[:, kt, cs], hb[:, kt, :]) for kt in range(KT)], [wu, hb])
                        k.op("dve", lambda e: e.tensor_tensor(out=act[:, ft, :], in0=pu[:], in1=s_[:], op=ALU.mult),
                             reads=[pu, s_], writes=[act])
                for dmt in range(KT):
                    w = wd.load(_w3(C.wb["fd"][:, dmt * 128:(dmt + 1) * 128]), FT, 128)
                    py = C.bank()
                    mm_acc(C, py[:], py, [(w[:, ft, :], act[:, ft, :]) for ft in range(FT)], [w, act])
                    if c.SP == 1:
                        k.op("dve", lambda e: e.tensor_tensor(out=xt[:, dmt, :], in0=xt[:, dmt, :], in1=py[:], op=ALU.add),
                             reads=[xt, py], writes=[xt])
                    else:
                        k.op("act", lambda e: e.activation(out=pp[:, dmt, :], in_=py[:], func=AF.Copy), reads=[py], writes=[pp])
                if c.SP > 1:
                    pair_allreduce(C, pp)
                    for dmt in range(KT):
                        k.op("dve", lambda e: e.tensor_tensor(out=xt[:, dmt, :], in0=xt[:, dmt, :], in1=pp[:, dmt, :], op=ALU.add),
                             reads=[xt, pp], writes=[xt])
            k.dma("sp", _w3(C.xres[:, tsl]), xt[:], reads=[xt], sbuf=xt)


def phase_final(C):
    k, c, I, Kc = C.k, C.c, C.I, C.K
    KT, NB = c.KT, c.NB
    with k.scope() as S:
        k.dma_fence("sp")
        g = S.sb("fg", [128, KT], F32)
        k.dma("sp", g[:], I["norm_final"], writes=[g], sbuf=g)
        xt = [S.sb(f"fx{i}", [128, KT, 512], F32) for i in range(2)]
        ot = [S.sb(f"fo{i}", [128, KT, 512], F32) for i in range(2)]
        sq = [S.sb(f"fsq{i}", [128, 512], BF16) for i in range(2)]
        rs = S.sb("frs", [128, 512], F32)
        rs2 = S.sb("frs2", [128, 512], F32)
        for tb in range(NB):
            tsl = slice(tb * 512, (tb + 1) * 512)
            x, o = xt[tb % 2], ot[tb % 2]
            k.dma("sp", x[:], _w3(C.xres[:, tsl]), writes=[x], sbuf=x)
            norm_sbuf(C, x, g, sq, rs, rs2, lambda kt: (o[:, kt, :], o), c.D)
            k.dma("sp", _w3(C.out[:, tsl]), o[:], reads=[o], sbuf=o)


N_CORES = 8
SPLIT = 2


def kernel(**inputs):
    c = Cfg(split=SPLIT)
    groups = [[i * SPLIT + j for j in range(SPLIT)] for i in range(N_CORES // SPLIT)]
    nc, C = build(c, groups=groups)
    x = np.asarray(inputs["x"], np.float32)
    per_rank = [shared_inputs(c, inputs, r) for r in range(SPLIT)]
    in_maps = []
    for i in range(N_CORES):
        m = dict(per_rank[i % SPLIT])
        m["xT"] = np.ascontiguousarray(x[i // SPLIT].T)
        in_maps.append(m)
    res = run_bass_kernel_spmd(nc, in_maps, core_ids=list(range(N_CORES)))
    out = np.stack([np.asarray(res.results[b * SPLIT]["outT"]).T for b in range(N_CORES // SPLIT)])
    return np.ascontiguousarray(out.astype(np.float32))
```
